# Optimizing a Trainium2 kernel written in Bass

```python
import math
import jax
import jax.numpy as jnp
from jax import lax
import numpy as np

D_MODEL = 1024
BATCH = 1
SEQ = 16384
DEPTH = 2

HEAD_DIM = 64
GDN_HEADS = 8
RWKV_HEADS = 8
HGRN_HEADS = 8
GDN_WIDTH = GDN_HEADS * HEAD_DIM
RWKV_WIDTH = RWKV_HEADS * HEAD_DIM
HGRN_EXPAND = 64
HGRN_FDIM = HGRN_HEADS * HGRN_EXPAND
HGRN_WIDTH = HGRN_HEADS * HEAD_DIM
D_MIX = GDN_WIDTH + RWKV_WIDTH + HGRN_WIDTH
CONV_WIDTH = 4
RWKV_DECAY_LORA = 64
RWKV_ICLR_LORA = 64
CHUNK = 64
NORM_EPS = 1e-6
L2_EPS = 1e-6
RWKV_GN_EPS = 64e-5

GDN_SPLIT = (GDN_WIDTH, GDN_WIDTH, GDN_WIDTH, GDN_HEADS, GDN_HEADS, GDN_WIDTH)
RWKV_SPLIT = (RWKV_WIDTH, RWKV_WIDTH, RWKV_WIDTH, RWKV_DECAY_LORA, RWKV_ICLR_LORA, RWKV_WIDTH)
HGRN_SPLIT = (HGRN_FDIM, HGRN_FDIM, HGRN_WIDTH, HGRN_WIDTH)
GDN_PROJ = sum(GDN_SPLIT)
RWKV_PROJ = sum(RWKV_SPLIT)
HGRN_PROJ = sum(HGRN_SPLIT)
PROJ_DIM = GDN_PROJ + RWKV_PROJ + HGRN_PROJ

kernel_name = 'hybrid_gdn_rwkv7_hgrn2_parallel_heads'

F32 = jnp.float32


def _split(t, sizes):
    out, start = [], 0
    for s in sizes:
        out.append(t[..., start:start + s])
        start += s
    return out


def _heads(t, h):
    return t.reshape(t.shape[:-1] + (h, t.shape[-1] // h))


def _rmsnorm(x, w):
    xf = x.astype(F32)
    y = xf * lax.rsqrt(jnp.mean(xf * xf, axis=-1, keepdims=True) + NORM_EPS)
    return (y * w.astype(F32)).astype(x.dtype)


def _l2norm(x):
    return x * lax.rsqrt(jnp.sum(x * x, axis=-1, keepdims=True) + L2_EPS)


def _causal_conv(x, w):
    return lax.conv_general_dilated(x, w[:, None, :], window_strides=(1,),
                                    padding=[(CONV_WIDTH - 1, 0)],
                                    dimension_numbers=('NWC', 'WIO', 'NWC'),
                                    feature_group_count=x.shape[-1])


def _token_shift(p, mu):
    prev = jnp.pad(p, ((0, 0), (1, 0), (0, 0)))[:, :-1]
    return p + mu * (prev - p)


def _gdn_chunked(q, k, v, beta, g):
    B, T, H, Dk = q.shape
    Dv = v.shape[-1]
    N = T // CHUNK

    def to_chunks(t):
        t = t.reshape((B, N, CHUNK, H) + t.shape[3:])
        return jnp.moveaxis(t, 3, 1)

    q, k, v, beta, g = (to_chunks(t) for t in (q, k, v, beta, g))
    gc = jnp.cumsum(g, axis=-1)
    causal = jnp.tril(jnp.ones((CHUNK, CHUNK), bool))
    strict = jnp.tril(jnp.ones((CHUNK, CHUNK), bool), -1)
    diff = gc[..., :, None] - gc[..., None, :]
    decay_mat = jnp.where(causal, jnp.exp(jnp.where(causal, diff, 0.0)), 0.0)
    k_beta = k * beta[..., None]
    A = jnp.where(strict, jnp.einsum('bhnik,bhnjk->bhnij', k_beta, k) * decay_mat, 0.0)
    eye = jnp.eye(CHUNK, dtype=A.dtype)
    Tm = lax.linalg.triangular_solve(eye + A, jnp.broadcast_to(eye, A.shape),
                                     left_side=True, lower=True)
    u = jnp.einsum('bhnij,bhnjd->bhnid', Tm, v * beta[..., None])
    w = jnp.einsum('bhnij,bhnjd->bhnid', Tm, k_beta * jnp.exp(gc)[..., None])
    qk = jnp.einsum('bhnik,bhnjk->bhnij', q, k) * decay_mat
    q_dec = q * jnp.exp(gc)[..., None]
    k_dec = k * jnp.exp(gc[..., -1:] - gc)[..., None]
    last = jnp.exp(gc[..., -1])

    def step(S, xs):
        u_n, w_n, qk_n, qd_n, kd_n, last_n = xs
        v_new = u_n - jnp.einsum('bhck,bhkv->bhcv', w_n, S)
        o = jnp.einsum('bhck,bhkv->bhcv', qd_n, S) + jnp.einsum('bhij,bhjv->bhiv', qk_n, v_new)
        S = S * last_n[..., None, None] + jnp.einsum('bhck,bhcv->bhkv', kd_n, v_new)
        return S, o

    xs = tuple(jnp.moveaxis(t, 2, 0) for t in (u, w, qk, q_dec, k_dec, last))
    S0 = jnp.zeros((B, H, Dk, Dv), q.dtype)
    _, o = lax.scan(step, S0, xs)
    o = jnp.moveaxis(o, 0, 2)
    return jnp.moveaxis(o, 1, 3).reshape(B, T, H, Dv)


def _gdn_branch(p, conv_w, a_log, dt_bias, norm_w):
    q, k, v, b_raw, a_raw, gate = _split(p, GDN_SPLIT)
    qkv = jax.nn.silu(_causal_conv(jnp.concatenate([q, k, v], axis=-1), conv_w.astype(F32)))
    q, k, v = _split(qkv, (GDN_WIDTH, GDN_WIDTH, GDN_WIDTH))
    q = _l2norm(_heads(q, GDN_HEADS)) * (HEAD_DIM ** -0.5)
    k = _l2norm(_heads(k, GDN_HEADS))
    v = _heads(v, GDN_HEADS)
    beta = jax.nn.sigmoid(b_raw)
    g = -jnp.exp(a_log.astype(F32)) * jax.nn.softplus(a_raw + dt_bias.astype(F32))
    o = _gdn_chunked(q, k, v, beta, g)
    o = _rmsnorm(o, norm_w).reshape(o.shape[:2] + (GDN_WIDTH,))
    return o * jax.nn.silu(gate)


def _rwkv7_scan(r, decay, k, v, kk, a):
    B, T, H, D = r.shape

    def step(S, xs):
        r_t, w_t, k_t, v_t, kk_t, a_t = xs
        sa = jnp.einsum('bhvk,bhk->bhv', S, -kk_t)
        S = (S * w_t[:, :, None, :] + sa[..., :, None] * (kk_t * a_t)[..., None, :]
             + v_t[..., :, None] * k_t[..., None, :])
        return S, jnp.einsum('bhvk,bhk->bhv', S, r_t)

    xs = tuple(jnp.moveaxis(t, 1, 0) for t in (r, decay, k, v, kk, a))
    S0 = jnp.zeros((B, H, D, D), r.dtype)
    _, o = lax.scan(step, S0, xs)
    return jnp.moveaxis(o, 0, 1)


def _rwkv7_branch(p, mu, w0, w_up, a0, a_up, k_k, k_a, r_k, ln_w, ln_b):
    p = _token_shift(p, mu.astype(F32))
    r, k, v, wd, ad, gate = _split(p, RWKV_SPLIT)
    w_raw = -jax.nn.softplus(-(w0.astype(F32) + jnp.tanh(wd) @ w_up.astype(F32))) - 0.5
    decay = jnp.exp(-jnp.exp(w_raw))
    a = jax.nn.sigmoid(a0.astype(F32) + ad @ a_up.astype(F32))
    kk = _l2norm(_heads(k * k_k.astype(F32), RWKV_HEADS))
    k = k * (1.0 + (a - 1.0) * k_a.astype(F32))
    rh, kh, vh = _heads(r, RWKV_HEADS), _heads(k, RWKV_HEADS), _heads(v, RWKV_HEADS)
    o = _rwkv7_scan(rh, _heads(decay, RWKV_HEADS), kh, vh, kk, _heads(a, RWKV_HEADS))
    mean = jnp.mean(o, axis=-1, keepdims=True)
    var = jnp.mean(jnp.square(o - mean), axis=-1, keepdims=True)
    o = (o - mean) * lax.rsqrt(var + RWKV_GN_EPS)
    o = o.reshape(o.shape[:2] + (RWKV_WIDTH,)) * ln_w.astype(F32) + ln_b.astype(F32)
    bonus = jnp.sum(rh * kh * r_k.astype(F32), axis=-1, keepdims=True) * vh
    o = o + bonus.reshape(o.shape)
    return o * jax.nn.silu(gate)


def _hgrn2_chunked(q, k, v, log_g):
    B, T, H, Dk = q.shape
    Dv = v.shape[-1]
    N = T // CHUNK

    def to_chunks(t):
        return t.reshape(B, N, CHUNK, H, t.shape[-1]).transpose(1, 0, 3, 2, 4)

    q, k, v, lg = (to_chunks(t) for t in (q, k, v, log_g))
    bcum = jnp.cumsum(lg, axis=3)
    causal = jnp.tril(jnp.ones((CHUNK, CHUNK), bool))[:, :, None]

    def step(S, xs):
        q_n, k_n, v_n, b_n = xs
        diff = b_n[..., :, None, :] - b_n[..., None, :, :]
        dec = jnp.where(causal, jnp.exp(jnp.where(causal, diff, 0.0)), 0.0)
        att = jnp.einsum('bhik,bhijk,bhjk->bhij', q_n, dec, k_n)
        o = (jnp.einsum('bhij,bhjv->bhiv', att, v_n)
             + jnp.einsum('bhik,bhkv->bhiv', q_n * jnp.exp(b_n), S))
        b_last = b_n[..., -1:, :]
        S = (S * jnp.exp(b_last)[..., 0, :, None]
             + jnp.einsum('bhjk,bhjv->bhkv', k_n * jnp.exp(b_last - b_n), v_n))
        return S, o

    S0 = jnp.zeros((B, H, Dk, Dv), q.dtype)
    _, o = lax.scan(step, S0, (q, k, v, bcum))
    return o.transpose(1, 0, 3, 2, 4).reshape(B, T, H, Dv)


def _hgrn2_branch(p, lb, norm_w):
    q, f, i, gate = _split(p, HGRN_SPLIT)
    q = jax.nn.silu(_heads(q, HGRN_HEADS))
    lb = _heads(lb, HGRN_HEADS)
    f = _heads(f, HGRN_HEADS)
    log_g = jax.nn.log_sigmoid(f) + jnp.log1p(lb * jnp.exp(-f))
    k = (1.0 - lb) * jax.nn.sigmoid(-f)
    o = _hgrn2_chunked(q, k, _heads(i, HGRN_HEADS), log_g)
    o = _rmsnorm(o.reshape(o.shape[:2] + (HGRN_WIDTH,)), norm_w)
    return o * jax.nn.silu(gate)


def setup_inputs(seed: int = 0) -> dict:
    key = jax.random.key(seed)
    ks = jax.random.split(key, 24)

    def nrm(k, shape, s):
        return s * jax.random.normal(k, shape, F32)

    x = jax.random.normal(ks[0], (BATCH, SEQ, D_MODEL), F32)
    norm_w = 1.0 + nrm(ks[1], (DEPTH, D_MODEL), 0.02)
    w_in = nrm(ks[2], (DEPTH, D_MODEL, PROJ_DIM), D_MODEL ** -0.5)
    gdn_conv_w = nrm(ks[3], (DEPTH, CONV_WIDTH, 3 * GDN_WIDTH), CONV_WIDTH ** -0.5)
    gdn_a_log = jnp.log(jax.random.uniform(ks[4], (DEPTH, GDN_HEADS), F32, 1.0, 16.0))
    dt = jnp.exp(jax.random.uniform(ks[5], (DEPTH, GDN_HEADS), F32, math.log(1e-3), math.log(1e-1)))
    gdn_dt_bias = dt + jnp.log(-jnp.expm1(-dt))
    gdn_norm_w = 1.0 + nrm(ks[6], (DEPTH, HEAD_DIM), 0.02)
    rwkv_mu = jax.random.uniform(ks[7], (DEPTH, RWKV_PROJ), F32, 0.0, 1.0)
    rwkv_w0 = jax.random.uniform(ks[8], (DEPTH, RWKV_WIDTH), F32, -6.0, -1.0)
    rwkv_w_up = nrm(ks[9], (DEPTH, RWKV_DECAY_LORA, RWKV_WIDTH), 0.5 * RWKV_DECAY_LORA ** -0.5)
    rwkv_a0 = nrm(ks[10], (DEPTH, RWKV_WIDTH), 0.1)
    rwkv_a_up = nrm(ks[11], (DEPTH, RWKV_ICLR_LORA, RWKV_WIDTH), 0.5 * RWKV_ICLR_LORA ** -0.5)
    rwkv_k_k = 0.85 + nrm(ks[12], (DEPTH, RWKV_WIDTH), 0.02)
    rwkv_k_a = 1.0 + nrm(ks[13], (DEPTH, RWKV_WIDTH), 0.02)
    rwkv_r_k = -0.04 + nrm(ks[14], (DEPTH, RWKV_HEADS, HEAD_DIM), 0.1)
    rwkv_ln_w = 1.0 + nrm(ks[15], (DEPTH, RWKV_WIDTH), 0.02)
    rwkv_ln_b = nrm(ks[16], (DEPTH, RWKV_WIDTH), 0.02)
    hgrn_lower_bounds = nrm(ks[17], (DEPTH, HGRN_FDIM), 0.5)
    hgrn_norm_w = 1.0 + nrm(ks[18], (DEPTH, HGRN_WIDTH), 0.02)
    w_out = nrm(ks[19], (DEPTH, D_MIX, D_MODEL), D_MIX ** -0.5)
    final_norm_w = 1.0 + nrm(ks[20], (D_MODEL,), 0.02)
    return {'x': x, 'norm_w': norm_w, 'w_in': w_in, 'gdn_conv_w': gdn_conv_w,
            'gdn_a_log': gdn_a_log, 'gdn_dt_bias': gdn_dt_bias, 'gdn_norm_w': gdn_norm_w,
            'rwkv_mu': rwkv_mu, 'rwkv_w0': rwkv_w0, 'rwkv_w_up': rwkv_w_up, 'rwkv_a0': rwkv_a0,
            'rwkv_a_up': rwkv_a_up, 'rwkv_k_k': rwkv_k_k, 'rwkv_k_a': rwkv_k_a, 'rwkv_r_k': rwkv_r_k,
            'rwkv_ln_w': rwkv_ln_w, 'rwkv_ln_b': rwkv_ln_b, 'hgrn_lower_bounds': hgrn_lower_bounds,
            'hgrn_norm_w': hgrn_norm_w, 'w_out': w_out, 'final_norm_w': final_norm_w}


def reference(x, norm_w, w_in, gdn_conv_w, gdn_a_log, gdn_dt_bias, gdn_norm_w,
              rwkv_mu, rwkv_w0, rwkv_w_up, rwkv_a0, rwkv_a_up, rwkv_k_k, rwkv_k_a, rwkv_r_k,
              rwkv_ln_w, rwkv_ln_b, hgrn_lower_bounds, hgrn_norm_w, w_out, final_norm_w):
    lb_soft = jax.nn.softmax(hgrn_lower_bounds.astype(F32), axis=0)
    lb_all = jnp.cumsum(lb_soft, axis=0) - lb_soft[0]
    for l in range(DEPTH):
        h = _rmsnorm(x, norm_w[l])
        proj = (h @ w_in[l]).astype(F32)
        p_gdn, p_rwkv, p_hgrn = _split(proj, (GDN_PROJ, RWKV_PROJ, HGRN_PROJ))
        y_a = _gdn_branch(p_gdn, gdn_conv_w[l], gdn_a_log[l], gdn_dt_bias[l], gdn_norm_w[l])
        y_b = _rwkv7_branch(p_rwkv, rwkv_mu[l], rwkv_w0[l], rwkv_w_up[l], rwkv_a0[l], rwkv_a_up[l],
                            rwkv_k_k[l], rwkv_k_a[l], rwkv_r_k[l], rwkv_ln_w[l], rwkv_ln_b[l])
        y_c = _hgrn2_branch(p_hgrn, lb_all[l], hgrn_norm_w[l])
        y = jnp.concatenate([y_a, y_b, y_c], axis=-1).astype(x.dtype)
        x = x + y @ w_out[l]
    return _rmsnorm(x, final_norm_w)
```

```python
from concourse.bass_utils import run_bass_kernel_spmd
from contextlib import ExitStack
import numpy as np
import concourse.bass as bass
import concourse.mybir as mybir

F32 = mybir.dt.float32
BF16 = mybir.dt.bfloat16
AF = mybir.ActivationFunctionType
ALU = mybir.AluOpType
AX = mybir.AxisListType

COMPUTE = ("pe", "act", "dve", "pool")


class _Rec:
    def __init__(self):
        self.call = None
    def __getattr__(self, name):
        def f(*a, **kw):
            self.call = (name, a, kw)
            return self
        return f


def _bind(fn):
    r = _Rec()
    fn(r)
    name, a, kw = r.call
    return lambda e: getattr(e, name)(*a, **kw)


class Prog:
    def __init__(self, nc):
        self.nc = nc
        self.es = ExitStack()
        self.lists = {e: [] for e in ("pe", "act", "dve", "pool", "sp")}
        self.count = {e: 0 for e in COMPUTE}
        self.waited = {e: {} for e in self.lists}
        self.lastw = {}
        self.lastr = {}
        self.sems = {}
        self.dma_count = {}
        self.psum_banks = []
        self.psum_rr = 0
        self.final_waits = {}
        self.scopes = [self.es]
        self.handles = {"pe": nc.tensor, "act": nc.scalar, "dve": nc.vector, "pool": nc.gpsimd, "sp": nc.sync}
        for e in COMPUTE:
            self.sem(e)

    def sb(self, name, shape, dtype=F32):
        return self.scopes[-1].enter_context(self.nc.sbuf_tensor(name, list(shape), dtype))

    def ps(self, name, shape, dtype=F32):
        return self.scopes[-1].enter_context(self.nc.psum_tensor(name, list(shape), dtype))

    def push_scope(self):
        self.scopes.append(ExitStack())

    def pop_scope(self):
        self.barrier()
        self.scopes.pop().close()

    def barrier(self):
        cur = {e: c for e, c in self.count.items() if c > 0}
        cur.update(self.dma_count)
        for eng in self.lists:
            ws = []
            wd = self.waited[eng]
            for s_, v in cur.items():
                if wd.get(s_, 0) >= v:
                    continue
                wd[s_] = v
                ws.append((s_, v))
            if ws:
                self._issue(eng, ws, None, None)

    def sem(self, name):
        if name not in self.sems:
            self.sems[name] = self.es.enter_context(self.nc.semaphore(name))
        return self.sems[name]

    def _deps(self, reads, writes):
        deps = {}
        def add(d):
            for s, v in d.items():
                if deps.get(s, 0) < v:
                    deps[s] = v
        for k in reads:
            add(self.lastw.get(k, {}))
        for k in writes:
            add(self.lastw.get(k, {}))
            add(self.lastr.get(k, {}))
        return deps

    def _record(self, reads, writes, sem, val):
        for k in reads:
            d = self.lastr.setdefault(k, {})
            d[sem] = val
        for k in writes:
            self.lastw[k] = {sem: val}
            self.lastr[k] = {}

    def _waits(self, eng, deps):
        ws = []
        wd = self.waited[eng]
        for s, v in deps.items():
            if s == "pe" and eng == "pe":
                continue
            if wd.get(s, 0) >= v:
                continue
            wd[s] = v
            ws.append((s, v))
        return ws

    def op(self, eng, fn, reads=(), writes=()):
        deps = self._deps(reads, writes)
        ws = self._waits(eng, deps)
        self.count[eng] += 1
        val = self.count[eng]
        for k in writes:
            prev = self.lastw.get(k, {})
            prevr = self.lastr.get(k, {})
            self.lastw[k] = dict(prev)
            self.lastw[k][eng] = val
            self.lastr[k] = {}
        for k in reads:
            d = self.lastr.setdefault(k, {})
            d[eng] = val
        self._issue(eng, ws, _bind(fn), (eng, 1))

    def dma(self, queue, semname, out, in_, reads=(), writes=(), **kw):
        deps = self._deps(reads, writes)
        ws = self._waits(queue, deps)
        self.dma_count[semname] = self.dma_count.get(semname, 0) + 16
        val = self.dma_count[semname]
        self.sem(semname)
        for k in writes:
            prev = self.lastw.get(k, {})
            self.lastw[k] = dict(prev)
            self.lastw[k][semname] = val
            self.lastr[k] = {}
        for k in reads:
            d = self.lastr.setdefault(k, {})
            d[semname] = val
        fn = lambda e, out=out, in_=in_, kw=kw: e.dma_start(out=out, in_=in_, **kw)
        self._issue(queue, ws, fn, (semname, 16))
        return (semname, val)

    def dma_fn(self, queue, semname, fn, reads=(), writes=(), inc=16):
        deps = self._deps(reads, writes)
        ws = self._waits(queue, deps)
        self.dma_count[semname] = self.dma_count.get(semname, 0) + inc
        val = self.dma_count[semname]
        self.sem(semname)
        for k in writes:
            prev = self.lastw.get(k, {})
            self.lastw[k] = dict(prev)
            self.lastw[k][semname] = val
            self.lastr[k] = {}
        for k in reads:
            d = self.lastr.setdefault(k, {})
            d[semname] = val
        self._issue(queue, ws, fn, (semname, inc))
        return (semname, val)

    def finish_wait(self, queue, ev):
        self._issue(queue, [ev], None, None)

    def _issue(self, ename, ws, fn, si):
        self.lists[ename].append((ws, fn, si))

    def emit(self):
        attrs = {"pe": "tensor", "act": "scalar", "dve": "vector", "pool": "gpsimd", "sp": "sync"}
        with self.nc.Block() as block:
            for ename, attr in attrs.items():
                lst = self.lists[ename]
                if not lst:
                    continue

                def body(eng, lst=lst):
                    for ws, fn, si in lst:
                        for s_, v in ws:
                            eng.wait_ge(self.sems[s_], v)
                        if fn is None:
                            continue
                        sname, inc = si
                        if inc == 1 and sname.startswith("cc"):
                            fn(eng).then_inc(self.sems[sname])
                        else:
                            fn(eng).then_inc(self.sems[sname], inc)
                getattr(block, attr)(body)

    def close(self):
        self.es.close()


NG = 15
GQ, GK, GV, GG, RR, RK, RV, RG, RWD, RAD, HQ, HF, HI, HGT, GSC = range(15)
PC_CONV = 0; PC_GNW = 12; PC_ALOG = 13; PC_DTB = 14; PC_MU = 15
PC_W0 = 21; PC_A0 = 22; PC_KK = 23; PC_KA = 24; PC_RK = 25; PC_LNW = 26; PC_LNB = 27
PC_LB0 = 28; PC_LB1 = 29; PC_HNW = 30
NPAR = 32
C_ID, C_ONES, C_TRI, C_MNI, C_MNS, C_M01I, C_M01S, C_RST = range(8)
NCONST = 8
NCH = 4
BLK = NCH * 64


def make_consts():
    c = np.zeros((64, NCONST, BLK), np.float32)
    j = np.arange(64)[:, None]; i = np.arange(64)[None, :]
    rep = lambda m: np.tile(m.astype(np.float32), (1, NCH))
    c[:, C_ID] = rep(j == i)
    c[:, C_ONES] = 1.0
    c[:, C_TRI] = rep(j <= i)
    c[:, C_MNI] = rep(np.where(i >= j, 0.0, -30000.0))
    c[:, C_MNS] = rep(np.where(i > j, 0.0, -30000.0))
    c[:, C_M01I] = rep(i >= j)
    c[:, C_M01S] = rep(i > j)
    r = np.ones((64, BLK), np.float32); r[:, ::64] = 0.0
    c[:, C_RST] = r
    return c.reshape(64, NCONST * BLK)


def mixer_body(p, nc, T, layer, A, tag):
    NB = T // BLK
    F32R = mybir.dt.float32r
    R = lambda ap: ap.bitcast(F32R)
    v3 = lambda ap: ap.rearrange("p (n i) -> p n i", n=NCH)
    bc = lambda colap: colap.unsqueeze(2).to_broadcast([64, NCH, 64])

    class Tl:
        def __init__(s, name, shape, dtype=F32):
            s.t = p.sb(tag + "sb_" + name, shape, dtype); s.k = name
        def __getitem__(s, idx):
            return s.t[idx]
    banks = [p.ps(tag + f"bank{i}", [128, 512], F32) for i in range(8)]

    cur = [None]
    def _emit(eng, fn, r, w):
        b_ = _bind(fn)
        r = list(r); w = list(w)
        if cur[0] is None:
            p.op(eng, b_, reads=r, writes=w)
        else:
            cur[0][-1].append(lambda: p.op(eng, b_, reads=r, writes=w))
    act = lambda fn, r, w: _emit("act", fn, r, w)
    dve = lambda fn, r, w: _emit("dve", fn, r, w)
    pool = lambda fn, r, w: _emit("pool", fn, r, w)
    pe = lambda fn, r, w: _emit("pe", fn, r, w)
    def cut():
        if cur[0] is not None and cur[0][-1]:
            cur[0].append([])

    def run_streams(streams):
        tot = [sum(1 for g in s if g) for s in streams]
        streams = [[g for g in s if g] for s in streams]
        idx = [0] * len(streams)
        while True:
            best, bf = None, None
            for i, s in enumerate(streams):
                if idx[i] < len(s):
                    f = idx[i] / max(1, tot[i])
                    if bf is None or f < bf:
                        best, bf = i, f
            if best is None:
                break
            for th in streams[best][idx[best]]:
                th()
            idx[best] += 1

    cst = Tl("cst", (64, NCONST * BLK))
    pp = Tl("pp", (64, NPAR))
    lora = Tl("lora", (64, 128))
    p.dma("sp", "d_c", cst[:], A["consts"][:, :], writes=[cst.k])
    p.dma("sp", "d_p", pp[:], A["pp"][:, :], writes=[pp.k])
    p.dma("sp", "d_l", lora[:], A["lora"][:, :], writes=[lora.k])
    C = lambda c, w=BLK: cst[:, c * BLK:c * BLK + w]
    PP = lambda c: pp[:, c:c + 1]
    IDN = C(C_ID, 64); ONES = C(C_ONES, 64); TRI = C(C_TRI, 64)
    cstr = Tl("cstr", (64, 64))
    act(lambda e: e.copy(out=R(cstr[:, 0:64]), in_=ONES), [cst.k], ["cstr"])
    ONESR = cstr[:, 0:64]

    wbf = Tl("wbf", (128, 8, NG * 64), BF16)
    wst = [Tl(f"wst{i}", (128, NG * 64)) for i in range(2)]
    for kc in range(8):
        s = wst[kc % 2]
        p.dma("sp", f"d_w{kc % 2}", s[:], A["w"][kc * 128:(kc + 1) * 128, :], writes=[s.k])
        if kc % 2 == 0:
            act(lambda e: e.copy(out=wbf[:, kc, :], in_=s[:]), [s.k], [f"wbf{kc}"])
        else:
            dve(lambda e: e.tensor_copy(out=wbf[:, kc, :], in_=s[:]), [s.k], [f"wbf{kc}"])
    WBK = [f"wbf{kc}" for kc in range(8)]

    dp = Tl("dp", (64, 24))
    DP = lambda c: dp[:, c:c + 1]
    D_NA, D_OMM, D_NW0, D_OMKA, D_LB, D_OML, D_NOML, D_T0, D_T1, D_T2, D_T3, D_T4, D_T5 = 0, 1, 7, 8, 9, 10, 11, 12, 13, 14, 15, 16, 17
    act(lambda e: e.activation(out=DP(D_T0), in_=PP(PC_ALOG), func=AF.Exp), [pp.k], ["dp_t0"])
    dve(lambda e: e.tensor_scalar(out=DP(D_NA), in0=DP(D_T0), scalar1=-1.0, scalar2=None, op0=ALU.mult), ["dp_t0"], ["dp"])
    dve(lambda e: e.tensor_scalar(out=dp[:, D_OMM:D_OMM + 6], in0=pp[:, PC_MU:PC_MU + 6], scalar1=-1.0, scalar2=1.0, op0=ALU.mult, op1=ALU.add), [pp.k], ["dp"])
    dve(lambda e: e.tensor_scalar(out=DP(D_NW0), in0=PP(PC_W0), scalar1=-1.0, scalar2=None, op0=ALU.mult), [pp.k], ["dp"])
    dve(lambda e: e.tensor_scalar(out=DP(D_OMKA), in0=PP(PC_KA), scalar1=-1.0, scalar2=1.0, op0=ALU.mult, op1=ALU.add), [pp.k], ["dp"])
    act(lambda e: e.activation(out=dp[:, D_T1:D_T1 + 2], in_=pp[:, PC_LB0:PC_LB0 + 2], func=AF.Exp), [pp.k], ["dp_t1"])
    dve(lambda e: e.tensor_tensor(out=DP(D_T3), in0=DP(D_T1), in1=DP(D_T2), op=ALU.add), ["dp_t1"], ["dp_t0b"])
    dve(lambda e: e.reciprocal(out=DP(D_T4), in_=DP(D_T3)), ["dp_t0b"], ["dp_t0c"])
    if layer == 0:
        dve(lambda e: e.tensor_tensor(out=DP(D_T5), in0=DP(D_T1), in1=DP(D_T1), op=ALU.subtract), ["dp_t1"], ["dp_lb0"])
    else:
        dve(lambda e: e.tensor_tensor(out=DP(D_T5), in0=DP(D_T3), in1=DP(D_T1), op=ALU.subtract), ["dp_t0b", "dp_t1"], ["dp_lb0"])
    dve(lambda e: e.tensor_tensor(out=DP(D_T5), in0=DP(D_T5), in1=DP(D_T4), op=ALU.mult), ["dp_lb0", "dp_t0c"], ["dp_lb1"])
    dve(lambda e: e.tensor_copy(out=DP(D_LB), in_=DP(D_T5)), ["dp_lb1"], ["dp_lb"])
    dve(lambda e: e.tensor_scalar(out=DP(D_OML), in0=DP(D_LB), scalar1=-1.0, scalar2=1.0, op0=ALU.mult, op1=ALU.add), ["dp_lb"], ["dp_oml"])
    dve(lambda e: e.tensor_scalar(out=DP(D_NOML), in0=DP(D_OML), scalar1=-1.0, scalar2=None, op0=ALU.mult), ["dp_oml"], ["dp"])
    DPK = ["dp", "dp_lb", "dp_oml", pp.k, cst.k, lora.k]

    hblk = [Tl(f"hblk{i}", (128, 8, BLK), BF16) for i in range(2)]
    raw = {g: [Tl(f"raw{g}_{s}", (64, BLK + 3)) for s in range(2)] for g in (GQ, GK, GV, RR, RK, RV, RG, RWD, RAD)}
    for g in raw:
        pool(lambda e, g=g: e.memset(raw[g][1][:, BLK:BLK + 3], 0.0), [], [raw[g][1].k])
    S = {m: [Tl(f"S{m}{s}", (64, 64)) for s in range(2)] for m in ("g", "r", "h")}
    for m in S:
        act(lambda e, m=m: e.mul(out=R(S[m][0][:]), in_=ONES, mul=0.0), [cst.k], [S[m][0].k])
    srow = Tl("srow", (64, BLK))
    pool(lambda e: e.memset(srow[:], 0.0), [], [srow.k])
    ssall = Tl("ssall", (64, max(128, T // 64)))
    pool(lambda e: e.memset(ssall[:], 0.0), [], [ssall.k])
    yout = [Tl(f"yout{i}", (64, 3, BLK), BF16) for i in range(2)]
    colsG, colsH, colsR = Tl("colsG", (64, 12, 16)), Tl("colsH", (64, 16, 16)), Tl("colsR", (64, 20, 16))
    CL = lambda i: (colsG if i <= 10 else colsH if i <= 14 else colsR)[:, i, 0:NCH]

    def mk(prefix, nplain, nr):
        d = {f"p{i}": Tl(f"{prefix}_p{i}", (64, BLK)) for i in range(nplain)}
        d.update({f"r{i}": Tl(f"{prefix}_r{i}", (64, BLK)) for i in range(nr)})
        return d
    WG = mk("g", 12, 15)
    WH = mk("h", 14, 6)
    WR = mk("w", 22, 22)

    class St:
        def __init__(s, rot, ob, name):
            s.rot = rot; s.i = 0; s.ob = banks[ob]; s.obk = f"bank{ob}"; s.name = name
        def nb(s):
            i = s.rot[s.i % len(s.rot)]; s.i += 1
            return banks[i], f"bank{i}"
    SG, SR, SH = St([0, 1], 5, "g"), St([2, 3], 6, "r"), St([4], 7, "h")

    def transp8(st, src, srckey, rows=64):
        bk, bkk = st.nb()
        for n in range(NCH):
            pe(lambda e, n=n, bk=bk: e.transpose(out=bk[0:64, n * 64:n * 64 + rows], in_=src[0:rows, n * 64:(n + 1) * 64], identity=IDN[0:rows, 0:rows]),
               [srckey, cst.k], [bkk])
        return bk, bkk

    def ones_mm(st, src, srckey):
        bk, bkk = st.nb()
        pe(lambda e: e.matmul(bk[0:64, 0:BLK], lhsT=R(ONESR), rhs=R(src), start=True, stop=True), [srckey, "cstr"], [bkk])
        return bk, bkk

    def chunk_mm(st, lhs, lhsk, rhs, rhsk):
        bk, bkk = st.nb()
        for n in range(NCH):
            pe(lambda e, n=n, bk=bk: e.matmul(bk[0:64, n * 64:(n + 1) * 64], lhsT=R(lhs[:, n * 64:(n + 1) * 64]), rhs=R(rhs[:, n * 64:(n + 1) * 64]), start=True, stop=True),
               [lhsk, rhsk], [bkk])
        return bk, bkk

    def neumann(st, Z0, Y0, Wt, tmpZ, tmpY):
        dve(lambda e: e.tensor_tensor(out=R(Wt[:]), in0=Z0[:], in1=C(C_ID), op=ALU.add), [Z0.k, cst.k], [Wt.k])
        Zc, Yc, Zn, Yn = Z0, Y0, tmpZ, tmpY
        for k in range(5):
            by, byk = chunk_mm(st, Zc, Zc.k, Yc, Yc.k)
            act(lambda e, by=by, Yn=Yn: e.copy(out=R(Yn[:]), in_=by[0:64, 0:BLK]), [byk], [Yn.k])
            if k < 4:
                bz, bzk = chunk_mm(st, Yc, Yc.k, Zc, Zc.k)
                dve(lambda e, bz=bz, Zn=Zn: e.tensor_copy(out=R(Zn[:]), in_=bz[0:64, 0:BLK]), [bzk], [Zn.k])
            cut()
            bw, bwk = chunk_mm(st, Yn, Yn.k, Wt, Wt.k)
            dve(lambda e, bw=bw: e.tensor_tensor(out=R(Wt[:]), in0=bw[0:64, 0:BLK], in1=Wt[:], op=ALU.add), [bwk, Wt.k], [Wt.k])
            cut()
            Zc, Yc, Zn, Yn = Zn, Yn, Zc, Yc

    def l2n(st, x, outt, scale, sq, rn):
        act(lambda e: e.activation(out=R(sq[:]), in_=x[:], func=AF.Square), [x.k], [sq.k])
        bk, bkk = ones_mm(st, sq[:], sq.k)
        act(lambda e: e.activation(out=rn[:], in_=bk[0:64, 0:BLK], func=AF.Ln, bias=1e-6), [bkk], [rn.k])
        act(lambda e: e.activation(out=rn[:], in_=rn[:], func=AF.Exp, scale=-0.5), [rn.k], [rn.k])
        dve(lambda e: e.scalar_tensor_tensor(out=R(outt[:]), in0=x[:], scalar=scale, in1=rn[:], op0=ALU.mult, op1=ALU.mult), [x.k, rn.k], [outt.k])
        cut()

    loads, stores, all_streams = [], [], []
    for b in range(NB):
        s = b % 2
        hb = hblk[s]
        loads.append(lambda b=b, s=s, hb=hb: p.dma("sp", f"d_h{s}", hb[:], A["hT_block"](b), reads=[A["hT_key"]], writes=[hb.k]))
        yo = yout[s]

        def proj(st, g):
            bk, bkk = st.nb()
            for kc in range(8):
                pe(lambda e, kc=kc, bk=bk: e.matmul(bk[0:64, 0:BLK], lhsT=wbf[:, kc, g * 64:(g + 1) * 64], rhs=hb[:, kc, :], start=(kc == 0), stop=(kc == 7)),
                   [hb.k, WBK[kc]], [bkk])
            return bk, bkk

        def proj2(st, g):
            bk, bkk = st.nb()
            for kc in range(8):
                pe(lambda e, kc=kc, bk=bk: e.matmul(bk[0:128, 0:BLK], lhsT=wbf[:, kc, g * 64:(g + 2) * 64], rhs=hb[:, kc, :], start=(kc == 0), stop=(kc == 7)),
                   [hb.k, WBK[kc]], [bkk])
            return bk, bkk

        def raw_evac(g, bk, bkk, half):
            r_ = raw[g][s]; ro = raw[g][1 - s]
            act(lambda e: e.copy(out=r_[:, 3:BLK + 3], in_=bk[half * 64:(half + 1) * 64, 0:BLK]), [bkk], [r_.k])
            pool(lambda e: e.tensor_copy(out=r_[:, 0:3], in_=ro[:, BLK:BLK + 3]), [ro.k], [r_.k])

        def gdn():
            W, st = WG, SG
            bk, bkk = proj2(st, GQ)
            raw_evac(GQ, bk, bkk, 0); raw_evac(GK, bk, bkk, 1)
            cut()
            bk, bkk = proj2(st, GV)
            raw_evac(GV, bk, bkk, 0)
            sg = W["p11"]
            act(lambda e, bk=bk: e.activation(out=sg[:], in_=bk[64:128, 0:BLK], func=AF.Silu), [bkk], [sg.k])
            cut()
            def conv(g, outt, ci):
                r_ = raw[g][s]
                dve(lambda e: e.tensor_scalar(out=outt[:], in0=r_[:, 3:BLK + 3], scalar1=PP(PC_CONV + ci * 4 + 3), scalar2=None, op0=ALU.mult), [r_.k] + DPK, [outt.k])
                for j in (2, 1, 0):
                    dve(lambda e, j=j: e.scalar_tensor_tensor(out=outt[:], in0=r_[:, j:j + BLK], scalar=PP(PC_CONV + ci * 4 + j), in1=outt[:], op0=ALU.mult, op1=ALU.add), [r_.k, outt.k], [outt.k])
                act(lambda e: e.activation(out=outt[:], in_=outt[:], func=AF.Silu), [outt.k], [outt.k])
                cut()
            qs, ks, vs = W["p0"], W["p1"], W["p2"]
            conv(GQ, qs, 0); conv(GK, ks, 1); conv(GV, vs, 2)
            qn, kn = W["r0"], W["r1"]
            l2n(st, qs, qn, 0.125, W["r2"], W["p3"]); l2n(st, ks, kn, 1.0, W["r2"], W["p3"])
            bk, bkk = proj(st, GSC)
            act(lambda e: e.activation(out=srow[0:1, :], in_=bk[0:1, 0:BLK], func=AF.Sigmoid), [bkk], [srow.k])
            act(lambda e: e.activation(out=srow[32:33, :], in_=bk[32:33, 0:BLK], func=AF.Exp, bias=pp[32:33, PC_DTB:PC_DTB + 1]), [bkk] + DPK, [srow.k])
            act(lambda e: e.activation(out=srow[32:33, :], in_=srow[32:33, :], func=AF.Ln, bias=1.0), [srow.k], [srow.k])
            dve(lambda e: e.tensor_scalar(out=srow[32:33, :], in0=srow[32:33, :], scalar1=dp[32:33, D_NA:D_NA + 1], scalar2=None, op0=ALU.mult), [srow.k] + DPK, [srow.k])
            bk, bkk = transp8(st, srow, srow.k, rows=33)
            beta, gcol = CL(0), CL(1)
            dve(lambda e: e.tensor_copy(out=beta, in_=v3(bk[0:64, 0:BLK])[:, :, 0]), [bkk], ["c_beta"])
            dve(lambda e: e.tensor_copy(out=gcol, in_=v3(bk[0:64, 0:BLK])[:, :, 32]), [bkk], ["c_g"])
            bk, bkk = st.nb()
            pe(lambda e: e.matmul(bk[0:64, 0:NCH], lhsT=TRI, rhs=gcol, start=True, stop=True), ["c_g", cst.k], [bkk])
            pe(lambda e: e.matmul(bk[0:64, NCH:2 * NCH], lhsT=ONES, rhs=gcol, start=True, stop=True), ["c_g", cst.k], [bkk])
            gc, gcl, eg, beg, edl, last, lnb, gcb = CL(2), CL(3), CL(4), CL(5), CL(6), CL(7), CL(8), CL(9)
            dve(lambda e: e.tensor_copy(out=colsG[:, 2:4, 0:NCH], in_=bk[0:64, 0:2 * NCH].rearrange("p (a n) -> p a n", a=2)), [bkk], ["c_gc"])
            act(lambda e: e.activation(out=eg, in_=gc, func=AF.Exp), ["c_gc"], ["c_eg"])
            act(lambda e: e.activation(out=last, in_=gcl, func=AF.Exp), ["c_gc"], ["c_last"])
            act(lambda e: e.activation(out=lnb, in_=beta, func=AF.Ln), ["c_beta"], ["c_lnb"])
            dve(lambda e: e.tensor_tensor(out=beg, in0=beta, in1=eg, op=ALU.mult), ["c_beta", "c_eg"], ["c_beg"])
            dve(lambda e: e.tensor_tensor(out=edl, in0=gcl, in1=gc, op=ALU.subtract), ["c_gc"], ["c_edl0"])
            act(lambda e: e.activation(out=edl, in_=edl, func=AF.Exp), ["c_edl0"], ["c_edl"])
            dve(lambda e: e.tensor_tensor(out=gcb, in0=gc, in1=lnb, op=ALU.add), ["c_gc", "c_lnb"], ["c_gcb"])
            cut()
            def rowb(colap, colk, maskc, outt, subcol, subk):
                D = W["p4"]
                dve(lambda e: e.tensor_tensor(out=v3(D[:]), in0=v3(C(C_ID)), in1=bc(colap), op=ALU.mult), [colk, cst.k], [D.k])
                bk, bkk = st.nb()
                pe(lambda e: e.matmul(bk[0:64, 0:BLK], lhsT=ONES, rhs=D[:], start=True, stop=(maskc is None)), [D.k, cst.k], [bkk])
                if maskc is not None:
                    pe(lambda e: e.matmul(bk[0:64, 0:BLK], lhsT=IDN, rhs=C(maskc), start=False, stop=True), [cst.k], [bkk])
                if subcol is not None:
                    dve(lambda e: e.tensor_tensor(out=v3(outt[:]), in0=v3(bk[0:64, 0:BLK]), in1=bc(subcol), op=ALU.subtract), [bkk, subk], [outt.k])
                    act(lambda e: e.activation(out=outt[:], in_=outt[:], func=AF.Exp), [outt.k], [outt.k])
                else:
                    act(lambda e: e.activation(out=outt[:], in_=bk[0:64, 0:BLK], func=AF.Exp), [bkk], [outt.k])
                cut()
            decT, AdT, egrow = W["p5"], W["p6"], W["p7"]
            rowb(gc, "c_gc", C_MNI, decT, gc, "c_gc")
            rowb(gcb, "c_gcb", C_MNS, AdT, gc, "c_gc")
            rowb(gc, "c_gc", None, egrow, None, None)
            qdT = W["r3"]
            dve(lambda e: e.tensor_tensor(out=R(qdT[:]), in0=qn[:], in1=egrow[:], op=ALU.mult), [qn.k, egrow.k], [qdT.k])
            Z0, QKm, Y0, Wt = W["r4"], W["r5"], W["r6"], W["r7"]
            bk, bkk = chunk_mm(st, kn, kn.k, kn, kn.k)
            dve(lambda e, bk=bk: e.scalar_tensor_tensor(out=R(Z0[:]), in0=bk[0:64, 0:BLK], scalar=-1.0, in1=AdT[:], op0=ALU.mult, op1=ALU.mult), [bkk, AdT.k], [Z0.k])
            cut()
            bk, bkk = chunk_mm(st, kn, kn.k, qn, qn.k)
            dve(lambda e, bk=bk: e.tensor_tensor(out=R(QKm[:]), in0=bk[0:64, 0:BLK], in1=decT[:], op=ALU.mult), [bkk, decT.k], [QKm.k])
            cut()
            bk, bkk = transp8(st, Z0, Z0.k)
            act(lambda e, bk=bk: e.copy(out=R(Y0[:]), in_=bk[0:64, 0:BLK]), [bkk], [Y0.k])
            cut()
            neumann(st, Z0, Y0, Wt, W["r8"], W["r9"])
            kbe, kdec, vb = W["r10"], W["r11"], W["r12"]
            bk, bkk = transp8(st, kn, kn.k)
            dve(lambda e, bk=bk: e.tensor_tensor(out=v3(R(kbe[:])), in0=v3(bk[0:64, 0:BLK]), in1=bc(beg), op=ALU.mult), [bkk, "c_beg"], [kbe.k])
            dve(lambda e, bk=bk: e.tensor_tensor(out=v3(R(kdec[:])), in0=v3(bk[0:64, 0:BLK]), in1=bc(edl), op=ALU.mult), [bkk, "c_edl"], [kdec.k])
            cut()
            bk, bkk = transp8(st, vs, vs.k)
            dve(lambda e, bk=bk: e.tensor_tensor(out=v3(R(vb[:])), in0=v3(bk[0:64, 0:BLK]), in1=bc(beta), op=ALU.mult), [bkk, "c_beta"], [vb.k])
            cut()
            u, wT = W["p8"], W["r13"]
            bk, bkk = chunk_mm(st, Wt, Wt.k, vb, vb.k)
            act(lambda e, bk=bk: e.copy(out=u[:], in_=bk[0:64, 0:BLK]), [bkk], [u.k])
            cut()
            bk, bkk = chunk_mm(st, kbe, kbe.k, Wt, Wt.k)
            act(lambda e, bk=bk: e.copy(out=R(wT[:]), in_=bk[0:64, 0:BLK]), [bkk], [wT.k])
            cut()
            ob, obk = st.ob, st.obk
            vnew = W["r14"]
            for n in range(NCH):
                cs = slice(n * 64, (n + 1) * 64)
                ci = b * NCH + n
                So, Sn = S["g"][ci % 2], S["g"][(ci + 1) % 2]
                ta, tak = st.nb()
                pe(lambda e, cs=cs, So=So, ta=ta: e.matmul(ta[0:64, 0:64], lhsT=R(wT[:, cs]), rhs=R(So[:]), start=True, stop=True), [wT.k, So.k], [tak])
                dve(lambda e, cs=cs, ta=ta: e.tensor_tensor(out=R(vnew[:, cs]), in0=u[:, cs], in1=ta[0:64, 0:64], op=ALU.subtract), [u.k, tak], [vnew.k + str(n)])
                pe(lambda e, cs=cs, So=So: e.matmul(ob[0:64, cs], lhsT=R(qdT[:, cs]), rhs=R(So[:]), start=True, stop=False), [qdT.k, So.k], [obk])
                pe(lambda e, cs=cs: e.matmul(ob[0:64, cs], lhsT=R(QKm[:, cs]), rhs=R(vnew[:, cs]), start=False, stop=True), [QKm.k, vnew.k + str(n)], [obk])
                tb, tbk = st.nb()
                pe(lambda e, cs=cs, tb=tb: e.matmul(tb[0:64, 0:64], lhsT=R(kdec[:, cs]), rhs=R(vnew[:, cs]), start=True, stop=True), [kdec.k, vnew.k + str(n)], [tbk])
                dve(lambda e, n=n, So=So, Sn=Sn, tb=tb: e.scalar_tensor_tensor(out=R(Sn[:]), in0=So[:], scalar=last[:, n:n + 1], in1=tb[0:64, 0:64], op0=ALU.mult, op1=ALU.add), [So.k, "c_last", tbk], [Sn.k])
                cut()
            og, sq, c1 = W["p9"], W["p10"], CL(10)
            act(lambda e: e.copy(out=og[:], in_=ob[0:64, 0:BLK]), [obk], [og.k])
            dve(lambda e: e.tensor_tensor(out=sq[:], in0=og[:], in1=og[:], op=ALU.mult), [og.k], [sq.k])
            dve(lambda e: e.tensor_reduce(out=c1, in_=v3(sq[:]), axis=AX.X, op=ALU.add), [sq.k], ["c_c1"])
            act(lambda e: e.activation(out=c1, in_=c1, func=AF.Ln, scale=1.0 / 64, bias=1e-6), ["c_c1"], ["c_c1"])
            act(lambda e: e.activation(out=c1, in_=c1, func=AF.Exp, scale=-0.5), ["c_c1"], ["c_c1"])
            dve(lambda e: e.tensor_tensor(out=v3(og[:]), in0=v3(og[:]), in1=bc(c1), op=ALU.mult), [og.k, "c_c1"], [og.k])
            cut()
            bk, bkk = transp8(st, og, og.k)
            dve(lambda e, bk=bk: e.scalar_tensor_tensor(out=yo[:, 0, :], in0=bk[0:64, 0:BLK], scalar=PP(PC_GNW), in1=sg[:], op0=ALU.mult, op1=ALU.mult), [bkk, sg.k] + DPK, [yo.k + "0"])
            cut()

        def hgrn():
            W, st = WH, SH
            hq, hsg, hv, hgt = W["p0"], W["p1"], W["p2"], W["p3"]
            bk, bkk = proj2(st, HQ)
            act(lambda e, bk=bk: e.activation(out=hq[:], in_=bk[0:64, 0:BLK], func=AF.Silu), [bkk], [hq.k])
            act(lambda e, bk=bk: e.activation(out=hsg[:], in_=bk[64:128, 0:BLK], func=AF.Sigmoid), [bkk], [hsg.k])
            cut()
            bk, bkk = proj2(st, HI)
            act(lambda e, bk=bk: e.copy(out=hv[:], in_=bk[0:64, 0:BLK]), [bkk], [hv.k])
            act(lambda e, bk=bk: e.activation(out=hgt[:], in_=bk[64:128, 0:BLK], func=AF.Silu), [bkk], [hgt.k])
            cut()
            hk, lg, bT = W["p4"], W["p5"], W["p6"]
            dve(lambda e: e.tensor_scalar(out=hk[:], in0=hsg[:], scalar1=DP(D_NOML), scalar2=DP(D_OML), op0=ALU.mult, op1=ALU.add), [hsg.k] + DPK, [hk.k])
            dve(lambda e: e.tensor_scalar(out=lg[:], in0=hsg[:], scalar1=DP(D_OML), scalar2=DP(D_LB), op0=ALU.mult, op1=ALU.add), [hsg.k] + DPK, [lg.k])
            act(lambda e: e.activation(out=lg[:], in_=lg[:], func=AF.Ln), [lg.k], [lg.k])
            dve(lambda e: e.tensor_tensor_scan(out=bT[:], data0=C(C_RST), data1=lg[:], initial=0.0, op0=ALU.mult, op1=ALU.add), [lg.k, cst.k], [bT.k])
            cut()
            bmid, blast, ebl = CL(11), CL(12), CL(13)
            dve(lambda e: e.tensor_copy(out=bmid, in_=v3(bT[:])[:, :, 31]), [bT.k], ["c_bmid"])
            dve(lambda e: e.tensor_copy(out=blast, in_=v3(bT[:])[:, :, 63]), [bT.k], ["c_blast"])
            act(lambda e: e.activation(out=ebl, in_=blast, func=AF.Exp), ["c_blast"], ["c_ebl"])
            qtT, ktT, qbT, kpT = W["r0"], W["r1"], W["r2"], W["p7"]
            tA, tB, tC, t1, t2 = W["p8"], W["p9"], W["p10"], W["p11"], W["p12"]
            dve(lambda e: e.tensor_tensor(out=v3(t1[:]), in0=v3(bT[:]), in1=bc(bmid), op=ALU.subtract), [bT.k, "c_bmid"], [t1.k])
            dve(lambda e: e.tensor_tensor(out=v3(t2[:]), in0=v3(bT[:]), in1=bc(blast), op=ALU.subtract), [bT.k, "c_blast"], [t2.k])
            act(lambda e: e.activation(out=tA[:], in_=t1[:], func=AF.Exp), [t1.k], [tA.k])
            act(lambda e: e.activation(out=tB[:], in_=t1[:], func=AF.Exp, scale=-1.0), [t1.k], [tB.k])
            act(lambda e: e.activation(out=tC[:], in_=bT[:], func=AF.Exp), [bT.k], [tC.k])
            act(lambda e: e.activation(out=kpT[:], in_=t2[:], func=AF.Exp, scale=-1.0), [t2.k], [kpT.k])
            cut()
            dve(lambda e: e.tensor_tensor(out=R(qtT[:]), in0=tA[:], in1=hq[:], op=ALU.mult), [tA.k, hq.k], [qtT.k])
            dve(lambda e: e.tensor_tensor(out=R(ktT[:]), in0=tB[:], in1=hk[:], op=ALU.mult), [tB.k, hk.k], [ktT.k])
            dve(lambda e: e.tensor_tensor(out=R(qbT[:]), in0=tC[:], in1=hq[:], op=ALU.mult), [tC.k, hq.k], [qbT.k])
            dve(lambda e: e.tensor_tensor(out=kpT[:], in0=kpT[:], in1=hk[:], op=ALU.mult), [kpT.k, hk.k], [kpT.k])
            cut()
            attm, kpTM, hvTM = W["r3"], W["r4"], W["r5"]
            bk, bkk = chunk_mm(st, ktT, ktT.k, qtT, qtT.k)
            dve(lambda e, bk=bk: e.tensor_tensor(out=R(attm[:]), in0=bk[0:64, 0:BLK], in1=C(C_M01I), op=ALU.mult), [bkk, cst.k], [attm.k])
            cut()
            bk, bkk = transp8(st, kpT, kpT.k)
            act(lambda e, bk=bk: e.copy(out=R(kpTM[:]), in_=bk[0:64, 0:BLK]), [bkk], [kpTM.k])
            cut()
            bk, bkk = transp8(st, hv, hv.k)
            act(lambda e, bk=bk: e.copy(out=R(hvTM[:]), in_=bk[0:64, 0:BLK]), [bkk], [hvTM.k])
            cut()
            ob, obk = st.ob, st.obk
            for n in range(NCH):
                cs = slice(n * 64, (n + 1) * 64)
                ci = b * NCH + n
                So, Sn = S["h"][ci % 2], S["h"][(ci + 1) % 2]
                pe(lambda e, cs=cs: e.matmul(ob[0:64, cs], lhsT=R(attm[:, cs]), rhs=R(hvTM[:, cs]), start=True, stop=False), [attm.k, hvTM.k], [obk])
                pe(lambda e, cs=cs, So=So: e.matmul(ob[0:64, cs], lhsT=R(qbT[:, cs]), rhs=R(So[:]), start=False, stop=True), [qbT.k, So.k], [obk])
                tc_, tck = st.nb()
                pe(lambda e, cs=cs, tc_=tc_: e.matmul(tc_[0:64, 0:64], lhsT=R(kpTM[:, cs]), rhs=R(hvTM[:, cs]), start=True, stop=True), [kpTM.k, hvTM.k], [tck])
                dve(lambda e, n=n, So=So, Sn=Sn, tc_=tc_: e.scalar_tensor_tensor(out=R(Sn[:]), in0=So[:], scalar=ebl[:, n:n + 1], in1=tc_[0:64, 0:64], op0=ALU.mult, op1=ALU.add), [So.k, "c_ebl", tck], [Sn.k])
                cut()
            oh, sqh = W["p8"], W["p13"]
            act(lambda e: e.copy(out=oh[:], in_=ob[0:64, 0:BLK]), [obk], [oh.k])
            dve(lambda e: e.tensor_tensor(out=sqh[:], in0=oh[:], in1=oh[:], op=ALU.mult), [oh.k], [sqh.k])
            dve(lambda e: e.tensor_reduce(out=ssall[:, b * NCH:(b + 1) * NCH], in_=v3(sqh[:]), axis=AX.X, op=ALU.add), [sqh.k], [ssall.k])
            cut()
            bk, bkk = transp8(st, oh, oh.k)
            dve(lambda e, bk=bk: e.scalar_tensor_tensor(out=yo[:, 2, :], in0=bk[0:64, 0:BLK], scalar=PP(PC_HNW), in1=hgt[:], op0=ALU.mult, op1=ALU.mult), [bkk, hgt.k] + DPK, [yo.k + "2"])
            cut()

        def rwkv():
            W, st = WR, SR
            for g0 in (RR, RV, RWD):
                bk, bkk = proj2(st, g0)
                raw_evac(g0, bk, bkk, 0); raw_evac(g0 + 1, bk, bkk, 1)
                cut()
            def shift(g, outt, mi):
                r_ = raw[g][s]
                dve(lambda e: e.tensor_scalar(out=outt[:], in0=r_[:, 3:BLK + 3], scalar1=DP(D_OMM + mi), scalar2=None, op0=ALU.mult), [r_.k] + DPK, [outt.k])
                dve(lambda e: e.scalar_tensor_tensor(out=outt[:], in0=r_[:, 2:BLK + 2], scalar=PP(PC_MU + mi), in1=outt[:], op0=ALU.mult, op1=ALU.add), [r_.k, outt.k], [outt.k])
                cut()
            rr_, rk_, rv_, rwd_, rad_, rg_ = W["p0"], W["p1"], W["p2"], W["p3"], W["p4"], W["p5"]
            shift(RR, rr_, 0); shift(RK, rk_, 1); shift(RV, rv_, 2); shift(RWD, rwd_, 3); shift(RAD, rad_, 4); shift(RG, rg_, 5)
            act(lambda e: e.activation(out=rg_[:], in_=rg_[:], func=AF.Silu), [rg_.k], [rg_.k])
            act(lambda e: e.activation(out=rwd_[:], in_=rwd_[:], func=AF.Tanh), [rwd_.k], [rwd_.k])
            lw, aa = W["p6"], W["p7"]
            bk, bkk = st.nb()
            pe(lambda e, bk=bk: e.matmul(bk[0:64, 0:BLK], lhsT=lora[:, 0:64], rhs=rwd_[:], start=True, stop=True), [lora.k, rwd_.k], [bkk])
            act(lambda e, bk=bk: e.activation(out=lw[:], in_=bk[0:64, 0:BLK], func=AF.Exp, scale=-1.0, bias=DP(D_NW0)), [bkk] + DPK, [lw.k])
            dve(lambda e: e.tensor_scalar(out=lw[:], in0=lw[:], scalar1=1.0, scalar2=None, op0=ALU.add), [lw.k], [lw.k])
            dve(lambda e: e.reciprocal(out=lw[:], in_=lw[:]), [lw.k], [lw.k])
            dve(lambda e: e.tensor_scalar(out=lw[:], in0=lw[:], scalar1=-0.6065306597126334, scalar2=None, op0=ALU.mult), [lw.k], [lw.k])
            cut()
            bk, bkk = st.nb()
            pe(lambda e, bk=bk: e.matmul(bk[0:64, 0:BLK], lhsT=lora[:, 64:128], rhs=rad_[:], start=True, stop=True), [lora.k, rad_.k], [bkk])
            act(lambda e, bk=bk: e.activation(out=aa[:], in_=bk[0:64, 0:BLK], func=AF.Sigmoid, bias=PP(PC_A0)), [bkk] + DPK, [aa.k])
            cut()
            kk, sqk, kkr = W["r0"], W["r1"], W["p8"]
            dve(lambda e: e.tensor_scalar(out=kkr[:], in0=rk_[:], scalar1=PP(PC_KK), scalar2=None, op0=ALU.mult), [rk_.k] + DPK, [kkr.k])
            l2n(st, kkr, kk, 1.0, sqk, W["p9"])
            k2 = W["p10"]
            dve(lambda e: e.tensor_scalar(out=k2[:], in0=aa[:], scalar1=PP(PC_KA), scalar2=DP(D_OMKA), op0=ALU.mult, op1=ALU.add), [aa.k] + DPK, [k2.k])
            dve(lambda e: e.tensor_tensor(out=k2[:], in0=k2[:], in1=rk_[:], op=ALU.mult), [k2.k, rk_.k], [k2.k])
            bon, bonF = W["r2"], W["p11"]
            dve(lambda e: e.scalar_tensor_tensor(out=R(bon[:]), in0=rr_[:], scalar=PP(PC_RK), in1=k2[:], op0=ALU.mult, op1=ALU.mult), [rr_.k, k2.k] + DPK, [bon.k])
            bk, bkk = ones_mm(st, bon[:], bon.k)
            dve(lambda e, bk=bk: e.tensor_tensor(out=bonF[:], in0=bk[0:64, 0:BLK], in1=rv_[:], op=ALU.mult), [bkk, rv_.k], [bonF.k])
            cut()
            cT, cxT = W["p12"], W["p13"]
            dve(lambda e: e.tensor_tensor_scan(out=cT[:], data0=C(C_RST), data1=lw[:], initial=0.0, op0=ALU.mult, op1=ALU.add), [lw.k, cst.k], [cT.k])
            dve(lambda e: e.tensor_tensor(out=cxT[:], in0=cT[:], in1=lw[:], op=ALU.subtract), [cT.k, lw.k], [cxT.k])
            clast, ecl = CL(15), CL(16)
            dve(lambda e: e.tensor_copy(out=clast, in_=v3(cT[:])[:, :, 63]), [cT.k], ["c_clast"])
            act(lambda e: e.activation(out=ecl, in_=clast, func=AF.Exp), ["c_clast"], ["c_ecl"])
            atT, rtT, btT, ktT2, bhT, khT = W["r3"], W["r4"], W["r5"], W["r6"], W["p14"], W["p15"]
            ec, enc, ecx, ecc = W["p16"], W["p17"], W["p18"], W["p19"]
            act(lambda e: e.activation(out=ec[:], in_=cT[:], func=AF.Exp), [cT.k], [ec.k])
            act(lambda e: e.activation(out=enc[:], in_=cT[:], func=AF.Exp, scale=-1.0), [cT.k], [enc.k])
            act(lambda e: e.activation(out=ecx[:], in_=cxT[:], func=AF.Exp), [cxT.k], [ecx.k])
            dve(lambda e: e.tensor_tensor(out=v3(ecc[:]), in0=v3(cT[:]), in1=bc(clast), op=ALU.subtract), [cT.k, "c_clast"], [ecc.k])
            act(lambda e: e.activation(out=ecc[:], in_=ecc[:], func=AF.Exp, scale=-1.0), [ecc.k], [ecc.k])
            cut()
            ka = W["p20"]
            dve(lambda e: e.tensor_tensor(out=ka[:], in0=kk[:], in1=aa[:], op=ALU.mult), [kk.k, aa.k], [ka.k])
            dve(lambda e: e.scalar_tensor_tensor(out=R(atT[:]), in0=kk[:], scalar=-1.0, in1=ecx[:], op0=ALU.mult, op1=ALU.mult), [kk.k, ecx.k], [atT.k])
            dve(lambda e: e.tensor_tensor(out=R(rtT[:]), in0=rr_[:], in1=ec[:], op=ALU.mult), [rr_.k, ec.k], [rtT.k])
            dve(lambda e: e.tensor_tensor(out=R(btT[:]), in0=ka[:], in1=enc[:], op=ALU.mult), [ka.k, enc.k], [btT.k])
            cut()
            dve(lambda e: e.tensor_tensor(out=R(ktT2[:]), in0=k2[:], in1=enc[:], op=ALU.mult), [k2.k, enc.k], [ktT2.k])
            dve(lambda e: e.tensor_tensor(out=bhT[:], in0=ka[:], in1=ecc[:], op=ALU.mult), [ka.k, ecc.k], [bhT.k])
            dve(lambda e: e.tensor_tensor(out=khT[:], in0=k2[:], in1=ecc[:], op=ALU.mult), [k2.k, ecc.k], [khT.k])
            cut()
            Z0r, AakT, ArbT, ArkT, Y0r, Wr = W["r7"], W["r8"], W["r9"], W["r10"], W["r11"], W["r12"]
            def mmask(l, r, maskc, outt):
                bk, bkk = chunk_mm(st, l, l.k, r, r.k)
                dve(lambda e, bk=bk: e.tensor_tensor(out=R(outt[:]), in0=bk[0:64, 0:BLK], in1=C(maskc), op=ALU.mult), [bkk, cst.k], [outt.k])
                cut()
            mmask(btT, atT, C_M01S, Z0r)
            bk, bkk = transp8(st, Z0r, Z0r.k)
            act(lambda e, bk=bk: e.copy(out=R(Y0r[:]), in_=bk[0:64, 0:BLK]), [bkk], [Y0r.k])
            cut()
            mmask(ktT2, atT, C_M01S, AakT)
            mmask(btT, rtT, C_M01I, ArbT)
            mmask(ktT2, rtT, C_M01I, ArkT)
            aTM, bhTM, khTM, vTM = W["r15"], W["r16"], W["r17"], W["r18"]
            for src, dst in ((atT, aTM), (bhT, bhTM), (khT, khTM), (rv_, vTM)):
                bk, bkk = transp8(st, src, src.k)
                act(lambda e, bk=bk, dst=dst: e.copy(out=R(dst[:]), in_=bk[0:64, 0:BLK]), [bkk], [dst.k])
                cut()
            neumann(st, Z0r, Y0r, Wr, W["r13"], W["r14"])
            ApT, X2, UV = W["r19"], W["r20"], W["p21"]
            bk, bkk = chunk_mm(st, aTM, aTM.k, Wr, Wr.k)
            act(lambda e, bk=bk: e.copy(out=R(ApT[:]), in_=bk[0:64, 0:BLK]), [bkk], [ApT.k])
            cut()
            bk, bkk = chunk_mm(st, AakT, AakT.k, vTM, vTM.k)
            act(lambda e, bk=bk: e.copy(out=R(X2[:]), in_=bk[0:64, 0:BLK]), [bkk], [X2.k])
            cut()
            bk, bkk = chunk_mm(st, Wr, Wr.k, X2, X2.k)
            act(lambda e, bk=bk: e.copy(out=UV[:], in_=bk[0:64, 0:BLK]), [bkk], [UV.k])
            cut()
            ob, obk = st.ob, st.obk
            Ut = W["r21"]
            for n in range(NCH):
                cs = slice(n * 64, (n + 1) * 64)
                ci = b * NCH + n
                So, Sn = S["r"][ci % 2], S["r"][(ci + 1) % 2]
                td, tdk = st.nb()
                pe(lambda e, cs=cs, So=So, td=td: e.matmul(td[0:64, 0:64], lhsT=R(ApT[:, cs]), rhs=R(So[:]), start=True, stop=True), [ApT.k, So.k], [tdk])
                dve(lambda e, cs=cs, td=td: e.tensor_tensor(out=R(Ut[:, cs]), in0=UV[:, cs], in1=td[0:64, 0:64], op=ALU.add), [UV.k, tdk], [Ut.k + str(n)])
                pe(lambda e, cs=cs, So=So: e.matmul(ob[0:64, cs], lhsT=R(rtT[:, cs]), rhs=R(So[:]), start=True, stop=False), [rtT.k, So.k], [obk])
                pe(lambda e, cs=cs: e.matmul(ob[0:64, cs], lhsT=R(ArbT[:, cs]), rhs=R(Ut[:, cs]), start=False, stop=False), [ArbT.k, Ut.k + str(n)], [obk])
                pe(lambda e, cs=cs: e.matmul(ob[0:64, cs], lhsT=R(ArkT[:, cs]), rhs=R(vTM[:, cs]), start=False, stop=True), [ArkT.k, vTM.k], [obk])
                te, tek = st.nb()
                pe(lambda e, cs=cs, te=te: e.matmul(te[0:64, 0:64], lhsT=R(bhTM[:, cs]), rhs=R(Ut[:, cs]), start=True, stop=False), [bhTM.k, Ut.k + str(n)], [tek])
                pe(lambda e, cs=cs, te=te: e.matmul(te[0:64, 0:64], lhsT=R(khTM[:, cs]), rhs=R(vTM[:, cs]), start=False, stop=True), [khTM.k, vTM.k], [tek])
                dve(lambda e, n=n, So=So, Sn=Sn, te=te: e.scalar_tensor_tensor(out=R(Sn[:]), in0=So[:], scalar=ecl[:, n:n + 1], in1=te[0:64, 0:64], op0=ALU.mult, op1=ALU.add), [So.k, "c_ecl", tek], [Sn.k])
                cut()
            orr, sqr = W["p16"], W["p17"]
            m1, m2 = CL(17), CL(18)
            act(lambda e: e.copy(out=orr[:], in_=ob[0:64, 0:BLK]), [obk], [orr.k])
            dve(lambda e: e.tensor_reduce(out=m1, in_=v3(orr[:]), axis=AX.X, op=ALU.add), [orr.k], ["c_m1"])
            dve(lambda e: e.tensor_scalar(out=m1, in0=m1, scalar1=1.0 / 64, scalar2=None, op0=ALU.mult), ["c_m1"], ["c_m1"])
            dve(lambda e: e.tensor_tensor(out=v3(orr[:]), in0=v3(orr[:]), in1=bc(m1), op=ALU.subtract), [orr.k, "c_m1"], [orr.k])
            dve(lambda e: e.tensor_tensor(out=sqr[:], in0=orr[:], in1=orr[:], op=ALU.mult), [orr.k], [sqr.k])
            dve(lambda e: e.tensor_reduce(out=m2, in_=v3(sqr[:]), axis=AX.X, op=ALU.add), [sqr.k], ["c_m2"])
            act(lambda e: e.activation(out=m2, in_=m2, func=AF.Ln, scale=1.0 / 64, bias=64e-5), ["c_m2"], ["c_m2"])
            act(lambda e: e.activation(out=m2, in_=m2, func=AF.Exp, scale=-0.5), ["c_m2"], ["c_m2"])
            dve(lambda e: e.tensor_tensor(out=v3(orr[:]), in0=v3(orr[:]), in1=bc(m2), op=ALU.mult), [orr.k, "c_m2"], [orr.k])
            cut()
            bk, bkk = transp8(st, orr, orr.k)
            dve(lambda e, bk=bk: e.tensor_scalar(out=sqr[:], in0=bk[0:64, 0:BLK], scalar1=PP(PC_LNW), scalar2=PP(PC_LNB), op0=ALU.mult, op1=ALU.add), [bkk] + DPK, [sqr.k])
            dve(lambda e: e.tensor_tensor(out=sqr[:], in0=sqr[:], in1=bonF[:], op=ALU.add), [sqr.k, bonF.k], [sqr.k])
            dve(lambda e: e.tensor_tensor(out=yo[:, 1, :], in0=sqr[:], in1=rg_[:], op=ALU.mult), [sqr.k, rg_.k], [yo.k + "1"])
            cut()

        blk_streams = []
        for fn in (gdn, rwkv, hgrn):
            cur[0] = [[]]
            fn()
            blk_streams.append([g for g in cur[0] if g])
            cur[0] = None
        all_streams.append(blk_streams)
        stores.append(lambda b=b, s=s, yo=yo: p.dma("sp", f"d_y{s}", A["y_block"](b), yo[:], reads=[yo.k + "0", yo.k + "1", yo.k + "2"], writes=[A["y_key"]]))

    ns = len(all_streams[0])
    tot = [sum(len(all_streams[b][i]) for b in range(NB)) for i in range(ns)]
    blk = [0] * ns; gi = [0] * ns; done = [0] * ns
    loaded = -1; stored = -1
    def advance(i):
        while blk[i] < NB and gi[i] >= len(all_streams[blk[i]][i]):
            blk[i] += 1; gi[i] = 0
    for i in range(ns):
        advance(i)
    while True:
        active = [i for i in range(ns) if blk[i] < NB]
        mb = min(blk)
        while stored < min(mb, NB) - 1:
            stored += 1
            stores[stored]()
        if not active:
            break
        cand = [i for i in active if blk[i] <= mb + 1]
        i = min(cand, key=lambda j: done[j] / max(1, tot[j]))
        while loaded < blk[i]:
            loaded += 1
            loads[loaded]()
        for th in all_streams[blk[i]][i][gi[i]]:
            th()
        gi[i] += 1; done[i] += 1
        advance(i)

    ncol = T // 64
    sst = Tl("sst", (128, 2, 64))
    evs = []
    for h in range((ncol + 127) // 128):
        w = min(128, ncol - h * 128)
        bk, bkk = banks[0], "bank0"
        pe(lambda e, h=h, w=w, bk=bk: e.transpose(out=bk[0:w, 0:64], in_=ssall[:, h * 128:h * 128 + w], identity=IDN), [ssall.k, cst.k], [bkk])
        act(lambda e, h=h, w=w, bk=bk: e.copy(out=sst[0:w, h, :], in_=bk[0:w, 0:64]), [bkk], [sst.k])
        evs.append(p.dma("sp", "d_ss", A["ssd"][h * 128:h * 128 + w, :], sst[0:w, h, :], reads=[sst.k], writes=[A["ss_key"]]))
    return evs


def build_mixer(nc, T, layer):
    p = Prog(nc)
    hT = nc.dram_tensor("hT", [1024, T], BF16, kind="ExternalInput").ap()
    wd = nc.dram_tensor("w", [1024, NG * 64], F32, kind="ExternalInput").ap()
    ppd = nc.dram_tensor("pp", [64, NPAR], F32, kind="ExternalInput").ap()
    lorad = nc.dram_tensor("lora", [64, 128], F32, kind="ExternalInput").ap()
    cd = nc.dram_tensor("consts", [64, NCONST * BLK], F32, kind="ExternalInput").ap()
    yT = nc.dram_tensor("yT", [192, T], BF16, kind="ExternalOutput").ap()
    ssd = nc.dram_tensor("ss", [T // 64, 64], F32, kind="ExternalOutput").ap()
    A = {"w": wd, "pp": ppd, "lora": lorad, "consts": cd, "ssd": ssd,
         "hT_block": lambda b: hT.rearrange("(kc q) t -> q kc t", q=128)[:, :, b * BLK:(b + 1) * BLK],
         "y_block": lambda b: yT.rearrange("(m c) t -> c m t", m=3)[:, :, b * BLK:(b + 1) * BLK],
         "hT_key": "hT_ext", "y_key": "y_ext", "ss_key": "ss_ext"}
    evs = mixer_body(p, nc, T, layer, A, "")
    for ev in evs:
        p.finish_wait("sp", ev)
    for s in range(2):
        if f"d_y{s}" in p.dma_count:
            p.finish_wait("sp", (f"d_y{s}", p.dma_count[f"d_y{s}"]))
    p.emit()
    p.close()
    return nc


def mixer_stage(p, nc, T, layer, hT_all, wd, ppd, lorad, cd, y_loc, ssd, tag, NT=2048):
    p.push_scope()
    bps = NT // BLK
    A = {"w": wd, "pp": ppd, "lora": lorad, "consts": cd, "ssd": ssd,
         "hT_block": lambda b: hT_all.rearrange("(r kc q) t -> r q kc t", r=8, q=128)[b // bps][:, :, (b % bps) * BLK:(b % bps + 1) * BLK],
         "y_block": lambda b: y_loc[(b // bps) * 192:(b // bps + 1) * 192, (b % bps) * BLK:(b % bps + 1) * BLK].rearrange("(m c) t -> c m t", m=3),
         "hT_key": "hT_all", "y_key": "y_loc", "ss_key": "ss_loc"}
    mixer_body(p, nc, T, layer, A, tag)
    p.pop_scope()


def make_ident_bf():
    import ml_dtypes
    return np.eye(128, dtype=np.float32).astype(ml_dtypes.bfloat16)


def on_stage(p, nc, NT, has_proj, final, xd, nwd, yall, ssall_d, wod, idd, idfd, outd, hTd, xnd, tag, xkey, ymine=None, smine=None):
    ntile = NT // 128
    p.push_scope()
    class Tl:
        def __init__(s, name, shape, dtype=F32):
            s.t = p.sb(tag + "sb_" + name, shape, dtype); s.k = name
        def __getitem__(s, idx):
            return s.t[idx]
    act = lambda fn, r, w: p.op("act", fn, reads=r, writes=w)
    dve = lambda fn, r, w: p.op("dve", fn, reads=r, writes=w)
    pool = lambda fn, r, w: p.op("pool", fn, reads=r, writes=w)
    pe = lambda fn, r, w: p.op("pe", fn, reads=r, writes=w)

    nw = Tl("nw", (128, 1024))
    p.dma("sp", "d_nw", nw[:], nwd[:, :], writes=[nw.k])
    if not final:
        idb = Tl("idb", (128, 128), BF16)
        p.dma("sp", "d_id", idb[:], idd[:, :], writes=[idb.k])
        pst = p.ps(tag + "pst", [128, 1024], BF16)
    if has_proj:
        ps1 = p.ps(tag + "ps1", [128, 1024], F32)
        ps2 = p.ps(tag + "ps2", [128, 1024], F32)
        wob = Tl("wob", (128, 12, 1024), BF16)
        wst = [Tl(f"wst{i}", (128, 1024)) for i in range(2)]
        for kc in range(12):
            s = wst[kc % 2]
            p.dma("sp", f"d_w{kc % 2}", s[:], wod[kc * 128:(kc + 1) * 128, :], writes=[s.k])
            if kc % 2 == 0:
                act(lambda e: e.copy(out=wob[:, kc, :], in_=s[:]), [s.k], [f"wob{kc}"])
            else:
                dve(lambda e: e.tensor_copy(out=wob[:, kc, :], in_=s[:]), [s.k], [f"wob{kc}"])
        ysb = Tl("ysb", (128, 12, NT), BF16)
        def yfn(e):
            pid = e.partition_id()
            return e.dma_start(out=ymine.rearrange("(h r) t -> h r t", h=8), in_=yall.rearrange("(h j) t -> h j t", h=8)[:, bass.ds(pid * 192, 192), :])
        p.dma_fn("pool", "d_ym", yfn, reads=["y_all"], writes=["y_mine"])
        def sfn(e):
            pid = e.partition_id()
            return e.dma_start(out=smine.rearrange("(h r) i -> h r i", h=8), in_=ssall_d.rearrange("(h j) i -> h j i", h=8)[:, bass.ds(pid * 32, 32), :])
        p.dma_fn("pool", "d_sm", sfn, reads=["ss_all"], writes=["ss_mine"])
        for h in range(8):
            for m in range(3):
                p.dma("sp", f"d_y{(h * 3 + m) % 4}", ysb[(h % 2) * 64:(h % 2 + 1) * 64, m * 4 + h // 2, :],
                      ymine[h * 192 + m * 64:h * 192 + (m + 1) * 64, :], reads=["y_mine"], writes=["ysb"])
        ssT = Tl("ssT", (128, 128))
        p.dma("sp", "d_ss", ssT[:], smine.rearrange("(ha two) i -> ha (two i)", two=2), reads=["ss_mine"], writes=[ssT.k])
        idf = Tl("idf", (128, 128))
        p.dma("sp", "d_idf", idf[:], idfd[:, :], writes=[idf.k])
        pss = p.ps(tag + "pss", [128, 512], F32)
        pe(lambda e: e.transpose(out=pss[:, 0:128], in_=ssT[:], identity=idf[:]), [ssT.k, idf.k], ["pss"])
        rst = Tl("rst", (128, ntile))
        dve(lambda e: e.tensor_reduce(out=rst[:], in_=pss[:, 0:128].rearrange("p (h a) -> p a h", h=8), axis=AX.X, op=ALU.add), ["pss"], [rst.k])
        act(lambda e: e.activation(out=rst[:], in_=rst[:], func=AF.Ln, scale=1.0 / 512, bias=1e-6), [rst.k], [rst.k])
        act(lambda e: e.activation(out=rst[:], in_=rst[:], func=AF.Exp, scale=-0.5), [rst.k], [rst.k])

    xt = [Tl(f"xt{i}", (128, 1024)) for i in range(2)]
    xn = [Tl(f"xn{i}", (128, 1024)) for i in range(2)]
    sq = Tl("sq", (128, 1024))
    hb = [Tl(f"hb{i}", (128, 1024), F32 if final else BF16) for i in range(2)]
    hTs = [Tl(f"hTs{i}", (128, 8, 128), BF16) for i in range(2)]
    cc = Tl("cc", (128, 2 * ntile))
    evs = []
    for t in range(ntile):
        s = t % 2
        ts_ = slice(t * 128, (t + 1) * 128)
        x_ = xt[s]
        p.dma("sp", f"d_x{s}", x_[:], xd[ts_, :], reads=[xkey], writes=[x_.k])
        if has_proj:
            q = t // (ntile // 4)
            for half in range(2):
                hs = slice(half * 512, (half + 1) * 512)
                for kc in range(8):
                    pe(lambda e: e.matmul(ps1[:, hs], lhsT=ysb[:, kc, ts_], rhs=wob[:, kc, hs], start=(kc == 0), stop=(kc == 7)), ["ysb", f"wob{kc}"], ["ps1"])
                for kc in range(8, 12):
                    pe(lambda e: e.matmul(ps2[:, hs], lhsT=ysb[:, kc, ts_], rhs=wob[:, kc, hs], start=(kc == 8), stop=(kc == 11)), ["ysb", f"wob{kc}"], ["ps2"])
            xo = xn[s]
            dve(lambda e: e.tensor_tensor(out=xo[:], in0=ps1[:, :], in1=x_[:], op=ALU.add), ["ps1", x_.k], [xo.k])
            dve(lambda e: e.scalar_tensor_tensor(out=xo[:], in0=ps2[:, :], scalar=rst[:, t:t + 1], in1=xo[:], op0=ALU.mult, op1=ALU.add), ["ps2", rst.k, xo.k], [xo.k])
            if not final:
                evs.append(p.dma("sp", f"d_xo{s}", xnd[ts_, :], xo[:], reads=[xo.k], writes=["xres"]))
        else:
            xo = x_
        c1 = cc[:, 2 * t:2 * t + 1]
        act(lambda e: e.activation(out=sq[:], in_=xo[:], func=AF.Square, accum_out=c1), [xo.k], [sq.k, f"cc{t}"])
        act(lambda e: e.activation(out=c1, in_=c1, func=AF.Ln, scale=1.0 / 1024, bias=1e-6), [f"cc{t}"], [f"cc{t}"])
        act(lambda e: e.activation(out=c1, in_=c1, func=AF.Exp, scale=-0.5), [f"cc{t}"], [f"cc{t}"])
        h_ = hb[s]
        dve(lambda e: e.scalar_tensor_tensor(out=h_[:], in0=xo[:], scalar=c1, in1=nw[:], op0=ALU.mult, op1=ALU.mult), [xo.k, f"cc{t}", nw.k], [h_.k])
        if final:
            evs.append(p.dma("sp", f"d_o{s}", outd[ts_, :], h_[:], reads=[h_.k]))
        else:
            for kc in range(8):
                pe(lambda e: e.transpose(out=pst[:, kc * 128:(kc + 1) * 128], in_=h_[:, kc * 128:(kc + 1) * 128], identity=idb[:]), [h_.k, idb.k], ["pst"])
            ht = hTs[s]
            act(lambda e: e.copy(out=ht[:].rearrange("p a b -> p (a b)"), in_=pst[:, :]), ["pst"], [ht.k])
            evs.append(p.dma("sp", f"d_h{s}", hTd.rearrange("(kc q) t -> q kc t", q=128)[:, :, ts_], ht[:], reads=[ht.k], writes=["hT_loc"]))
    if final:
        for ev in evs[-2:]:
            p.finish_wait("sp", ev)
    p.pop_scope()


T_FULL = 16384
NCORE = 8
NT_CORE = T_FULL // NCORE
GDN_PROJ = 512 * 3 + 16 + 512
RWKV_PROJ = 512 * 3 + 128 + 512


def core_inputs(inp, l, h):
    w_in = inp['w_in'][l]
    g0 = 0; r0 = GDN_PROJ; h0 = GDN_PROJ + RWKV_PROJ
    hs = slice(h * 64, (h + 1) * 64)
    def col(base, idx): return w_in[:, base + idx * 512 + h * 64: base + idx * 512 + (h + 1) * 64]
    w = np.zeros((1024, NG * 64), np.float32)
    w[:, GQ*64:(GQ+1)*64] = col(g0, 0); w[:, GK*64:(GK+1)*64] = col(g0, 1); w[:, GV*64:(GV+1)*64] = col(g0, 2)
    w[:, GSC*64 + 0] = w_in[:, g0 + 1536 + h]; w[:, GSC*64 + 32] = w_in[:, g0 + 1536 + 8 + h]
    w[:, GG*64:(GG+1)*64] = w_in[:, g0 + 1552 + h*64: g0 + 1552 + (h+1)*64]
    w[:, RR*64:(RR+1)*64] = col(r0, 0); w[:, RK*64:(RK+1)*64] = col(r0, 1); w[:, RV*64:(RV+1)*64] = col(r0, 2)
    w[:, RWD*64:(RWD+1)*64] = w_in[:, r0 + 1536: r0 + 1600]; w[:, RAD*64:(RAD+1)*64] = w_in[:, r0 + 1600: r0 + 1664]
    w[:, RG*64:(RG+1)*64] = w_in[:, r0 + 1664 + h*64: r0 + 1664 + (h+1)*64]
    for gi, g in enumerate((HQ, HF, HI, HGT)):
        w[:, g*64:(g+1)*64] = col(h0, gi)
    pp = np.zeros((64, NPAR), np.float32)
    cw = inp['gdn_conv_w'][l]
    for ci in range(3):
        for j in range(4):
            pp[:, PC_CONV + ci*4 + j] = cw[j, ci*512 + h*64: ci*512 + (h+1)*64]
    pp[:, PC_GNW] = inp['gdn_norm_w'][l]
    pp[:, PC_ALOG] = inp['gdn_a_log'][l, h]; pp[:, PC_DTB] = inp['gdn_dt_bias'][l, h]
    mu = inp['rwkv_mu'][l]
    pp[:, PC_MU+0] = mu[0+h*64:0+(h+1)*64]; pp[:, PC_MU+1] = mu[512+h*64:512+(h+1)*64]; pp[:, PC_MU+2] = mu[1024+h*64:1024+(h+1)*64]
    pp[:, PC_MU+3] = mu[1536:1600]; pp[:, PC_MU+4] = mu[1600:1664]; pp[:, PC_MU+5] = mu[1664+h*64:1664+(h+1)*64]
    pp[:, PC_W0] = inp['rwkv_w0'][l, hs]; pp[:, PC_A0] = inp['rwkv_a0'][l, hs]; pp[:, PC_KK] = inp['rwkv_k_k'][l, hs]
    pp[:, PC_KA] = inp['rwkv_k_a'][l, hs]; pp[:, PC_RK] = inp['rwkv_r_k'][l, h]; pp[:, PC_LNW] = inp['rwkv_ln_w'][l, hs]
    pp[:, PC_LNB] = inp['rwkv_ln_b'][l, hs]
    pp[:, PC_LB0] = inp['hgrn_lower_bounds'][0, hs]; pp[:, PC_LB1] = inp['hgrn_lower_bounds'][1, hs]
    pp[:, PC_HNW] = inp['hgrn_norm_w'][l, hs]
    lora = np.concatenate([inp['rwkv_w_up'][l][:, hs], inp['rwkv_a_up'][l][:, hs]], axis=1).astype(np.float32)
    return w, pp, np.ascontiguousarray(lora)


def build_fused(nc):
    p = Prog(nc)
    NT = NT_CORE
    ext = lambda n, s, d: nc.dram_tensor(n, list(s), d, kind="ExternalInput").ap()
    internal = lambda n, s, d: nc.dram_tensor(n, list(s), d, kind="Internal").ap()
    x = ext("x", (NT, 1024), F32)
    nws = [ext(f"nw{i}", (128, 1024), F32) for i in range(3)]
    ws = [ext(f"w{l}", (1024, NG * 64), F32) for l in range(2)]
    pps = [ext(f"pp{l}", (64, NPAR), F32) for l in range(2)]
    loras = [ext(f"lora{l}", (64, 128), F32) for l in range(2)]
    wos = [ext(f"wo{l}", (1536, 1024), F32) for l in range(2)]
    cd = ext("consts", (64, NCONST * BLK), F32)
    idb = ext("identb", (128, 128), BF16)
    idf = ext("identf", (128, 128), F32)
    out = nc.dram_tensor("out", [NT, 1024], F32, kind="ExternalOutput").ap()
    hT_loc = internal("hT_loc", (1024, NT), BF16)
    hT_all = [internal(f"hT_all{l}", (8 * 1024, NT), BF16) for l in range(2)]
    y_loc = internal("y_loc", (1536, NT), BF16)
    y_all = [internal(f"y_all{l}", (8 * 1536, NT), BF16) for l in range(2)]
    ss_loc = internal("ss_loc", (T_FULL // 64, 64), F32)
    ss_all = [internal(f"ss_all{l}", (8 * (T_FULL // 64), 64), F32) for l in range(2)]
    xres = internal("xres", (NT, 1024), F32)
    ymine = internal("y_mine", (8 * 192, NT), BF16)
    smine = internal("ss_mine", (8 * 32, 64), F32)
    rg = [list(range(NCORE))]
    ncc = [0]
    def allgather(src, dst, rkey, wkey):
        ncc[0] += 1
        p.dma_fn("pool", f"cc{ncc[0]}",
                 lambda e, src=src, dst=dst: e.collective_compute("AllGather", ALU.bypass, replica_groups=rg, ins=[src[:, :]], outs=[dst[:, :]]),
                 reads=[rkey, "cc_chain"], writes=[wkey, "cc_chain"], inc=1)

    on_stage(p, nc, NT, False, False, x, nws[0], None, None, None, idb, idf, None, hT_loc, None, "n0_", "x_ext")
    allgather(hT_loc, hT_all[0], "hT_loc", "hT_all")
    for l in range(2):
        mixer_stage(p, nc, T_FULL, l, hT_all[l], ws[l], pps[l], loras[l], cd, y_loc, ss_loc, f"m{l}_")
        allgather(y_loc, y_all[l], "y_loc", "y_all")
        allgather(ss_loc, ss_all[l], "ss_loc", "ss_all")
        if l == 0:
            on_stage(p, nc, NT, True, False, x, nws[1], y_all[0], ss_all[0], wos[0], idb, idf, None, hT_loc, xres, "o0_", "x_ext", ymine, smine)
            allgather(hT_loc, hT_all[1], "hT_loc", "hT_all")
        else:
            on_stage(p, nc, NT, True, True, xres, nws[2], y_all[1], ss_all[1], wos[1], idb, idf, out, None, None, "o1_", "xres", ymine, smine)
    p.emit()
    p.close()
    return nc


def kernel(**inputs):
    inp = {k: np.ascontiguousarray(np.asarray(v)) for k, v in inputs.items()}
    x = inp['x'][0]
    consts = make_consts()
    idb = make_ident_bf()
    idf = np.eye(128, dtype=np.float32)
    rep = lambda v: np.ascontiguousarray(np.broadcast_to(v[None, :], (128, 1024))).astype(np.float32)
    nw = [rep(inp['norm_w'][0]), rep(inp['norm_w'][1]), rep(inp['final_norm_w'])]
    maps = []
    for c in range(NCORE):
        m = {"x": np.ascontiguousarray(x[c * NT_CORE:(c + 1) * NT_CORE]), "consts": consts, "identb": idb, "identf": idf}
        for i in range(3):
            m[f"nw{i}"] = nw[i]
        for l in range(2):
            w, pp, lora = core_inputs(inp, l, c)
            m[f"w{l}"] = w; m[f"pp{l}"] = pp; m[f"lora{l}"] = lora
            m[f"wo{l}"] = inp['w_out'][l]
        maps.append(m)
    nc = bass.Bass("TRN2", target_bir_lowering=False, num_devices=NCORE)
    build_fused(nc)
    res = run_bass_kernel_spmd(nc, maps, core_ids=list(range(NCORE))).results
    out = np.concatenate([res[c]["out"] for c in range(NCORE)], axis=0)
    return out[None].astype(np.float32)
```

```python
from concourse.bass_utils import run_bass_kernel_spmd
from contextlib import ExitStack
import numpy as np
import concourse.bass as bass
import concourse.mybir as mybir

F32 = mybir.dt.float32
BF16 = mybir.dt.bfloat16
AF = mybir.ActivationFunctionType
ALU = mybir.AluOpType
AX = mybir.AxisListType

COMPUTE = ("pe", "act", "dve", "pool")


class _Rec:
    def __init__(self):
        self.call = None
    def __getattr__(self, name):
        def f(*a, **kw):
            self.call = (name, a, kw)
            return self
        return f


def _bind(fn):
    r = _Rec()
    fn(r)
    name, a, kw = r.call
    return lambda e: getattr(e, name)(*a, **kw)


class Prog:
    def __init__(self, nc):
        self.nc = nc
        self.es = ExitStack()
        self.lists = {e: [] for e in ("pe", "act", "dve", "pool", "sp")}
        self.count = {e: 0 for e in COMPUTE}
        self.waited = {e: {} for e in self.lists}
        self.lastw = {}
        self.lastr = {}
        self.sems = {}
        self.dma_count = {}
        self.psum_banks = []
        self.psum_rr = 0
        self.final_waits = {}
        self.scopes = [self.es]
        self.handles = {"pe": nc.tensor, "act": nc.scalar, "dve": nc.vector, "pool": nc.gpsimd, "sp": nc.sync}
        for e in COMPUTE:
            self.sem(e)

    def sb(self, name, shape, dtype=F32):
        return self.scopes[-1].enter_context(self.nc.sbuf_tensor(name, list(shape), dtype))

    def ps(self, name, shape, dtype=F32):
        return self.scopes[-1].enter_context(self.nc.psum_tensor(name, list(shape), dtype))

    def push_scope(self):
        self.scopes.append(ExitStack())

    def pop_scope(self):
        self.barrier()
        self.scopes.pop().close()

    def barrier(self):
        cur = {e: c for e, c in self.count.items() if c > 0}
        cur.update(self.dma_count)
        for eng in self.lists:
            ws = []
            wd = self.waited[eng]
            for s_, v in cur.items():
                if wd.get(s_, 0) >= v:
                    continue
                wd[s_] = v
                ws.append((s_, v))
            if ws:
                self._issue(eng, ws, None, None)

    def sem(self, name):
        if name not in self.sems:
            self.sems[name] = self.es.enter_context(self.nc.semaphore(name))
        return self.sems[name]

    def _deps(self, reads, writes):
        deps = {}
        def add(d):
            for s, v in d.items():
                if deps.get(s, 0) < v:
                    deps[s] = v
        for k in reads:
            add(self.lastw.get(k, {}))
        for k in writes:
            add(self.lastw.get(k, {}))
            add(self.lastr.get(k, {}))
        return deps

    def _record(self, reads, writes, sem, val):
        for k in reads:
            d = self.lastr.setdefault(k, {})
            d[sem] = val
        for k in writes:
            self.lastw[k] = {sem: val}
            self.lastr[k] = {}

    def _waits(self, eng, deps):
        ws = []
        wd = self.waited[eng]
        for s, v in deps.items():
            if s == "pe" and eng == "pe":
                continue
            if wd.get(s, 0) >= v:
                continue
            wd[s] = v
            ws.append((s, v))
        return ws

    def op(self, eng, fn, reads=(), writes=()):
        deps = self._deps(reads, writes)
        ws = self._waits(eng, deps)
        self.count[eng] += 1
        val = self.count[eng]
        for k in writes:
            prev = self.lastw.get(k, {})
            prevr = self.lastr.get(k, {})
            self.lastw[k] = dict(prev)
            self.lastw[k][eng] = val
            self.lastr[k] = {}
        for k in reads:
            d = self.lastr.setdefault(k, {})
            d[eng] = val
        self._issue(eng, ws, _bind(fn), (eng, 1))

    def dma(self, queue, semname, out, in_, reads=(), writes=(), **kw):
        deps = self._deps(reads, writes)
        ws = self._waits(queue, deps)
        self.dma_count[semname] = self.dma_count.get(semname, 0) + 16
        val = self.dma_count[semname]
        self.sem(semname)
        for k in writes:
            prev = self.lastw.get(k, {})
            self.lastw[k] = dict(prev)
            self.lastw[k][semname] = val
            self.lastr[k] = {}
        for k in reads:
            d = self.lastr.setdefault(k, {})
            d[semname] = val
        fn = lambda e, out=out, in_=in_, kw=kw: e.dma_start(out=out, in_=in_, **kw)
        self._issue(queue, ws, fn, (semname, 16))
        return (semname, val)

    def dma_fn(self, queue, semname, fn, reads=(), writes=(), inc=16):
        deps = self._deps(reads, writes)
        ws = self._waits(queue, deps)
        self.dma_count[semname] = self.dma_count.get(semname, 0) + inc
        val = self.dma_count[semname]
        self.sem(semname)
        for k in writes:
            prev = self.lastw.get(k, {})
            self.lastw[k] = dict(prev)
            self.lastw[k][semname] = val
            self.lastr[k] = {}
        for k in reads:
            d = self.lastr.setdefault(k, {})
            d[semname] = val
        self._issue(queue, ws, fn, (semname, inc))
        return (semname, val)

    def finish_wait(self, queue, ev):
        self._issue(queue, [ev], None, None)

    def _issue(self, ename, ws, fn, si):
        self.lists[ename].append((ws, fn, si))

    def emit(self):
        attrs = {"pe": "tensor", "act": "scalar", "dve": "vector", "pool": "gpsimd", "sp": "sync"}
        with self.nc.Block() as block:
            for ename, attr in attrs.items():
                lst = self.lists[ename]
                if not lst:
                    continue

                def body(eng, lst=lst):
                    for ws, fn, si in lst:
                        for s_, v in ws:
                            eng.wait_ge(self.sems[s_], v)
                        if fn is None:
                            continue
                        sname, inc = si
                        if inc == 1 and sname.startswith("cc"):
                            fn(eng).then_inc(self.sems[sname])
                        else:
                            fn(eng).then_inc(self.sems[sname], inc)
                getattr(block, attr)(body)

    def close(self):
        self.es.close()


NG = 15
GQ, GK, GV, GG, RR, RK, RV, RG, RWD, RAD, HQ, HF, HI, HGT, GSC = range(15)
PC_CONV = 0; PC_GNW = 12; PC_ALOG = 13; PC_DTB = 14; PC_MU = 15
PC_W0 = 21; PC_A0 = 22; PC_KK = 23; PC_KA = 24; PC_RK = 25; PC_LNW = 26; PC_LNB = 27
PC_LB0 = 28; PC_LB1 = 29; PC_HNW = 30
NPAR = 32
C_ID, C_ONES, C_TRI, C_MNI, C_MNS, C_M01I, C_M01S, C_RST = range(8)
NCONST = 8
NCH = 4
BLK = NCH * 64


def make_consts():
    c = np.zeros((64, NCONST, BLK), np.float32)
    j = np.arange(64)[:, None]; i = np.arange(64)[None, :]
    rep = lambda m: np.tile(m.astype(np.float32), (1, NCH))
    c[:, C_ID] = rep(j == i)
    c[:, C_ONES] = 1.0
    c[:, C_TRI] = rep(j <= i)
    c[:, C_MNI] = rep(np.where(i >= j, 0.0, -30000.0))
    c[:, C_MNS] = rep(np.where(i > j, 0.0, -30000.0))
    c[:, C_M01I] = rep(i >= j)
    c[:, C_M01S] = rep(i > j)
    r = np.ones((64, BLK), np.float32); r[:, ::64] = 0.0
    c[:, C_RST] = r
    return c.reshape(64, NCONST * BLK)


def mixer_body(p, nc, T, layer, A, tag):
    NB = T // BLK
    F32R = mybir.dt.float32r
    R = lambda ap: ap.bitcast(F32R)
    v3 = lambda ap: ap.rearrange("p (n i) -> p n i", n=NCH)
    bc = lambda colap: colap.unsqueeze(2).to_broadcast([64, NCH, 64])

    class Tl:
        def __init__(s, name, shape, dtype=F32):
            s.t = p.sb(tag + "sb_" + name, shape, dtype); s.k = name
        def __getitem__(s, idx):
            return s.t[idx]
    banks = [p.ps(tag + f"bank{i}", [128, 512], F32) for i in range(8)]

    cur = [None]
    def _emit(eng, fn, r, w):
        b_ = _bind(fn)
        r = list(r); w = list(w)
        if cur[0] is None:
            p.op(eng, b_, reads=r, writes=w)
        else:
            cur[0][-1].append(lambda: p.op(eng, b_, reads=r, writes=w))
    act = lambda fn, r, w: _emit("act", fn, r, w)
    dve = lambda fn, r, w: _emit("dve", fn, r, w)
    pool = lambda fn, r, w: _emit("pool", fn, r, w)
    pe = lambda fn, r, w: _emit("pe", fn, r, w)
    def cut():
        if cur[0] is not None and cur[0][-1]:
            cur[0].append([])

    def run_streams(streams):
        tot = [sum(1 for g in s if g) for s in streams]
        streams = [[g for g in s if g] for s in streams]
        idx = [0] * len(streams)
        while True:
            best, bf = None, None
            for i, s in enumerate(streams):
                if idx[i] < len(s):
                    f = idx[i] / max(1, tot[i])
                    if bf is None or f < bf:
                        best, bf = i, f
            if best is None:
                break
            for th in streams[best][idx[best]]:
                th()
            idx[best] += 1

    cst = Tl("cst", (64, NCONST * BLK))
    pp = Tl("pp", (64, NPAR))
    lora = Tl("lora", (64, 128))
    p.dma("sp", "d_c", cst[:], A["consts"][:, :], writes=[cst.k])
    p.dma("sp", "d_p", pp[:], A["pp"][:, :], writes=[pp.k])
    p.dma("sp", "d_l", lora[:], A["lora"][:, :], writes=[lora.k])
    C = lambda c, w=BLK: cst[:, c * BLK:c * BLK + w]
    PP = lambda c: pp[:, c:c + 1]
    IDN = C(C_ID, 64); ONES = C(C_ONES, 64); TRI = C(C_TRI, 64)
    cstr = Tl("cstr", (64, 64))
    act(lambda e: e.copy(out=R(cstr[:, 0:64]), in_=ONES), [cst.k], ["cstr"])
    ONESR = cstr[:, 0:64]

    wbf = Tl("wbf", (128, 8, NG * 64), BF16)
    wst = [Tl(f"wst{i}", (128, NG * 64)) for i in range(2)]
    for kc in range(8):
        s = wst[kc % 2]
        p.dma("sp", f"d_w{kc % 2}", s[:], A["w"][kc * 128:(kc + 1) * 128, :], writes=[s.k])
        if kc % 2 == 0:
            act(lambda e: e.copy(out=wbf[:, kc, :], in_=s[:]), [s.k], [f"wbf{kc}"])
        else:
            dve(lambda e: e.tensor_copy(out=wbf[:, kc, :], in_=s[:]), [s.k], [f"wbf{kc}"])
    WBK = [f"wbf{kc}" for kc in range(8)]

    dp = Tl("dp", (64, 24))
    DP = lambda c: dp[:, c:c + 1]
    D_NA, D_OMM, D_NW0, D_OMKA, D_LB, D_OML, D_NOML, D_T0, D_T1, D_T2, D_T3, D_T4, D_T5 = 0, 1, 7, 8, 9, 10, 11, 12, 13, 14, 15, 16, 17
    act(lambda e: e.activation(out=DP(D_T0), in_=PP(PC_ALOG), func=AF.Exp), [pp.k], ["dp_t0"])
    dve(lambda e: e.tensor_scalar(out=DP(D_NA), in0=DP(D_T0), scalar1=-1.0, scalar2=None, op0=ALU.mult), ["dp_t0"], ["dp"])
    dve(lambda e: e.tensor_scalar(out=dp[:, D_OMM:D_OMM + 6], in0=pp[:, PC_MU:PC_MU + 6], scalar1=-1.0, scalar2=1.0, op0=ALU.mult, op1=ALU.add), [pp.k], ["dp"])
    dve(lambda e: e.tensor_scalar(out=DP(D_NW0), in0=PP(PC_W0), scalar1=-1.0, scalar2=None, op0=ALU.mult), [pp.k], ["dp"])
    dve(lambda e: e.tensor_scalar(out=DP(D_OMKA), in0=PP(PC_KA), scalar1=-1.0, scalar2=1.0, op0=ALU.mult, op1=ALU.add), [pp.k], ["dp"])
    act(lambda e: e.activation(out=dp[:, D_T1:D_T1 + 2], in_=pp[:, PC_LB0:PC_LB0 + 2], func=AF.Exp), [pp.k], ["dp_t1"])
    dve(lambda e: e.tensor_tensor(out=DP(D_T3), in0=DP(D_T1), in1=DP(D_T2), op=ALU.add), ["dp_t1"], ["dp_t0b"])
    dve(lambda e: e.reciprocal(out=DP(D_T4), in_=DP(D_T3)), ["dp_t0b"], ["dp_t0c"])
    if layer == 0:
        dve(lambda e: e.tensor_tensor(out=DP(D_T5), in0=DP(D_T1), in1=DP(D_T1), op=ALU.subtract), ["dp_t1"], ["dp_lb0"])
    else:
        dve(lambda e: e.tensor_tensor(out=DP(D_T5), in0=DP(D_T3), in1=DP(D_T1), op=ALU.subtract), ["dp_t0b", "dp_t1"], ["dp_lb0"])
    dve(lambda e: e.tensor_tensor(out=DP(D_T5), in0=DP(D_T5), in1=DP(D_T4), op=ALU.mult), ["dp_lb0", "dp_t0c"], ["dp_lb1"])
    dve(lambda e: e.tensor_copy(out=DP(D_LB), in_=DP(D_T5)), ["dp_lb1"], ["dp_lb"])
    dve(lambda e: e.tensor_scalar(out=DP(D_OML), in0=DP(D_LB), scalar1=-1.0, scalar2=1.0, op0=ALU.mult, op1=ALU.add), ["dp_lb"], ["dp_oml"])
    dve(lambda e: e.tensor_scalar(out=DP(D_NOML), in0=DP(D_OML), scalar1=-1.0, scalar2=None, op0=ALU.mult), ["dp_oml"], ["dp"])
    DPK = ["dp", "dp_lb", "dp_oml", pp.k, cst.k, lora.k]

    hblk = [Tl(f"hblk{i}", (128, 8, BLK), BF16) for i in range(2)]
    raw = {g: [Tl(f"raw{g}_{s}", (64, BLK + 3)) for s in range(2)] for g in (GQ, GK, GV, RR, RK, RV, RG, RWD, RAD)}
    for g in raw:
        pool(lambda e, g=g: e.memset(raw[g][1][:, BLK:BLK + 3], 0.0), [], [raw[g][1].k])
    S = {m: [Tl(f"S{m}{s}", (64, 64)) for s in range(2)] for m in ("g", "r", "h")}
    for m in S:
        act(lambda e, m=m: e.mul(out=R(S[m][0][:]), in_=ONES, mul=0.0), [cst.k], [S[m][0].k])
    srow = Tl("srow", (64, BLK))
    pool(lambda e: e.memset(srow[:], 0.0), [], [srow.k])
    ssall = Tl("ssall", (64, max(128, T // 64)))
    pool(lambda e: e.memset(ssall[:], 0.0), [], [ssall.k])
    yout = [Tl(f"yout{i}", (64, 3, BLK), BF16) for i in range(2)]
    colsG, colsH, colsR = Tl("colsG", (64, 12, 16)), Tl("colsH", (64, 16, 16)), Tl("colsR", (64, 20, 16))
    CL = lambda i: (colsG if i <= 10 else colsH if i <= 14 else colsR)[:, i, 0:NCH]

    def mk(prefix, nplain, nr):
        d = {f"p{i}": Tl(f"{prefix}_p{i}", (64, BLK)) for i in range(nplain)}
        d.update({f"r{i}": Tl(f"{prefix}_r{i}", (64, BLK)) for i in range(nr)})
        return d
    WG = mk("g", 12, 15)
    WH = mk("h", 14, 6)
    WR = mk("w", 22, 22)

    class St:
        def __init__(s, rot, ob, name):
            s.rot = rot; s.i = 0; s.ob = banks[ob]; s.obk = f"bank{ob}"; s.name = name
        def nb(s):
            i = s.rot[s.i % len(s.rot)]; s.i += 1
            return banks[i], f"bank{i}"
    SG, SR, SH = St([0, 1], 5, "g"), St([2, 3], 6, "r"), St([4], 7, "h")

    def transp8(st, src, srckey, rows=64):
        bk, bkk = st.nb()
        for n in range(NCH):
            pe(lambda e, n=n, bk=bk: e.transpose(out=bk[0:64, n * 64:n * 64 + rows], in_=src[0:rows, n * 64:(n + 1) * 64], identity=IDN[0:rows, 0:rows]),
               [srckey, cst.k], [bkk])
        return bk, bkk

    def ones_mm(st, src, srckey):
        bk, bkk = st.nb()
        pe(lambda e: e.matmul(bk[0:64, 0:BLK], lhsT=R(ONESR), rhs=R(src), start=True, stop=True), [srckey, "cstr"], [bkk])
        return bk, bkk

    def chunk_mm(st, lhs, lhsk, rhs, rhsk):
        bk, bkk = st.nb()
        for n in range(NCH):
            pe(lambda e, n=n, bk=bk: e.matmul(bk[0:64, n * 64:(n + 1) * 64], lhsT=R(lhs[:, n * 64:(n + 1) * 64]), rhs=R(rhs[:, n * 64:(n + 1) * 64]), start=True, stop=True),
               [lhsk, rhsk], [bkk])
        return bk, bkk

    def neumann(st, Z0, Y0, Wt, tmpZ, tmpY):
        dve(lambda e: e.tensor_tensor(out=R(Wt[:]), in0=Z0[:], in1=C(C_ID), op=ALU.add), [Z0.k, cst.k], [Wt.k])
        Zc, Yc, Zn, Yn = Z0, Y0, tmpZ, tmpY
        for k in range(5):
            by, byk = chunk_mm(st, Zc, Zc.k, Yc, Yc.k)
            act(lambda e, by=by, Yn=Yn: e.copy(out=R(Yn[:]), in_=by[0:64, 0:BLK]), [byk], [Yn.k])
            if k < 4:
                bz, bzk = chunk_mm(st, Yc, Yc.k, Zc, Zc.k)
                dve(lambda e, bz=bz, Zn=Zn: e.tensor_copy(out=R(Zn[:]), in_=bz[0:64, 0:BLK]), [bzk], [Zn.k])
            cut()
            bw, bwk = chunk_mm(st, Yn, Yn.k, Wt, Wt.k)
            dve(lambda e, bw=bw: e.tensor_tensor(out=R(Wt[:]), in0=bw[0:64, 0:BLK], in1=Wt[:], op=ALU.add), [bwk, Wt.k], [Wt.k])
            cut()
            Zc, Yc, Zn, Yn = Zn, Yn, Zc, Yc

    def l2n(st, x, outt, scale, sq, rn):
        act(lambda e: e.activation(out=R(sq[:]), in_=x[:], func=AF.Square), [x.k], [sq.k])
        bk, bkk = ones_mm(st, sq[:], sq.k)
        act(lambda e: e.activation(out=rn[:], in_=bk[0:64, 0:BLK], func=AF.Ln, bias=1e-6), [bkk], [rn.k])
        act(lambda e: e.activation(out=rn[:], in_=rn[:], func=AF.Exp, scale=-0.5), [rn.k], [rn.k])
        dve(lambda e: e.scalar_tensor_tensor(out=R(outt[:]), in0=x[:], scalar=scale, in1=rn[:], op0=ALU.mult, op1=ALU.mult), [x.k, rn.k], [outt.k])
        cut()

    loads, stores, all_streams = [], [], []
    for b in range(NB):
        s = b % 2
        hb = hblk[s]
        loads.append(lambda b=b, s=s, hb=hb: p.dma("sp", f"d_h{s}", hb[:], A["hT_block"](b), reads=[A["hT_key"]], writes=[hb.k]))
        yo = yout[s]

        def proj(st, g):
            bk, bkk = st.nb()
            for kc in range(8):
                pe(lambda e, kc=kc, bk=bk: e.matmul(bk[0:64, 0:BLK], lhsT=wbf[:, kc, g * 64:(g + 1) * 64], rhs=hb[:, kc, :], start=(kc == 0), stop=(kc == 7)),
                   [hb.k, WBK[kc]], [bkk])
            return bk, bkk

        def proj2(st, g):
            bk, bkk = st.nb()
            for kc in range(8):
                pe(lambda e, kc=kc, bk=bk: e.matmul(bk[0:128, 0:BLK], lhsT=wbf[:, kc, g * 64:(g + 2) * 64], rhs=hb[:, kc, :], start=(kc == 0), stop=(kc == 7)),
                   [hb.k, WBK[kc]], [bkk])
            return bk, bkk

        def raw_evac(g, bk, bkk, half):
            r_ = raw[g][s]; ro = raw[g][1 - s]
            act(lambda e: e.copy(out=r_[:, 3:BLK + 3], in_=bk[half * 64:(half + 1) * 64, 0:BLK]), [bkk], [r_.k])
            pool(lambda e: e.tensor_copy(out=r_[:, 0:3], in_=ro[:, BLK:BLK + 3]), [ro.k], [r_.k])

        def gdn():
            W, st = WG, SG
            bk, bkk = proj2(st, GQ)
            raw_evac(GQ, bk, bkk, 0); raw_evac(GK, bk, bkk, 1)
            cut()
            bk, bkk = proj2(st, GV)
            raw_evac(GV, bk, bkk, 0)
            sg = W["p11"]
            act(lambda e, bk=bk: e.activation(out=sg[:], in_=bk[64:128, 0:BLK], func=AF.Silu), [bkk], [sg.k])
            cut()
            def conv(g, outt, ci):
                r_ = raw[g][s]
                dve(lambda e: e.tensor_scalar(out=outt[:], in0=r_[:, 3:BLK + 3], scalar1=PP(PC_CONV + ci * 4 + 3), scalar2=None, op0=ALU.mult), [r_.k] + DPK, [outt.k])
                for j in (2, 1, 0):
                    dve(lambda e, j=j: e.scalar_tensor_tensor(out=outt[:], in0=r_[:, j:j + BLK], scalar=PP(PC_CONV + ci * 4 + j), in1=outt[:], op0=ALU.mult, op1=ALU.add), [r_.k, outt.k], [outt.k])
                act(lambda e: e.activation(out=outt[:], in_=outt[:], func=AF.Silu), [outt.k], [outt.k])
                cut()
            qs, ks, vs = W["p0"], W["p1"], W["p2"]
            conv(GQ, qs, 0); conv(GK, ks, 1); conv(GV, vs, 2)
            qn, kn = W["r0"], W["r1"]
            l2n(st, qs, qn, 0.125, W["r2"], W["p3"]); l2n(st, ks, kn, 1.0, W["r2"], W["p3"])
            bk, bkk = proj(st, GSC)
            act(lambda e: e.activation(out=srow[0:1, :], in_=bk[0:1, 0:BLK], func=AF.Sigmoid), [bkk], [srow.k])
            act(lambda e: e.activation(out=srow[32:33, :], in_=bk[32:33, 0:BLK], func=AF.Exp, bias=pp[32:33, PC_DTB:PC_DTB + 1]), [bkk] + DPK, [srow.k])
            act(lambda e: e.activation(out=srow[32:33, :], in_=srow[32:33, :], func=AF.Ln, bias=1.0), [srow.k], [srow.k])
            dve(lambda e: e.tensor_scalar(out=srow[32:33, :], in0=srow[32:33, :], scalar1=dp[32:33, D_NA:D_NA + 1], scalar2=None, op0=ALU.mult), [srow.k] + DPK, [srow.k])
            bk, bkk = transp8(st, srow, srow.k, rows=33)
            beta, gcol = CL(0), CL(1)
            dve(lambda e: e.tensor_copy(out=beta, in_=v3(bk[0:64, 0:BLK])[:, :, 0]), [bkk], ["c_beta"])
            dve(lambda e: e.tensor_copy(out=gcol, in_=v3(bk[0:64, 0:BLK])[:, :, 32]), [bkk], ["c_g"])
            bk, bkk = st.nb()
            pe(lambda e: e.matmul(bk[0:64, 0:NCH], lhsT=TRI, rhs=gcol, start=True, stop=True), ["c_g", cst.k], [bkk])
            pe(lambda e: e.matmul(bk[0:64, NCH:2 * NCH], lhsT=ONES, rhs=gcol, start=True, stop=True), ["c_g", cst.k], [bkk])
            gc, gcl, eg, beg, edl, last, lnb, gcb = CL(2), CL(3), CL(4), CL(5), CL(6), CL(7), CL(8), CL(9)
            dve(lambda e: e.tensor_copy(out=colsG[:, 2:4, 0:NCH], in_=bk[0:64, 0:2 * NCH].rearrange("p (a n) -> p a n", a=2)), [bkk], ["c_gc"])
            act(lambda e: e.activation(out=eg, in_=gc, func=AF.Exp), ["c_gc"], ["c_eg"])
            act(lambda e: e.activation(out=last, in_=gcl, func=AF.Exp), ["c_gc"], ["c_last"])
            act(lambda e: e.activation(out=lnb, in_=beta, func=AF.Ln), ["c_beta"], ["c_lnb"])
            dve(lambda e: e.tensor_tensor(out=beg, in0=beta, in1=eg, op=ALU.mult), ["c_beta", "c_eg"], ["c_beg"])
            dve(lambda e: e.tensor_tensor(out=edl, in0=gcl, in1=gc, op=ALU.subtract), ["c_gc"], ["c_edl0"])
            act(lambda e: e.activation(out=edl, in_=edl, func=AF.Exp), ["c_edl0"], ["c_edl"])
            dve(lambda e: e.tensor_tensor(out=gcb, in0=gc, in1=lnb, op=ALU.add), ["c_gc", "c_lnb"], ["c_gcb"])
            cut()
            def rowb(colap, colk, maskc, outt, subcol, subk):
                D = W["p4"]
                dve(lambda e: e.tensor_tensor(out=v3(D[:]), in0=v3(C(C_ID)), in1=bc(colap), op=ALU.mult), [colk, cst.k], [D.k])
                bk, bkk = st.nb()
                pe(lambda e: e.matmul(bk[0:64, 0:BLK], lhsT=ONES, rhs=D[:], start=True, stop=(maskc is None)), [D.k, cst.k], [bkk])
                if maskc is not None:
                    pe(lambda e: e.matmul(bk[0:64, 0:BLK], lhsT=IDN, rhs=C(maskc), start=False, stop=True), [cst.k], [bkk])
                if subcol is not None:
                    dve(lambda e: e.tensor_tensor(out=v3(outt[:]), in0=v3(bk[0:64, 0:BLK]), in1=bc(subcol), op=ALU.subtract), [bkk, subk], [outt.k])
                    act(lambda e: e.activation(out=outt[:], in_=outt[:], func=AF.Exp), [outt.k], [outt.k])
                else:
                    act(lambda e: e.activation(out=outt[:], in_=bk[0:64, 0:BLK], func=AF.Exp), [bkk], [outt.k])
                cut()
            decT, AdT, egrow = W["p5"], W["p6"], W["p7"]
            rowb(gc, "c_gc", C_MNI, decT, gc, "c_gc")
            rowb(gcb, "c_gcb", C_MNS, AdT, gc, "c_gc")
            rowb(gc, "c_gc", None, egrow, None, None)
            qdT = W["r3"]
            dve(lambda e: e.tensor_tensor(out=R(qdT[:]), in0=qn[:], in1=egrow[:], op=ALU.mult), [qn.k, egrow.k], [qdT.k])
            Z0, QKm, Y0, Wt = W["r4"], W["r5"], W["r6"], W["r7"]
            bk, bkk = chunk_mm(st, kn, kn.k, kn, kn.k)
            dve(lambda e, bk=bk: e.scalar_tensor_tensor(out=R(Z0[:]), in0=bk[0:64, 0:BLK], scalar=-1.0, in1=AdT[:], op0=ALU.mult, op1=ALU.mult), [bkk, AdT.k], [Z0.k])
            cut()
            bk, bkk = chunk_mm(st, kn, kn.k, qn, qn.k)
            dve(lambda e, bk=bk: e.tensor_tensor(out=R(QKm[:]), in0=bk[0:64, 0:BLK], in1=decT[:], op=ALU.mult), [bkk, decT.k], [QKm.k])
            cut()
            bk, bkk = transp8(st, Z0, Z0.k)
            act(lambda e, bk=bk: e.copy(out=R(Y0[:]), in_=bk[0:64, 0:BLK]), [bkk], [Y0.k])
            cut()
            neumann(st, Z0, Y0, Wt, W["r8"], W["r9"])
            kbe, kdec, vb = W["r10"], W["r11"], W["r12"]
            bk, bkk = transp8(st, kn, kn.k)
            dve(lambda e, bk=bk: e.tensor_tensor(out=v3(R(kbe[:])), in0=v3(bk[0:64, 0:BLK]), in1=bc(beg), op=ALU.mult), [bkk, "c_beg"], [kbe.k])
            dve(lambda e, bk=bk: e.tensor_tensor(out=v3(R(kdec[:])), in0=v3(bk[0:64, 0:BLK]), in1=bc(edl), op=ALU.mult), [bkk, "c_edl"], [kdec.k])
            cut()
            bk, bkk = transp8(st, vs, vs.k)
            dve(lambda e, bk=bk: e.tensor_tensor(out=v3(R(vb[:])), in0=v3(bk[0:64, 0:BLK]), in1=bc(beta), op=ALU.mult), [bkk, "c_beta"], [vb.k])
            cut()
            u, wT = W["p8"], W["r13"]
            bk, bkk = chunk_mm(st, Wt, Wt.k, vb, vb.k)
            act(lambda e, bk=bk: e.copy(out=u[:], in_=bk[0:64, 0:BLK]), [bkk], [u.k])
            cut()
            bk, bkk = chunk_mm(st, kbe, kbe.k, Wt, Wt.k)
            act(lambda e, bk=bk: e.copy(out=R(wT[:]), in_=bk[0:64, 0:BLK]), [bkk], [wT.k])
            cut()
            ob, obk = st.ob, st.obk
            vnew = W["r14"]
            for n in range(NCH):
                cs = slice(n * 64, (n + 1) * 64)
                ci = b * NCH + n
                So, Sn = S["g"][ci % 2], S["g"][(ci + 1) % 2]
                ta, tak = st.nb()
                pe(lambda e, cs=cs, So=So, ta=ta: e.matmul(ta[0:64, 0:64], lhsT=R(wT[:, cs]), rhs=R(So[:]), start=True, stop=True), [wT.k, So.k], [tak])
                dve(lambda e, cs=cs, ta=ta: e.tensor_tensor(out=R(vnew[:, cs]), in0=u[:, cs], in1=ta[0:64, 0:64], op=ALU.subtract), [u.k, tak], [vnew.k + str(n)])
                pe(lambda e, cs=cs, So=So: e.matmul(ob[0:64, cs], lhsT=R(qdT[:, cs]), rhs=R(So[:]), start=True, stop=False), [qdT.k, So.k], [obk])
                pe(lambda e, cs=cs: e.matmul(ob[0:64, cs], lhsT=R(QKm[:, cs]), rhs=R(vnew[:, cs]), start=False, stop=True), [QKm.k, vnew.k + str(n)], [obk])
                tb, tbk = st.nb()
                pe(lambda e, cs=cs, tb=tb: e.matmul(tb[0:64, 0:64], lhsT=R(kdec[:, cs]), rhs=R(vnew[:, cs]), start=True, stop=True), [kdec.k, vnew.k + str(n)], [tbk])
                dve(lambda e, n=n, So=So, Sn=Sn, tb=tb: e.scalar_tensor_tensor(out=R(Sn[:]), in0=So[:], scalar=last[:, n:n + 1], in1=tb[0:64, 0:64], op0=ALU.mult, op1=ALU.add), [So.k, "c_last", tbk], [Sn.k])
                cut()
            og, sq, c1 = W["p9"], W["p10"], CL(10)
            act(lambda e: e.copy(out=og[:], in_=ob[0:64, 0:BLK]), [obk], [og.k])
            dve(lambda e: e.tensor_tensor(out=sq[:], in0=og[:], in1=og[:], op=ALU.mult), [og.k], [sq.k])
            dve(lambda e: e.tensor_reduce(out=c1, in_=v3(sq[:]), axis=AX.X, op=ALU.add), [sq.k], ["c_c1"])
            act(lambda e: e.activation(out=c1, in_=c1, func=AF.Ln, scale=1.0 / 64, bias=1e-6), ["c_c1"], ["c_c1"])
            act(lambda e: e.activation(out=c1, in_=c1, func=AF.Exp, scale=-0.5), ["c_c1"], ["c_c1"])
            dve(lambda e: e.tensor_tensor(out=v3(og[:]), in0=v3(og[:]), in1=bc(c1), op=ALU.mult), [og.k, "c_c1"], [og.k])
            cut()
            bk, bkk = transp8(st, og, og.k)
            dve(lambda e, bk=bk: e.scalar_tensor_tensor(out=yo[:, 0, :], in0=bk[0:64, 0:BLK], scalar=PP(PC_GNW), in1=sg[:], op0=ALU.mult, op1=ALU.mult), [bkk, sg.k] + DPK, [yo.k + "0"])
            cut()

        def hgrn():
            W, st = WH, SH
            hq, hsg, hv, hgt = W["p0"], W["p1"], W["p2"], W["p3"]
            bk, bkk = proj2(st, HQ)
            act(lambda e, bk=bk: e.activation(out=hq[:], in_=bk[0:64, 0:BLK], func=AF.Silu), [bkk], [hq.k])
            act(lambda e, bk=bk: e.activation(out=hsg[:], in_=bk[64:128, 0:BLK], func=AF.Sigmoid), [bkk], [hsg.k])
            cut()
            bk, bkk = proj2(st, HI)
            act(lambda e, bk=bk: e.copy(out=hv[:], in_=bk[0:64, 0:BLK]), [bkk], [hv.k])
            act(lambda e, bk=bk: e.activation(out=hgt[:], in_=bk[64:128, 0:BLK], func=AF.Silu), [bkk], [hgt.k])
            cut()
            hk, lg, bT = W["p4"], W["p5"], W["p6"]
            dve(lambda e: e.tensor_scalar(out=hk[:], in0=hsg[:], scalar1=DP(D_NOML), scalar2=DP(D_OML), op0=ALU.mult, op1=ALU.add), [hsg.k] + DPK, [hk.k])
            dve(lambda e: e.tensor_scalar(out=lg[:], in0=hsg[:], scalar1=DP(D_OML), scalar2=DP(D_LB), op0=ALU.mult, op1=ALU.add), [hsg.k] + DPK, [lg.k])
            act(lambda e: e.activation(out=lg[:], in_=lg[:], func=AF.Ln), [lg.k], [lg.k])
            dve(lambda e: e.tensor_tensor_scan(out=bT[:], data0=C(C_RST), data1=lg[:], initial=0.0, op0=ALU.mult, op1=ALU.add), [lg.k, cst.k], [bT.k])
            cut()
            bmid, blast, ebl = CL(11), CL(12), CL(13)
            dve(lambda e: e.tensor_copy(out=bmid, in_=v3(bT[:])[:, :, 31]), [bT.k], ["c_bmid"])
            dve(lambda e: e.tensor_copy(out=blast, in_=v3(bT[:])[:, :, 63]), [bT.k], ["c_blast"])
            act(lambda e: e.activation(out=ebl, in_=blast, func=AF.Exp), ["c_blast"], ["c_ebl"])
            qtT, ktT, qbT, kpT = W["r0"], W["r1"], W["r2"], W["p7"]
            tA, tB, tC, t1, t2 = W["p8"], W["p9"], W["p10"], W["p11"], W["p12"]
            dve(lambda e: e.tensor_tensor(out=v3(t1[:]), in0=v3(bT[:]), in1=bc(bmid), op=ALU.subtract), [bT.k, "c_bmid"], [t1.k])
            dve(lambda e: e.tensor_tensor(out=v3(t2[:]), in0=v3(bT[:]), in1=bc(blast), op=ALU.subtract), [bT.k, "c_blast"], [t2.k])
            act(lambda e: e.activation(out=tA[:], in_=t1[:], func=AF.Exp), [t1.k], [tA.k])
            act(lambda e: e.activation(out=tB[:], in_=t1[:], func=AF.Exp, scale=-1.0), [t1.k], [tB.k])
            act(lambda e: e.activation(out=tC[:], in_=bT[:], func=AF.Exp), [bT.k], [tC.k])
            act(lambda e: e.activation(out=kpT[:], in_=t2[:], func=AF.Exp, scale=-1.0), [t2.k], [kpT.k])
            cut()
            dve(lambda e: e.tensor_tensor(out=R(qtT[:]), in0=tA[:], in1=hq[:], op=ALU.mult), [tA.k, hq.k], [qtT.k])
            dve(lambda e: e.tensor_tensor(out=R(ktT[:]), in0=tB[:], in1=hk[:], op=ALU.mult), [tB.k, hk.k], [ktT.k])
            dve(lambda e: e.tensor_tensor(out=R(qbT[:]), in0=tC[:], in1=hq[:], op=ALU.mult), [tC.k, hq.k], [qbT.k])
            dve(lambda e: e.tensor_tensor(out=kpT[:], in0=kpT[:], in1=hk[:], op=ALU.mult), [kpT.k, hk.k], [kpT.k])
            cut()
            attm, kpTM, hvTM = W["r3"], W["r4"], W["r5"]
            bk, bkk = chunk_mm(st, ktT, ktT.k, qtT, qtT.k)
            dve(lambda e, bk=bk: e.tensor_tensor(out=R(attm[:]), in0=bk[0:64, 0:BLK], in1=C(C_M01I), op=ALU.mult), [bkk, cst.k], [attm.k])
            cut()
            bk, bkk = transp8(st, kpT, kpT.k)
            act(lambda e, bk=bk: e.copy(out=R(kpTM[:]), in_=bk[0:64, 0:BLK]), [bkk], [kpTM.k])
            cut()
            bk, bkk = transp8(st, hv, hv.k)
            act(lambda e, bk=bk: e.copy(out=R(hvTM[:]), in_=bk[0:64, 0:BLK]), [bkk], [hvTM.k])
            cut()
            ob, obk = st.ob, st.obk
            for n in range(NCH):
                cs = slice(n * 64, (n + 1) * 64)
                ci = b * NCH + n
                So, Sn = S["h"][ci % 2], S["h"][(ci + 1) % 2]
                pe(lambda e, cs=cs: e.matmul(ob[0:64, cs], lhsT=R(attm[:, cs]), rhs=R(hvTM[:, cs]), start=True, stop=False), [attm.k, hvTM.k], [obk])
                pe(lambda e, cs=cs, So=So: e.matmul(ob[0:64, cs], lhsT=R(qbT[:, cs]), rhs=R(So[:]), start=False, stop=True), [qbT.k, So.k], [obk])
                tc_, tck = st.nb()
                pe(lambda e, cs=cs, tc_=tc_: e.matmul(tc_[0:64, 0:64], lhsT=R(kpTM[:, cs]), rhs=R(hvTM[:, cs]), start=True, stop=True), [kpTM.k, hvTM.k], [tck])
                dve(lambda e, n=n, So=So, Sn=Sn, tc_=tc_: e.scalar_tensor_tensor(out=R(Sn[:]), in0=So[:], scalar=ebl[:, n:n + 1], in1=tc_[0:64, 0:64], op0=ALU.mult, op1=ALU.add), [So.k, "c_ebl", tck], [Sn.k])
                cut()
            oh, sqh = W["p8"], W["p13"]
            act(lambda e: e.copy(out=oh[:], in_=ob[0:64, 0:BLK]), [obk], [oh.k])
            dve(lambda e: e.tensor_tensor(out=sqh[:], in0=oh[:], in1=oh[:], op=ALU.mult), [oh.k], [sqh.k])
            dve(lambda e: e.tensor_reduce(out=ssall[:, b * NCH:(b + 1) * NCH], in_=v3(sqh[:]), axis=AX.X, op=ALU.add), [sqh.k], [ssall.k])
            cut()
            bk, bkk = transp8(st, oh, oh.k)
            dve(lambda e, bk=bk: e.scalar_tensor_tensor(out=yo[:, 2, :], in0=bk[0:64, 0:BLK], scalar=PP(PC_HNW), in1=hgt[:], op0=ALU.mult, op1=ALU.mult), [bkk, hgt.k] + DPK, [yo.k + "2"])
            cut()

        def rwkv():
            W, st = WR, SR
            for g0 in (RR, RV, RWD):
                bk, bkk = proj2(st, g0)
                raw_evac(g0, bk, bkk, 0); raw_evac(g0 + 1, bk, bkk, 1)
                cut()
            def shift(g, outt, mi):
                r_ = raw[g][s]
                dve(lambda e: e.tensor_scalar(out=outt[:], in0=r_[:, 3:BLK + 3], scalar1=DP(D_OMM + mi), scalar2=None, op0=ALU.mult), [r_.k] + DPK, [outt.k])
                dve(lambda e: e.scalar_tensor_tensor(out=outt[:], in0=r_[:, 2:BLK + 2], scalar=PP(PC_MU + mi), in1=outt[:], op0=ALU.mult, op1=ALU.add), [r_.k, outt.k], [outt.k])
                cut()
            rr_, rk_, rv_, rwd_, rad_, rg_ = W["p0"], W["p1"], W["p2"], W["p3"], W["p4"], W["p5"]
            shift(RR, rr_, 0); shift(RK, rk_, 1); shift(RV, rv_, 2); shift(RWD, rwd_, 3); shift(RAD, rad_, 4); shift(RG, rg_, 5)
            act(lambda e: e.activation(out=rg_[:], in_=rg_[:], func=AF.Silu), [rg_.k], [rg_.k])
            act(lambda e: e.activation(out=rwd_[:], in_=rwd_[:], func=AF.Tanh), [rwd_.k], [rwd_.k])
            lw, aa = W["p6"], W["p7"]
            bk, bkk = st.nb()
            pe(lambda e, bk=bk: e.matmul(bk[0:64, 0:BLK], lhsT=lora[:, 0:64], rhs=rwd_[:], start=True, stop=True), [lora.k, rwd_.k], [bkk])
            act(lambda e, bk=bk: e.activation(out=lw[:], in_=bk[0:64, 0:BLK], func=AF.Exp, scale=-1.0, bias=DP(D_NW0)), [bkk] + DPK, [lw.k])
            dve(lambda e: e.tensor_scalar(out=lw[:], in0=lw[:], scalar1=1.0, scalar2=None, op0=ALU.add), [lw.k], [lw.k])
            dve(lambda e: e.reciprocal(out=lw[:], in_=lw[:]), [lw.k], [lw.k])
            dve(lambda e: e.tensor_scalar(out=lw[:], in0=lw[:], scalar1=-0.6065306597126334, scalar2=None, op0=ALU.mult), [lw.k], [lw.k])
            cut()
            bk, bkk = st.nb()
            pe(lambda e, bk=bk: e.matmul(bk[0:64, 0:BLK], lhsT=lora[:, 64:128], rhs=rad_[:], start=True, stop=True), [lora.k, rad_.k], [bkk])
            act(lambda e, bk=bk: e.activation(out=aa[:], in_=bk[0:64, 0:BLK], func=AF.Sigmoid, bias=PP(PC_A0)), [bkk] + DPK, [aa.k])
            cut()
            kk, sqk, kkr = W["r0"], W["r1"], W["p8"]
            dve(lambda e: e.tensor_scalar(out=kkr[:], in0=rk_[:], scalar1=PP(PC_KK), scalar2=None, op0=ALU.mult), [rk_.k] + DPK, [kkr.k])
            l2n(st, kkr, kk, 1.0, sqk, W["p9"])
            k2 = W["p10"]
            dve(lambda e: e.tensor_scalar(out=k2[:], in0=aa[:], scalar1=PP(PC_KA), scalar2=DP(D_OMKA), op0=ALU.mult, op1=ALU.add), [aa.k] + DPK, [k2.k])
            dve(lambda e: e.tensor_tensor(out=k2[:], in0=k2[:], in1=rk_[:], op=ALU.mult), [k2.k, rk_.k], [k2.k])
            bon, bonF = W["r2"], W["p11"]
            dve(lambda e: e.scalar_tensor_tensor(out=R(bon[:]), in0=rr_[:], scalar=PP(PC_RK), in1=k2[:], op0=ALU.mult, op1=ALU.mult), [rr_.k, k2.k] + DPK, [bon.k])
            bk, bkk = ones_mm(st, bon[:], bon.k)
            dve(lambda e, bk=bk: e.tensor_tensor(out=bonF[:], in0=bk[0:64, 0:BLK], in1=rv_[:], op=ALU.mult), [bkk, rv_.k], [bonF.k])
            cut()
            cT, cxT = W["p12"], W["p13"]
            dve(lambda e: e.tensor_tensor_scan(out=cT[:], data0=C(C_RST), data1=lw[:], initial=0.0, op0=ALU.mult, op1=ALU.add), [lw.k, cst.k], [cT.k])
            dve(lambda e: e.tensor_tensor(out=cxT[:], in0=cT[:], in1=lw[:], op=ALU.subtract), [cT.k, lw.k], [cxT.k])
            clast, ecl = CL(15), CL(16)
            dve(lambda e: e.tensor_copy(out=clast, in_=v3(cT[:])[:, :, 63]), [cT.k], ["c_clast"])
            act(lambda e: e.activation(out=ecl, in_=clast, func=AF.Exp), ["c_clast"], ["c_ecl"])
            atT, rtT, btT, ktT2, bhT, khT = W["r3"], W["r4"], W["r5"], W["r6"], W["p14"], W["p15"]
            ec, enc, ecx, ecc = W["p16"], W["p17"], W["p18"], W["p19"]
            act(lambda e: e.activation(out=ec[:], in_=cT[:], func=AF.Exp), [cT.k], [ec.k])
            act(lambda e: e.activation(out=enc[:], in_=cT[:], func=AF.Exp, scale=-1.0), [cT.k], [enc.k])
            act(lambda e: e.activation(out=ecx[:], in_=cxT[:], func=AF.Exp), [cxT.k], [ecx.k])
            dve(lambda e: e.tensor_tensor(out=v3(ecc[:]), in0=v3(cT[:]), in1=bc(clast), op=ALU.subtract), [cT.k, "c_clast"], [ecc.k])
            act(lambda e: e.activation(out=ecc[:], in_=ecc[:], func=AF.Exp, scale=-1.0), [ecc.k], [ecc.k])
            cut()
            ka = W["p20"]
            dve(lambda e: e.tensor_tensor(out=ka[:], in0=kk[:], in1=aa[:], op=ALU.mult), [kk.k, aa.k], [ka.k])
            dve(lambda e: e.scalar_tensor_tensor(out=R(atT[:]), in0=kk[:], scalar=-1.0, in1=ecx[:], op0=ALU.mult, op1=ALU.mult), [kk.k, ecx.k], [atT.k])
            dve(lambda e: e.tensor_tensor(out=R(rtT[:]), in0=rr_[:], in1=ec[:], op=ALU.mult), [rr_.k, ec.k], [rtT.k])
            dve(lambda e: e.tensor_tensor(out=R(btT[:]), in0=ka[:], in1=enc[:], op=ALU.mult), [ka.k, enc.k], [btT.k])
            cut()
            dve(lambda e: e.tensor_tensor(out=R(ktT2[:]), in0=k2[:], in1=enc[:], op=ALU.mult), [k2.k, enc.k], [ktT2.k])
            dve(lambda e: e.tensor_tensor(out=bhT[:], in0=ka[:], in1=ecc[:], op=ALU.mult), [ka.k, ecc.k], [bhT.k])
            dve(lambda e: e.tensor_tensor(out=khT[:], in0=k2[:], in1=ecc[:], op=ALU.mult), [k2.k, ecc.k], [khT.k])
            cut()
            Z0r, AakT, ArbT, ArkT, Y0r, Wr = W["r7"], W["r8"], W["r9"], W["r10"], W["r11"], W["r12"]
            def mmask(l, r, maskc, outt):
                bk, bkk = chunk_mm(st, l, l.k, r, r.k)
                dve(lambda e, bk=bk: e.tensor_tensor(out=R(outt[:]), in0=bk[0:64, 0:BLK], in1=C(maskc), op=ALU.mult), [bkk, cst.k], [outt.k])
                cut()
            mmask(btT, atT, C_M01S, Z0r)
            bk, bkk = transp8(st, Z0r, Z0r.k)
            act(lambda e, bk=bk: e.copy(out=R(Y0r[:]), in_=bk[0:64, 0:BLK]), [bkk], [Y0r.k])
            cut()
            mmask(ktT2, atT, C_M01S, AakT)
            mmask(btT, rtT, C_M01I, ArbT)
            mmask(ktT2, rtT, C_M01I, ArkT)
            aTM, bhTM, khTM, vTM = W["r15"], W["r16"], W["r17"], W["r18"]
            for src, dst in ((atT, aTM), (bhT, bhTM), (khT, khTM), (rv_, vTM)):
                bk, bkk = transp8(st, src, src.k)
                act(lambda e, bk=bk, dst=dst: e.copy(out=R(dst[:]), in_=bk[0:64, 0:BLK]), [bkk], [dst.k])
                cut()
            neumann(st, Z0r, Y0r, Wr, W["r13"], W["r14"])
            ApT, X2, UV = W["r19"], W["r20"], W["p21"]
            bk, bkk = chunk_mm(st, aTM, aTM.k, Wr, Wr.k)
            act(lambda e, bk=bk: e.copy(out=R(ApT[:]), in_=bk[0:64, 0:BLK]), [bkk], [ApT.k])
            cut()
            bk, bkk = chunk_mm(st, AakT, AakT.k, vTM, vTM.k)
            act(lambda e, bk=bk: e.copy(out=R(X2[:]), in_=bk[0:64, 0:BLK]), [bkk], [X2.k])
            cut()
            bk, bkk = chunk_mm(st, Wr, Wr.k, X2, X2.k)
            act(lambda e, bk=bk: e.copy(out=UV[:], in_=bk[0:64, 0:BLK]), [bkk], [UV.k])
            cut()
            ob, obk = st.ob, st.obk
            Ut = W["r21"]
            for n in range(NCH):
                cs = slice(n * 64, (n + 1) * 64)
                ci = b * NCH + n
                So, Sn = S["r"][ci % 2], S["r"][(ci + 1) % 2]
                td, tdk = st.nb()
                pe(lambda e, cs=cs, So=So, td=td: e.matmul(td[0:64, 0:64], lhsT=R(ApT[:, cs]), rhs=R(So[:]), start=True, stop=True), [ApT.k, So.k], [tdk])
                dve(lambda e, cs=cs, td=td: e.tensor_tensor(out=R(Ut[:, cs]), in0=UV[:, cs], in1=td[0:64, 0:64], op=ALU.add), [UV.k, tdk], [Ut.k + str(n)])
                pe(lambda e, cs=cs, So=So: e.matmul(ob[0:64, cs], lhsT=R(rtT[:, cs]), rhs=R(So[:]), start=True, stop=False), [rtT.k, So.k], [obk])
                pe(lambda e, cs=cs: e.matmul(ob[0:64, cs], lhsT=R(ArbT[:, cs]), rhs=R(Ut[:, cs]), start=False, stop=False), [ArbT.k, Ut.k + str(n)], [obk])
                pe(lambda e, cs=cs: e.matmul(ob[0:64, cs], lhsT=R(ArkT[:, cs]), rhs=R(vTM[:, cs]), start=False, stop=True), [ArkT.k, vTM.k], [obk])
                te, tek = st.nb()
                pe(lambda e, cs=cs, te=te: e.matmul(te[0:64, 0:64], lhsT=R(bhTM[:, cs]), rhs=R(Ut[:, cs]), start=True, stop=False), [bhTM.k, Ut.k + str(n)], [tek])
                pe(lambda e, cs=cs, te=te: e.matmul(te[0:64, 0:64], lhsT=R(khTM[:, cs]), rhs=R(vTM[:, cs]), start=False, stop=True), [khTM.k, vTM.k], [tek])
                dve(lambda e, n=n, So=So, Sn=Sn, te=te: e.scalar_tensor_tensor(out=R(Sn[:]), in0=So[:], scalar=ecl[:, n:n + 1], in1=te[0:64, 0:64], op0=ALU.mult, op1=ALU.add), [So.k, "c_ecl", tek], [Sn.k])
                cut()
            orr, sqr = W["p16"], W["p17"]
            m1, m2 = CL(17), CL(18)
            act(lambda e: e.copy(out=orr[:], in_=ob[0:64, 0:BLK]), [obk], [orr.k])
            dve(lambda e: e.tensor_reduce(out=m1, in_=v3(orr[:]), axis=AX.X, op=ALU.add), [orr.k], ["c_m1"])
            dve(lambda e: e.tensor_scalar(out=m1, in0=m1, scalar1=1.0 / 64, scalar2=None, op0=ALU.mult), ["c_m1"], ["c_m1"])
            dve(lambda e: e.tensor_tensor(out=v3(orr[:]), in0=v3(orr[:]), in1=bc(m1), op=ALU.subtract), [orr.k, "c_m1"], [orr.k])
            dve(lambda e: e.tensor_tensor(out=sqr[:], in0=orr[:], in1=orr[:], op=ALU.mult), [orr.k], [sqr.k])
            dve(lambda e: e.tensor_reduce(out=m2, in_=v3(sqr[:]), axis=AX.X, op=ALU.add), [sqr.k], ["c_m2"])
            act(lambda e: e.activation(out=m2, in_=m2, func=AF.Ln, scale=1.0 / 64, bias=64e-5), ["c_m2"], ["c_m2"])
            act(lambda e: e.activation(out=m2, in_=m2, func=AF.Exp, scale=-0.5), ["c_m2"], ["c_m2"])
            dve(lambda e: e.tensor_tensor(out=v3(orr[:]), in0=v3(orr[:]), in1=bc(m2), op=ALU.mult), [orr.k, "c_m2"], [orr.k])
            cut()
            bk, bkk = transp8(st, orr, orr.k)
            dve(lambda e, bk=bk: e.tensor_scalar(out=sqr[:], in0=bk[0:64, 0:BLK], scalar1=PP(PC_LNW), scalar2=PP(PC_LNB), op0=ALU.mult, op1=ALU.add), [bkk] + DPK, [sqr.k])
            dve(lambda e: e.tensor_tensor(out=sqr[:], in0=sqr[:], in1=bonF[:], op=ALU.add), [sqr.k, bonF.k], [sqr.k])
            dve(lambda e: e.tensor_tensor(out=yo[:, 1, :], in0=sqr[:], in1=rg_[:], op=ALU.mult), [sqr.k, rg_.k], [yo.k + "1"])
            cut()

        blk_streams = []
        for fn in (gdn, rwkv, hgrn):
            cur[0] = [[]]
            fn()
            blk_streams.append([g for g in cur[0] if g])
            cur[0] = None
        all_streams.append(blk_streams)
        stores.append(lambda b=b, s=s, yo=yo: p.dma("sp", f"d_y{s}", A["y_block"](b), yo[:], reads=[yo.k + "0", yo.k + "1", yo.k + "2"], writes=[A["y_key"]]))

    ns = len(all_streams[0])
    tot = [sum(len(all_streams[b][i]) for b in range(NB)) for i in range(ns)]
    blk = [0] * ns; gi = [0] * ns; done = [0] * ns
    loaded = -1; stored = -1
    def advance(i):
        while blk[i] < NB and gi[i] >= len(all_streams[blk[i]][i]):
            blk[i] += 1; gi[i] = 0
    for i in range(ns):
        advance(i)
    while True:
        active = [i for i in range(ns) if blk[i] < NB]
        mb = min(blk)
        while stored < min(mb, NB) - 1:
            stored += 1
            stores[stored]()
        if not active:
            break
        cand = [i for i in active if blk[i] <= mb + 1]
        i = min(cand, key=lambda j: done[j] / max(1, tot[j]))
        while loaded < blk[i]:
            loaded += 1
            loads[loaded]()
        for th in all_streams[blk[i]][i][gi[i]]:
            th()
        gi[i] += 1; done[i] += 1
        advance(i)

    ncol = T // 64
    sst = Tl("sst", (128, 2, 64))
    evs = []
    for h in range((ncol + 127) // 128):
        w = min(128, ncol - h * 128)
        bk, bkk = banks[0], "bank0"
        pe(lambda e, h=h, w=w, bk=bk: e.transpose(out=bk[0:w, 0:64], in_=ssall[:, h * 128:h * 128 + w], identity=IDN), [ssall.k, cst.k], [bkk])
        act(lambda e, h=h, w=w, bk=bk: e.copy(out=sst[0:w, h, :], in_=bk[0:w, 0:64]), [bkk], [sst.k])
        evs.append(p.dma("sp", "d_ss", A["ssd"][h * 128:h * 128 + w, :], sst[0:w, h, :], reads=[sst.k], writes=[A["ss_key"]]))
    return evs


def build_mixer(nc, T, layer):
    p = Prog(nc)
    hT = nc.dram_tensor("hT", [1024, T], BF16, kind="ExternalInput").ap()
    wd = nc.dram_tensor("w", [1024, NG * 64], F32, kind="ExternalInput").ap()
    ppd = nc.dram_tensor("pp", [64, NPAR], F32, kind="ExternalInput").ap()
    lorad = nc.dram_tensor("lora", [64, 128], F32, kind="ExternalInput").ap()
    cd = nc.dram_tensor("consts", [64, NCONST * BLK], F32, kind="ExternalInput").ap()
    yT = nc.dram_tensor("yT", [192, T], BF16, kind="ExternalOutput").ap()
    ssd = nc.dram_tensor("ss", [T // 64, 64], F32, kind="ExternalOutput").ap()
    A = {"w": wd, "pp": ppd, "lora": lorad, "consts": cd, "ssd": ssd,
         "hT_block": lambda b: hT.rearrange("(kc q) t -> q kc t", q=128)[:, :, b * BLK:(b + 1) * BLK],
         "y_block": lambda b: yT.rearrange("(m c) t -> c m t", m=3)[:, :, b * BLK:(b + 1) * BLK],
         "hT_key": "hT_ext", "y_key": "y_ext", "ss_key": "ss_ext"}
    evs = mixer_body(p, nc, T, layer, A, "")
    for ev in evs:
        p.finish_wait("sp", ev)
    for s in range(2):
        if f"d_y{s}" in p.dma_count:
            p.finish_wait("sp", (f"d_y{s}", p.dma_count[f"d_y{s}"]))
    p.emit()
    p.close()
    return nc


def mixer_stage(p, nc, T, layer, hT_all, wd, ppd, lorad, cd, y_loc, ssd, tag, NT=2048):
    p.push_scope()
    bps = NT // BLK
    A = {"w": wd, "pp": ppd, "lora": lorad, "consts": cd, "ssd": ssd,
         "hT_block": lambda b: hT_all.rearrange("(r kc q) t -> r q kc t", r=8, q=128)[b // bps][:, :, (b % bps) * BLK:(b % bps + 1) * BLK],
         "y_block": lambda b: y_loc[(b // bps) * 192:(b // bps + 1) * 192, (b % bps) * BLK:(b % bps + 1) * BLK].rearrange("(m c) t -> c m t", m=3),
         "hT_key": "hT_all", "y_key": "y_loc", "ss_key": "ss_loc"}
    mixer_body(p, nc, T, layer, A, tag)
    p.pop_scope()


def make_ident_bf():
    import ml_dtypes
    return np.eye(128, dtype=np.float32).astype(ml_dtypes.bfloat16)


def build_on(nc, NT, has_proj, final):
    p = Prog(nc)
    ntile = NT // 128
    xd = nc.dram_tensor("x", [NT, 1024], F32, kind="ExternalInput").ap()
    nwd = nc.dram_tensor("nw", [128, 1024], F32, kind="ExternalInput").ap()
    if has_proj:
        yd = nc.dram_tensor("yT", [1536, NT], BF16, kind="ExternalInput").ap()
        ssd = nc.dram_tensor("ss", [128, ntile * 8], F32, kind="ExternalInput").ap()
        wod = nc.dram_tensor("wo", [1536, 1024], F32, kind="ExternalInput").ap()
    if final:
        outd = nc.dram_tensor("out", [NT, 1024], F32, kind="ExternalOutput").ap()
    else:
        idd = nc.dram_tensor("identb", [128, 128], BF16, kind="ExternalInput").ap()
        hTd = nc.dram_tensor("hT", [1024, NT], BF16, kind="ExternalOutput").ap()
        if has_proj:
            xnd = nc.dram_tensor("xn", [NT, 1024], F32, kind="ExternalOutput").ap()

    class Tl:
        def __init__(s, name, shape, dtype=F32):
            s.t = p.sb("sb_" + name, shape, dtype); s.k = name
        def __getitem__(s, idx):
            return s.t[idx]
    act = lambda fn, r, w: p.op("act", fn, reads=r, writes=w)
    dve = lambda fn, r, w: p.op("dve", fn, reads=r, writes=w)
    pool = lambda fn, r, w: p.op("pool", fn, reads=r, writes=w)
    pe = lambda fn, r, w: p.op("pe", fn, reads=r, writes=w)

    nw = Tl("nw", (128, 1024))
    p.dma("sp", "d_nw", nw[:], nwd[:, :], writes=[nw.k])
    if not final:
        idb = Tl("idb", (128, 128), BF16)
        p.dma("sp", "d_id", idb[:], idd[:, :], writes=[idb.k])
        pst = p.ps("pst", [128, 1024], BF16)
    if has_proj:
        ps1 = p.ps("ps1", [128, 1024], F32)
        ps2 = p.ps("ps2", [128, 1024], F32)
        wob = Tl("wob", (128, 12, 1024), BF16)
        wst = [Tl(f"wst{i}", (128, 1024)) for i in range(2)]
        for kc in range(12):
            s = wst[kc % 2]
            p.dma("sp", f"d_w{kc % 2}", s[:], wod[kc * 128:(kc + 1) * 128, :], writes=[s.k])
            if kc % 2 == 0:
                act(lambda e: e.copy(out=wob[:, kc, :], in_=s[:]), [s.k], [f"wob{kc}"])
            else:
                dve(lambda e: e.tensor_copy(out=wob[:, kc, :], in_=s[:]), [s.k], [f"wob{kc}"])
        ysb = Tl("ysb", (128, 12, NT), BF16)
        for q in range(4):
            qs = slice(q * (NT // 4), (q + 1) * (NT // 4))
            p.dma("sp", f"d_y{q}", ysb[:, :, qs], yd.rearrange("(kc q) t -> q kc t", q=128)[:, :, qs], writes=[f"ysb{q}"])
        sst = Tl("sst", (128, ntile, 8))
        p.dma("sp", "d_ss", sst[:], ssd.rearrange("p (a h) -> p a h", h=8), writes=[sst.k])
        rst = Tl("rst", (128, ntile))
        dve(lambda e: e.tensor_reduce(out=rst[:], in_=sst[:], axis=AX.X, op=ALU.add), [sst.k], [rst.k])
        act(lambda e: e.activation(out=rst[:], in_=rst[:], func=AF.Ln, scale=1.0 / 512, bias=1e-6), [rst.k], [rst.k])
        act(lambda e: e.activation(out=rst[:], in_=rst[:], func=AF.Exp, scale=-0.5), [rst.k], [rst.k])

    xt = [Tl(f"xt{i}", (128, 1024)) for i in range(2)]
    xn = [Tl(f"xn{i}", (128, 1024)) for i in range(2)]
    sq = Tl("sq", (128, 1024))
    hb = [Tl(f"hb{i}", (128, 1024), F32 if final else BF16) for i in range(2)]
    hTs = [Tl(f"hTs{i}", (128, 8, 128), BF16) for i in range(2)]
    cc = Tl("cc", (128, 2 * ntile))
    evs = []
    for t in range(ntile):
        s = t % 2
        ts_ = slice(t * 128, (t + 1) * 128)
        x_ = xt[s]
        p.dma("sp", f"d_x{s}", x_[:], xd[ts_, :], writes=[x_.k])
        if has_proj:
            q = t // (ntile // 4)
            for half in range(2):
                hs = slice(half * 512, (half + 1) * 512)
                for kc in range(8):
                    pe(lambda e: e.matmul(ps1[:, hs], lhsT=ysb[:, kc, ts_], rhs=wob[:, kc, hs], start=(kc == 0), stop=(kc == 7)), [f"ysb{q}", f"wob{kc}"], ["ps1"])
                for kc in range(8, 12):
                    pe(lambda e: e.matmul(ps2[:, hs], lhsT=ysb[:, kc, ts_], rhs=wob[:, kc, hs], start=(kc == 8), stop=(kc == 11)), [f"ysb{q}", f"wob{kc}"], ["ps2"])
            xo = xn[s]
            dve(lambda e: e.tensor_tensor(out=xo[:], in0=ps1[:, :], in1=x_[:], op=ALU.add), ["ps1", x_.k], [xo.k])
            dve(lambda e: e.scalar_tensor_tensor(out=xo[:], in0=ps2[:, :], scalar=rst[:, t:t + 1], in1=xo[:], op0=ALU.mult, op1=ALU.add), ["ps2", rst.k, xo.k], [xo.k])
            if not final:
                evs.append(p.dma("sp", f"d_xo{s}", xnd[ts_, :], xo[:], reads=[xo.k]))
        else:
            xo = x_
        c1 = cc[:, 2 * t:2 * t + 1]
        act(lambda e: e.activation(out=sq[:], in_=xo[:], func=AF.Square, accum_out=c1), [xo.k], [sq.k, f"cc{t}"])
        act(lambda e: e.activation(out=c1, in_=c1, func=AF.Ln, scale=1.0 / 1024, bias=1e-6), [f"cc{t}"], [f"cc{t}"])
        act(lambda e: e.activation(out=c1, in_=c1, func=AF.Exp, scale=-0.5), [f"cc{t}"], [f"cc{t}"])
        h_ = hb[s]
        dve(lambda e: e.scalar_tensor_tensor(out=h_[:], in0=xo[:], scalar=c1, in1=nw[:], op0=ALU.mult, op1=ALU.mult), [xo.k, f"cc{t}", nw.k], [h_.k])
        if final:
            evs.append(p.dma("sp", f"d_o{s}", outd[ts_, :], h_[:], reads=[h_.k]))
        else:
            for kc in range(8):
                pe(lambda e: e.transpose(out=pst[:, kc * 128:(kc + 1) * 128], in_=h_[:, kc * 128:(kc + 1) * 128], identity=idb[:]), [h_.k, idb.k], ["pst"])
            ht = hTs[s]
            act(lambda e: e.copy(out=ht[:].rearrange("p a b -> p (a b)"), in_=pst[:, :]), ["pst"], [ht.k])
            evs.append(p.dma("sp", f"d_h{s}", hTd.rearrange("(kc q) t -> q kc t", q=128)[:, :, ts_], ht[:], reads=[ht.k]))
    for ev in evs[-6:]:
        p.finish_wait("sp", ev)
    p.emit()
    p.close()
    return nc


T_FULL = 16384
NCORE = 8
NT_CORE = T_FULL // NCORE
GDN_PROJ = 512 * 3 + 16 + 512
RWKV_PROJ = 512 * 3 + 128 + 512


def core_inputs(inp, l, h, hT_bf, consts):
    w_in = inp['w_in'][l]
    g0 = 0; r0 = GDN_PROJ; h0 = GDN_PROJ + RWKV_PROJ
    hs = slice(h * 64, (h + 1) * 64)
    def col(base, idx): return w_in[:, base + idx * 512 + h * 64: base + idx * 512 + (h + 1) * 64]
    w = np.zeros((1024, NG * 64), np.float32)
    w[:, GQ*64:(GQ+1)*64] = col(g0, 0); w[:, GK*64:(GK+1)*64] = col(g0, 1); w[:, GV*64:(GV+1)*64] = col(g0, 2)
    w[:, GSC*64 + 0] = w_in[:, g0 + 1536 + h]; w[:, GSC*64 + 32] = w_in[:, g0 + 1536 + 8 + h]
    w[:, GG*64:(GG+1)*64] = w_in[:, g0 + 1552 + h*64: g0 + 1552 + (h+1)*64]
    w[:, RR*64:(RR+1)*64] = col(r0, 0); w[:, RK*64:(RK+1)*64] = col(r0, 1); w[:, RV*64:(RV+1)*64] = col(r0, 2)
    w[:, RWD*64:(RWD+1)*64] = w_in[:, r0 + 1536: r0 + 1600]; w[:, RAD*64:(RAD+1)*64] = w_in[:, r0 + 1600: r0 + 1664]
    w[:, RG*64:(RG+1)*64] = w_in[:, r0 + 1664 + h*64: r0 + 1664 + (h+1)*64]
    for gi, g in enumerate((HQ, HF, HI, HGT)):
        w[:, g*64:(g+1)*64] = col(h0, gi)
    pp = np.zeros((64, NPAR), np.float32)
    cw = inp['gdn_conv_w'][l]
    for ci in range(3):
        for j in range(4):
            pp[:, PC_CONV + ci*4 + j] = cw[j, ci*512 + h*64: ci*512 + (h+1)*64]
    pp[:, PC_GNW] = inp['gdn_norm_w'][l]
    pp[:, PC_ALOG] = inp['gdn_a_log'][l, h]; pp[:, PC_DTB] = inp['gdn_dt_bias'][l, h]
    mu = inp['rwkv_mu'][l]
    pp[:, PC_MU+0] = mu[0+h*64:0+(h+1)*64]; pp[:, PC_MU+1] = mu[512+h*64:512+(h+1)*64]; pp[:, PC_MU+2] = mu[1024+h*64:1024+(h+1)*64]
    pp[:, PC_MU+3] = mu[1536:1600]; pp[:, PC_MU+4] = mu[1600:1664]; pp[:, PC_MU+5] = mu[1664+h*64:1664+(h+1)*64]
    pp[:, PC_W0] = inp['rwkv_w0'][l, hs]; pp[:, PC_A0] = inp['rwkv_a0'][l, hs]; pp[:, PC_KK] = inp['rwkv_k_k'][l, hs]
    pp[:, PC_KA] = inp['rwkv_k_a'][l, hs]; pp[:, PC_RK] = inp['rwkv_r_k'][l, h]; pp[:, PC_LNW] = inp['rwkv_ln_w'][l, hs]
    pp[:, PC_LNB] = inp['rwkv_ln_b'][l, hs]
    pp[:, PC_LB0] = inp['hgrn_lower_bounds'][0, hs]; pp[:, PC_LB1] = inp['hgrn_lower_bounds'][1, hs]
    pp[:, PC_HNW] = inp['hgrn_norm_w'][l, hs]
    lora = np.concatenate([inp['rwkv_w_up'][l][:, hs], inp['rwkv_a_up'][l][:, hs]], axis=1).astype(np.float32)
    return {"hT": hT_bf, "w": w, "pp": pp, "lora": np.ascontiguousarray(lora), "consts": consts}


def _run_mixer(inp, l, hT_full, consts):
    nc = bass.Bass("TRN2", target_bir_lowering=False)
    build_mixer(nc, T_FULL, l)
    maps = [core_inputs(inp, l, h, hT_full, consts) for h in range(NCORE)]
    res = run_bass_kernel_spmd(nc, maps, core_ids=list(range(NCORE))).results
    yT_full = np.empty((1536, T_FULL), dtype=res[0]["yT"].dtype)
    for h in range(NCORE):
        yh = res[h]["yT"]
        for m in range(3):
            yT_full[m * 512 + h * 64:m * 512 + (h + 1) * 64] = yh[m * 64:(m + 1) * 64]
    ss = np.stack([res[h]["ss"].reshape(T_FULL) for h in range(NCORE)], axis=0)
    return yT_full, ss


def _run_on(xs, nwv, has_proj, final, yT_full=None, ss=None, wo=None):
    nc = bass.Bass("TRN2", target_bir_lowering=False)
    build_on(nc, NT_CORE, has_proj, final)
    nw = np.ascontiguousarray(np.broadcast_to(nwv[None, :], (128, 1024))).astype(np.float32)
    idb = make_ident_bf()
    maps = []
    for c in range(NCORE):
        m = {"x": xs[c], "nw": nw}
        if not final:
            m["identb"] = idb
        if has_proj:
            tsl = slice(c * NT_CORE, (c + 1) * NT_CORE)
            m["yT"] = np.ascontiguousarray(yT_full[:, tsl])
            ssl = np.stack([ss[h, tsl].reshape(NT_CORE // 128, 128).T for h in range(NCORE)], axis=-1)
            m["ss"] = np.ascontiguousarray(ssl.reshape(128, -1)).astype(np.float32)
            m["wo"] = wo
        maps.append(m)
    return run_bass_kernel_spmd(nc, maps, core_ids=list(range(NCORE))).results


def kernel(**inputs):
    inp = {k: np.ascontiguousarray(np.asarray(v)) for k, v in inputs.items()}
    x = inp['x'][0]
    xs = [np.ascontiguousarray(x[c * NT_CORE:(c + 1) * NT_CORE]) for c in range(NCORE)]
    consts = make_consts()
    r = _run_on(xs, inp['norm_w'][0], False, False)
    hT_full = np.concatenate([r[c]["hT"] for c in range(NCORE)], axis=1)
    yT_full, ss = _run_mixer(inp, 0, hT_full, consts)
    r = _run_on(xs, inp['norm_w'][1], True, False, yT_full, ss, inp['w_out'][0])
    xs = [r[c]["xn"] for c in range(NCORE)]
    hT_full = np.concatenate([r[c]["hT"] for c in range(NCORE)], axis=1)
    yT_full, ss = _run_mixer(inp, 1, hT_full, consts)
    r = _run_on(xs, inp['final_norm_w'], True, True, yT_full, ss, inp['w_out'][1])
    out = np.concatenate([r[c]["out"] for c in range(NCORE)], axis=0)
    return out[None].astype(np.float32)
```

```python
from concourse.bass_utils import run_bass_kernel_spmd
from contextlib import ExitStack
import numpy as np
import concourse.bass as bass
import concourse.mybir as mybir

F32 = mybir.dt.float32
BF16 = mybir.dt.bfloat16
AF = mybir.ActivationFunctionType
ALU = mybir.AluOpType
AX = mybir.AxisListType

COMPUTE = ("pe", "act", "dve", "pool")


class _Rec:
    def __init__(self):
        self.call = None
    def __getattr__(self, name):
        def f(*a, **kw):
            self.call = (name, a, kw)
            return self
        return f


def _bind(fn):
    r = _Rec()
    fn(r)
    name, a, kw = r.call
    return lambda e: getattr(e, name)(*a, **kw)


class Prog:
    def __init__(self, nc):
        self.nc = nc
        self.es = ExitStack()
        self.lists = {e: [] for e in ("pe", "act", "dve", "pool", "sp")}
        self.count = {e: 0 for e in COMPUTE}
        self.waited = {e: {} for e in self.lists}
        self.lastw = {}
        self.lastr = {}
        self.sems = {}
        self.dma_count = {}
        self.psum_banks = []
        self.psum_rr = 0
        self.final_waits = {}
        self.scopes = [self.es]
        self.handles = {"pe": nc.tensor, "act": nc.scalar, "dve": nc.vector, "pool": nc.gpsimd, "sp": nc.sync}
        for e in COMPUTE:
            self.sem(e)

    def sb(self, name, shape, dtype=F32):
        return self.scopes[-1].enter_context(self.nc.sbuf_tensor(name, list(shape), dtype))

    def ps(self, name, shape, dtype=F32):
        return self.scopes[-1].enter_context(self.nc.psum_tensor(name, list(shape), dtype))

    def push_scope(self):
        self.scopes.append(ExitStack())

    def pop_scope(self):
        self.barrier()
        self.scopes.pop().close()

    def barrier(self):
        cur = {e: c for e, c in self.count.items() if c > 0}
        cur.update(self.dma_count)
        for eng in self.lists:
            ws = []
            wd = self.waited[eng]
            for s_, v in cur.items():
                if wd.get(s_, 0) >= v:
                    continue
                wd[s_] = v
                ws.append((s_, v))
            if ws:
                self._issue(eng, ws, None, None)

    def sem(self, name):
        if name not in self.sems:
            self.sems[name] = self.es.enter_context(self.nc.semaphore(name))
        return self.sems[name]

    def _deps(self, reads, writes):
        deps = {}
        def add(d):
            for s, v in d.items():
                if deps.get(s, 0) < v:
                    deps[s] = v
        for k in reads:
            add(self.lastw.get(k, {}))
        for k in writes:
            add(self.lastw.get(k, {}))
            add(self.lastr.get(k, {}))
        return deps

    def _record(self, reads, writes, sem, val):
        for k in reads:
            d = self.lastr.setdefault(k, {})
            d[sem] = val
        for k in writes:
            self.lastw[k] = {sem: val}
            self.lastr[k] = {}

    def _waits(self, eng, deps):
        ws = []
        wd = self.waited[eng]
        for s, v in deps.items():
            if s == "pe" and eng == "pe":
                continue
            if wd.get(s, 0) >= v:
                continue
            wd[s] = v
            ws.append((s, v))
        return ws

    def op(self, eng, fn, reads=(), writes=()):
        deps = self._deps(reads, writes)
        ws = self._waits(eng, deps)
        self.count[eng] += 1
        val = self.count[eng]
        for k in writes:
            prev = self.lastw.get(k, {})
            prevr = self.lastr.get(k, {})
            self.lastw[k] = dict(prev)
            self.lastw[k][eng] = val
            self.lastr[k] = {}
        for k in reads:
            d = self.lastr.setdefault(k, {})
            d[eng] = val
        self._issue(eng, ws, _bind(fn), (eng, 1))

    def dma(self, queue, semname, out, in_, reads=(), writes=(), **kw):
        deps = self._deps(reads, writes)
        ws = self._waits(queue, deps)
        self.dma_count[semname] = self.dma_count.get(semname, 0) + 16
        val = self.dma_count[semname]
        self.sem(semname)
        for k in writes:
            prev = self.lastw.get(k, {})
            self.lastw[k] = dict(prev)
            self.lastw[k][semname] = val
            self.lastr[k] = {}
        for k in reads:
            d = self.lastr.setdefault(k, {})
            d[semname] = val
        fn = lambda e, out=out, in_=in_, kw=kw: e.dma_start(out=out, in_=in_, **kw)
        self._issue(queue, ws, fn, (semname, 16))
        return (semname, val)

    def dma_fn(self, queue, semname, fn, reads=(), writes=(), inc=16):
        deps = self._deps(reads, writes)
        ws = self._waits(queue, deps)
        self.dma_count[semname] = self.dma_count.get(semname, 0) + inc
        val = self.dma_count[semname]
        self.sem(semname)
        for k in writes:
            prev = self.lastw.get(k, {})
            self.lastw[k] = dict(prev)
            self.lastw[k][semname] = val
            self.lastr[k] = {}
        for k in reads:
            d = self.lastr.setdefault(k, {})
            d[semname] = val
        self._issue(queue, ws, fn, (semname, inc))
        return (semname, val)

    def finish_wait(self, queue, ev):
        self._issue(queue, [ev], None, None)

    def _issue(self, ename, ws, fn, si):
        self.lists[ename].append((ws, fn, si))

    def emit(self):
        attrs = {"pe": "tensor", "act": "scalar", "dve": "vector", "pool": "gpsimd", "sp": "sync"}
        with self.nc.Block() as block:
            for ename, attr in attrs.items():
                lst = self.lists[ename]
                if not lst:
                    continue

                def body(eng, lst=lst):
                    for ws, fn, si in lst:
                        for s_, v in ws:
                            eng.wait_ge(self.sems[s_], v)
                        if fn is None:
                            continue
                        sname, inc = si
                        if inc == 1 and sname.startswith("cc"):
                            fn(eng).then_inc(self.sems[sname])
                        else:
                            fn(eng).then_inc(self.sems[sname], inc)
                getattr(block, attr)(body)

    def close(self):
        self.es.close()


NG = 15
GQ, GK, GV, GG, RR, RK, RV, RG, RWD, RAD, HQ, HF, HI, HGT, GSC = range(15)
PC_CONV = 0; PC_GNW = 12; PC_ALOG = 13; PC_DTB = 14; PC_MU = 15
PC_W0 = 21; PC_A0 = 22; PC_KK = 23; PC_KA = 24; PC_RK = 25; PC_LNW = 26; PC_LNB = 27
PC_LB0 = 28; PC_LB1 = 29; PC_HNW = 30
NPAR = 32
C_ID, C_ONES, C_TRI, C_MNI, C_MNS, C_M01I, C_M01S, C_RST = range(8)
NCONST = 8
NCH = 4
BLK = NCH * 64


def make_consts():
    c = np.zeros((64, NCONST, BLK), np.float32)
    j = np.arange(64)[:, None]; i = np.arange(64)[None, :]
    rep = lambda m: np.tile(m.astype(np.float32), (1, NCH))
    c[:, C_ID] = rep(j == i)
    c[:, C_ONES] = 1.0
    c[:, C_TRI] = rep(j <= i)
    c[:, C_MNI] = rep(np.where(i >= j, 0.0, -30000.0))
    c[:, C_MNS] = rep(np.where(i > j, 0.0, -30000.0))
    c[:, C_M01I] = rep(i >= j)
    c[:, C_M01S] = rep(i > j)
    r = np.ones((64, BLK), np.float32); r[:, ::64] = 0.0
    c[:, C_RST] = r
    return c.reshape(64, NCONST * BLK)


def mixer_body(p, nc, T, layer, A, tag):
    NB = T // BLK
    F32R = mybir.dt.float32r
    R = lambda ap: ap.bitcast(F32R)
    v3 = lambda ap: ap.rearrange("p (n i) -> p n i", n=NCH)
    bc = lambda colap: colap.unsqueeze(2).to_broadcast([64, NCH, 64])

    class Tl:
        def __init__(s, name, shape, dtype=F32):
            s.t = p.sb(tag + "sb_" + name, shape, dtype); s.k = name
        def __getitem__(s, idx):
            return s.t[idx]
    banks = [p.ps(tag + f"bank{i}", [128, 512], F32) for i in range(8)]

    cur = [None]
    def _emit(eng, fn, r, w):
        b_ = _bind(fn)
        r = list(r); w = list(w)
        if cur[0] is None:
            p.op(eng, b_, reads=r, writes=w)
        else:
            cur[0][-1].append(lambda: p.op(eng, b_, reads=r, writes=w))
    act = lambda fn, r, w: _emit("act", fn, r, w)
    dve = lambda fn, r, w: _emit("dve", fn, r, w)
    pool = lambda fn, r, w: _emit("pool", fn, r, w)
    pe = lambda fn, r, w: _emit("pe", fn, r, w)
    def cut():
        if cur[0] is not None and cur[0][-1]:
            cur[0].append([])

    def run_streams(streams):
        tot = [sum(1 for g in s if g) for s in streams]
        streams = [[g for g in s if g] for s in streams]
        idx = [0] * len(streams)
        while True:
            best, bf = None, None
            for i, s in enumerate(streams):
                if idx[i] < len(s):
                    f = idx[i] / max(1, tot[i])
                    if bf is None or f < bf:
                        best, bf = i, f
            if best is None:
                break
            for th in streams[best][idx[best]]:
                th()
            idx[best] += 1

    cst = Tl("cst", (64, NCONST * BLK))
    pp = Tl("pp", (64, NPAR))
    lora = Tl("lora", (64, 128))
    p.dma("sp", "d_c", cst[:], A["consts"][:, :], writes=[cst.k])
    p.dma("sp", "d_p", pp[:], A["pp"][:, :], writes=[pp.k])
    p.dma("sp", "d_l", lora[:], A["lora"][:, :], writes=[lora.k])
    C = lambda c, w=BLK: cst[:, c * BLK:c * BLK + w]
    PP = lambda c: pp[:, c:c + 1]
    IDN = C(C_ID, 64); ONES = C(C_ONES, 64); TRI = C(C_TRI, 64)
    cstr = Tl("cstr", (64, 64))
    act(lambda e: e.copy(out=R(cstr[:, 0:64]), in_=ONES), [cst.k], ["cstr"])
    ONESR = cstr[:, 0:64]

    wbf = Tl("wbf", (128, 8, NG * 64), BF16)
    wst = [Tl(f"wst{i}", (128, NG * 64)) for i in range(2)]
    for kc in range(8):
        s = wst[kc % 2]
        p.dma("sp", f"d_w{kc % 2}", s[:], A["w"][kc * 128:(kc + 1) * 128, :], writes=[s.k])
        if kc % 2 == 0:
            act(lambda e: e.copy(out=wbf[:, kc, :], in_=s[:]), [s.k], [f"wbf{kc}"])
        else:
            dve(lambda e: e.tensor_copy(out=wbf[:, kc, :], in_=s[:]), [s.k], [f"wbf{kc}"])
    WBK = [f"wbf{kc}" for kc in range(8)]

    dp = Tl("dp", (64, 24))
    DP = lambda c: dp[:, c:c + 1]
    D_NA, D_OMM, D_NW0, D_OMKA, D_LB, D_OML, D_NOML, D_T0, D_T1, D_T2, D_T3, D_T4, D_T5 = 0, 1, 7, 8, 9, 10, 11, 12, 13, 14, 15, 16, 17
    act(lambda e: e.activation(out=DP(D_T0), in_=PP(PC_ALOG), func=AF.Exp), [pp.k], ["dp_t0"])
    dve(lambda e: e.tensor_scalar(out=DP(D_NA), in0=DP(D_T0), scalar1=-1.0, scalar2=None, op0=ALU.mult), ["dp_t0"], ["dp"])
    dve(lambda e: e.tensor_scalar(out=dp[:, D_OMM:D_OMM + 6], in0=pp[:, PC_MU:PC_MU + 6], scalar1=-1.0, scalar2=1.0, op0=ALU.mult, op1=ALU.add), [pp.k], ["dp"])
    dve(lambda e: e.tensor_scalar(out=DP(D_NW0), in0=PP(PC_W0), scalar1=-1.0, scalar2=None, op0=ALU.mult), [pp.k], ["dp"])
    dve(lambda e: e.tensor_scalar(out=DP(D_OMKA), in0=PP(PC_KA), scalar1=-1.0, scalar2=1.0, op0=ALU.mult, op1=ALU.add), [pp.k], ["dp"])
    act(lambda e: e.activation(out=dp[:, D_T1:D_T1 + 2], in_=pp[:, PC_LB0:PC_LB0 + 2], func=AF.Exp), [pp.k], ["dp_t1"])
    dve(lambda e: e.tensor_tensor(out=DP(D_T3), in0=DP(D_T1), in1=DP(D_T2), op=ALU.add), ["dp_t1"], ["dp_t0b"])
    dve(lambda e: e.reciprocal(out=DP(D_T4), in_=DP(D_T3)), ["dp_t0b"], ["dp_t0c"])
    if layer == 0:
        dve(lambda e: e.tensor_tensor(out=DP(D_T5), in0=DP(D_T1), in1=DP(D_T1), op=ALU.subtract), ["dp_t1"], ["dp_lb0"])
    else:
        dve(lambda e: e.tensor_tensor(out=DP(D_T5), in0=DP(D_T3), in1=DP(D_T1), op=ALU.subtract), ["dp_t0b", "dp_t1"], ["dp_lb0"])
    dve(lambda e: e.tensor_tensor(out=DP(D_T5), in0=DP(D_T5), in1=DP(D_T4), op=ALU.mult), ["dp_lb0", "dp_t0c"], ["dp_lb1"])
    dve(lambda e: e.tensor_copy(out=DP(D_LB), in_=DP(D_T5)), ["dp_lb1"], ["dp_lb"])
    dve(lambda e: e.tensor_scalar(out=DP(D_OML), in0=DP(D_LB), scalar1=-1.0, scalar2=1.0, op0=ALU.mult, op1=ALU.add), ["dp_lb"], ["dp_oml"])
    dve(lambda e: e.tensor_scalar(out=DP(D_NOML), in0=DP(D_OML), scalar1=-1.0, scalar2=None, op0=ALU.mult), ["dp_oml"], ["dp"])
    DPK = ["dp", "dp_lb", "dp_oml", pp.k, cst.k, lora.k]

    hblk = [Tl(f"hblk{i}", (128, 8, BLK), BF16) for i in range(2)]
    raw = {g: [Tl(f"raw{g}_{s}", (64, BLK + 3)) for s in range(2)] for g in (GQ, GK, GV, RR, RK, RV, RG, RWD, RAD)}
    for g in raw:
        pool(lambda e, g=g: e.memset(raw[g][1][:, BLK:BLK + 3], 0.0), [], [raw[g][1].k])
    S = {m: [Tl(f"S{m}{s}", (64, 64)) for s in range(2)] for m in ("g", "r", "h")}
    for m in S:
        act(lambda e, m=m: e.mul(out=R(S[m][0][:]), in_=ONES, mul=0.0), [cst.k], [S[m][0].k])
    srow = Tl("srow", (64, BLK))
    pool(lambda e: e.memset(srow[:], 0.0), [], [srow.k])
    ssall = Tl("ssall", (64, max(128, T // 64)))
    pool(lambda e: e.memset(ssall[:], 0.0), [], [ssall.k])
    yout = [Tl(f"yout{i}", (64, 3, BLK), BF16) for i in range(2)]
    colsG, colsH, colsR = Tl("colsG", (64, 12, 16)), Tl("colsH", (64, 16, 16)), Tl("colsR", (64, 20, 16))
    CL = lambda i: (colsG if i <= 10 else colsH if i <= 14 else colsR)[:, i, 0:NCH]

    def mk(prefix, nplain, nr):
        d = {f"p{i}": Tl(f"{prefix}_p{i}", (64, BLK)) for i in range(nplain)}
        d.update({f"r{i}": Tl(f"{prefix}_r{i}", (64, BLK)) for i in range(nr)})
        return d
    WG = mk("g", 12, 15)
    WH = mk("h", 14, 6)
    WR = mk("w", 22, 22)

    class St:
        def __init__(s, rot, ob, name):
            s.rot = rot; s.i = 0; s.ob = banks[ob]; s.obk = f"bank{ob}"; s.name = name
        def nb(s):
            i = s.rot[s.i % len(s.rot)]; s.i += 1
            return banks[i], f"bank{i}"
    SG, SR, SH = St([0, 1], 5, "g"), St([2, 3], 6, "r"), St([4], 7, "h")

    def transp8(st, src, srckey, rows=64):
        bk, bkk = st.nb()
        for n in range(NCH):
            pe(lambda e, n=n, bk=bk: e.transpose(out=bk[0:64, n * 64:n * 64 + rows], in_=src[0:rows, n * 64:(n + 1) * 64], identity=IDN[0:rows, 0:rows]),
               [srckey, cst.k], [bkk])
        return bk, bkk

    def ones_mm(st, src, srckey):
        bk, bkk = st.nb()
        pe(lambda e: e.matmul(bk[0:64, 0:BLK], lhsT=R(ONESR), rhs=R(src), start=True, stop=True), [srckey, "cstr"], [bkk])
        return bk, bkk

    def chunk_mm(st, lhs, lhsk, rhs, rhsk):
        bk, bkk = st.nb()
        for n in range(NCH):
            pe(lambda e, n=n, bk=bk: e.matmul(bk[0:64, n * 64:(n + 1) * 64], lhsT=R(lhs[:, n * 64:(n + 1) * 64]), rhs=R(rhs[:, n * 64:(n + 1) * 64]), start=True, stop=True),
               [lhsk, rhsk], [bkk])
        return bk, bkk

    def neumann(st, Z0, Y0, Wt, tmpZ, tmpY):
        dve(lambda e: e.tensor_tensor(out=R(Wt[:]), in0=Z0[:], in1=C(C_ID), op=ALU.add), [Z0.k, cst.k], [Wt.k])
        Zc, Yc, Zn, Yn = Z0, Y0, tmpZ, tmpY
        for k in range(5):
            by, byk = chunk_mm(st, Zc, Zc.k, Yc, Yc.k)
            act(lambda e, by=by, Yn=Yn: e.copy(out=R(Yn[:]), in_=by[0:64, 0:BLK]), [byk], [Yn.k])
            if k < 4:
                bz, bzk = chunk_mm(st, Yc, Yc.k, Zc, Zc.k)
                dve(lambda e, bz=bz, Zn=Zn: e.tensor_copy(out=R(Zn[:]), in_=bz[0:64, 0:BLK]), [bzk], [Zn.k])
            cut()
            bw, bwk = chunk_mm(st, Yn, Yn.k, Wt, Wt.k)
            dve(lambda e, bw=bw: e.tensor_tensor(out=R(Wt[:]), in0=bw[0:64, 0:BLK], in1=Wt[:], op=ALU.add), [bwk, Wt.k], [Wt.k])
            cut()
            Zc, Yc, Zn, Yn = Zn, Yn, Zc, Yc

    def l2n(st, x, outt, scale, sq, rn):
        act(lambda e: e.activation(out=R(sq[:]), in_=x[:], func=AF.Square), [x.k], [sq.k])
        bk, bkk = ones_mm(st, sq[:], sq.k)
        act(lambda e: e.activation(out=rn[:], in_=bk[0:64, 0:BLK], func=AF.Ln, bias=1e-6), [bkk], [rn.k])
        act(lambda e: e.activation(out=rn[:], in_=rn[:], func=AF.Exp, scale=-0.5), [rn.k], [rn.k])
        dve(lambda e: e.scalar_tensor_tensor(out=R(outt[:]), in0=x[:], scalar=scale, in1=rn[:], op0=ALU.mult, op1=ALU.mult), [x.k, rn.k], [outt.k])
        cut()

    loads, stores, all_streams = [], [], []
    for b in range(NB):
        s = b % 2
        hb = hblk[s]
        loads.append(lambda b=b, s=s, hb=hb: p.dma("sp", f"d_h{s}", hb[:], A["hT_block"](b), reads=[A["hT_key"]], writes=[hb.k]))
        yo = yout[s]

        def proj(st, g):
            bk, bkk = st.nb()
            for kc in range(8):
                pe(lambda e, kc=kc, bk=bk: e.matmul(bk[0:64, 0:BLK], lhsT=wbf[:, kc, g * 64:(g + 1) * 64], rhs=hb[:, kc, :], start=(kc == 0), stop=(kc == 7)),
                   [hb.k, WBK[kc]], [bkk])
            return bk, bkk

        def proj2(st, g):
            bk, bkk = st.nb()
            for kc in range(8):
                pe(lambda e, kc=kc, bk=bk: e.matmul(bk[0:128, 0:BLK], lhsT=wbf[:, kc, g * 64:(g + 2) * 64], rhs=hb[:, kc, :], start=(kc == 0), stop=(kc == 7)),
                   [hb.k, WBK[kc]], [bkk])
            return bk, bkk

        def raw_evac(g, bk, bkk, half):
            r_ = raw[g][s]; ro = raw[g][1 - s]
            act(lambda e: e.copy(out=r_[:, 3:BLK + 3], in_=bk[half * 64:(half + 1) * 64, 0:BLK]), [bkk], [r_.k])
            pool(lambda e: e.tensor_copy(out=r_[:, 0:3], in_=ro[:, BLK:BLK + 3]), [ro.k], [r_.k])

        def gdn():
            W, st = WG, SG
            bk, bkk = proj2(st, GQ)
            raw_evac(GQ, bk, bkk, 0); raw_evac(GK, bk, bkk, 1)
            cut()
            bk, bkk = proj2(st, GV)
            raw_evac(GV, bk, bkk, 0)
            sg = W["p11"]
            act(lambda e, bk=bk: e.activation(out=sg[:], in_=bk[64:128, 0:BLK], func=AF.Silu), [bkk], [sg.k])
            cut()
            def conv(g, outt, ci):
                r_ = raw[g][s]
                dve(lambda e: e.tensor_scalar(out=outt[:], in0=r_[:, 3:BLK + 3], scalar1=PP(PC_CONV + ci * 4 + 3), scalar2=None, op0=ALU.mult), [r_.k] + DPK, [outt.k])
                for j in (2, 1, 0):
                    dve(lambda e, j=j: e.scalar_tensor_tensor(out=outt[:], in0=r_[:, j:j + BLK], scalar=PP(PC_CONV + ci * 4 + j), in1=outt[:], op0=ALU.mult, op1=ALU.add), [r_.k, outt.k], [outt.k])
                act(lambda e: e.activation(out=outt[:], in_=outt[:], func=AF.Silu), [outt.k], [outt.k])
                cut()
            qs, ks, vs = W["p0"], W["p1"], W["p2"]
            conv(GQ, qs, 0); conv(GK, ks, 1); conv(GV, vs, 2)
            qn, kn = W["r0"], W["r1"]
            l2n(st, qs, qn, 0.125, W["r2"], W["p3"]); l2n(st, ks, kn, 1.0, W["r2"], W["p3"])
            bk, bkk = proj(st, GSC)
            act(lambda e: e.activation(out=srow[0:1, :], in_=bk[0:1, 0:BLK], func=AF.Sigmoid), [bkk], [srow.k])
            act(lambda e: e.activation(out=srow[32:33, :], in_=bk[32:33, 0:BLK], func=AF.Exp, bias=pp[32:33, PC_DTB:PC_DTB + 1]), [bkk] + DPK, [srow.k])
            act(lambda e: e.activation(out=srow[32:33, :], in_=srow[32:33, :], func=AF.Ln, bias=1.0), [srow.k], [srow.k])
            dve(lambda e: e.tensor_scalar(out=srow[32:33, :], in0=srow[32:33, :], scalar1=dp[32:33, D_NA:D_NA + 1], scalar2=None, op0=ALU.mult), [srow.k] + DPK, [srow.k])
            bk, bkk = transp8(st, srow, srow.k, rows=33)
            beta, gcol = CL(0), CL(1)
            dve(lambda e: e.tensor_copy(out=beta, in_=v3(bk[0:64, 0:BLK])[:, :, 0]), [bkk], ["c_beta"])
            dve(lambda e: e.tensor_copy(out=gcol, in_=v3(bk[0:64, 0:BLK])[:, :, 32]), [bkk], ["c_g"])
            bk, bkk = st.nb()
            pe(lambda e: e.matmul(bk[0:64, 0:NCH], lhsT=TRI, rhs=gcol, start=True, stop=True), ["c_g", cst.k], [bkk])
            pe(lambda e: e.matmul(bk[0:64, NCH:2 * NCH], lhsT=ONES, rhs=gcol, start=True, stop=True), ["c_g", cst.k], [bkk])
            gc, gcl, eg, beg, edl, last, lnb, gcb = CL(2), CL(3), CL(4), CL(5), CL(6), CL(7), CL(8), CL(9)
            dve(lambda e: e.tensor_copy(out=colsG[:, 2:4, 0:NCH], in_=bk[0:64, 0:2 * NCH].rearrange("p (a n) -> p a n", a=2)), [bkk], ["c_gc"])
            act(lambda e: e.activation(out=eg, in_=gc, func=AF.Exp), ["c_gc"], ["c_eg"])
            act(lambda e: e.activation(out=last, in_=gcl, func=AF.Exp), ["c_gc"], ["c_last"])
            act(lambda e: e.activation(out=lnb, in_=beta, func=AF.Ln), ["c_beta"], ["c_lnb"])
            dve(lambda e: e.tensor_tensor(out=beg, in0=beta, in1=eg, op=ALU.mult), ["c_beta", "c_eg"], ["c_beg"])
            dve(lambda e: e.tensor_tensor(out=edl, in0=gcl, in1=gc, op=ALU.subtract), ["c_gc"], ["c_edl0"])
            act(lambda e: e.activation(out=edl, in_=edl, func=AF.Exp), ["c_edl0"], ["c_edl"])
            dve(lambda e: e.tensor_tensor(out=gcb, in0=gc, in1=lnb, op=ALU.add), ["c_gc", "c_lnb"], ["c_gcb"])
            cut()
            def rowb(colap, colk, maskc, outt, subcol, subk):
                D = W["p4"]
                dve(lambda e: e.tensor_tensor(out=v3(D[:]), in0=v3(C(C_ID)), in1=bc(colap), op=ALU.mult), [colk, cst.k], [D.k])
                bk, bkk = st.nb()
                pe(lambda e: e.matmul(bk[0:64, 0:BLK], lhsT=ONES, rhs=D[:], start=True, stop=True), [D.k, cst.k], [bkk])
                if subcol is not None:
                    dve(lambda e: e.tensor_tensor(out=v3(outt[:]), in0=v3(bk[0:64, 0:BLK]), in1=bc(subcol), op=ALU.subtract), [bkk, subk], [outt.k])
                    if maskc is not None:
                        dve(lambda e: e.tensor_tensor(out=outt[:], in0=outt[:], in1=C(maskc), op=ALU.add), [outt.k, cst.k], [outt.k])
                    act(lambda e: e.activation(out=outt[:], in_=outt[:], func=AF.Exp), [outt.k], [outt.k])
                else:
                    act(lambda e: e.activation(out=outt[:], in_=bk[0:64, 0:BLK], func=AF.Exp), [bkk], [outt.k])
                cut()
            decT, AdT, egrow = W["p5"], W["p6"], W["p7"]
            rowb(gc, "c_gc", C_MNI, decT, gc, "c_gc")
            rowb(gcb, "c_gcb", C_MNS, AdT, gc, "c_gc")
            rowb(gc, "c_gc", None, egrow, None, None)
            qdT = W["r3"]
            dve(lambda e: e.tensor_tensor(out=R(qdT[:]), in0=qn[:], in1=egrow[:], op=ALU.mult), [qn.k, egrow.k], [qdT.k])
            Z0, QKm, Y0, Wt = W["r4"], W["r5"], W["r6"], W["r7"]
            bk, bkk = chunk_mm(st, kn, kn.k, kn, kn.k)
            dve(lambda e, bk=bk: e.scalar_tensor_tensor(out=R(Z0[:]), in0=bk[0:64, 0:BLK], scalar=-1.0, in1=AdT[:], op0=ALU.mult, op1=ALU.mult), [bkk, AdT.k], [Z0.k])
            cut()
            bk, bkk = chunk_mm(st, kn, kn.k, qn, qn.k)
            dve(lambda e, bk=bk: e.tensor_tensor(out=R(QKm[:]), in0=bk[0:64, 0:BLK], in1=decT[:], op=ALU.mult), [bkk, decT.k], [QKm.k])
            cut()
            bk, bkk = transp8(st, Z0, Z0.k)
            act(lambda e, bk=bk: e.copy(out=R(Y0[:]), in_=bk[0:64, 0:BLK]), [bkk], [Y0.k])
            cut()
            neumann(st, Z0, Y0, Wt, W["r8"], W["r9"])
            kbe, kdec, vb = W["r10"], W["r11"], W["r12"]
            bk, bkk = transp8(st, kn, kn.k)
            dve(lambda e, bk=bk: e.tensor_tensor(out=v3(R(kbe[:])), in0=v3(bk[0:64, 0:BLK]), in1=bc(beg), op=ALU.mult), [bkk, "c_beg"], [kbe.k])
            dve(lambda e, bk=bk: e.tensor_tensor(out=v3(R(kdec[:])), in0=v3(bk[0:64, 0:BLK]), in1=bc(edl), op=ALU.mult), [bkk, "c_edl"], [kdec.k])
            cut()
            bk, bkk = transp8(st, vs, vs.k)
            dve(lambda e, bk=bk: e.tensor_tensor(out=v3(R(vb[:])), in0=v3(bk[0:64, 0:BLK]), in1=bc(beta), op=ALU.mult), [bkk, "c_beta"], [vb.k])
            cut()
            u, wT = W["p8"], W["r13"]
            bk, bkk = chunk_mm(st, Wt, Wt.k, vb, vb.k)
            act(lambda e, bk=bk: e.copy(out=u[:], in_=bk[0:64, 0:BLK]), [bkk], [u.k])
            cut()
            bk, bkk = chunk_mm(st, kbe, kbe.k, Wt, Wt.k)
            act(lambda e, bk=bk: e.copy(out=R(wT[:]), in_=bk[0:64, 0:BLK]), [bkk], [wT.k])
            cut()
            ob, obk = st.ob, st.obk
            vnew = W["r14"]
            for n in range(NCH):
                cs = slice(n * 64, (n + 1) * 64)
                ci = b * NCH + n
                So, Sn = S["g"][ci % 2], S["g"][(ci + 1) % 2]
                ta, tak = st.nb()
                pe(lambda e, cs=cs, So=So, ta=ta: e.matmul(ta[0:64, 0:64], lhsT=R(wT[:, cs]), rhs=R(So[:]), start=True, stop=True), [wT.k, So.k], [tak])
                dve(lambda e, cs=cs, ta=ta: e.tensor_tensor(out=R(vnew[:, cs]), in0=u[:, cs], in1=ta[0:64, 0:64], op=ALU.subtract), [u.k, tak], [vnew.k + str(n)])
                pe(lambda e, cs=cs, So=So: e.matmul(ob[0:64, cs], lhsT=R(qdT[:, cs]), rhs=R(So[:]), start=True, stop=False), [qdT.k, So.k], [obk])
                pe(lambda e, cs=cs: e.matmul(ob[0:64, cs], lhsT=R(QKm[:, cs]), rhs=R(vnew[:, cs]), start=False, stop=True), [QKm.k, vnew.k + str(n)], [obk])
                tb, tbk = st.nb()
                pe(lambda e, cs=cs, tb=tb: e.matmul(tb[0:64, 0:64], lhsT=R(kdec[:, cs]), rhs=R(vnew[:, cs]), start=True, stop=True), [kdec.k, vnew.k + str(n)], [tbk])
                dve(lambda e, n=n, So=So, Sn=Sn, tb=tb: e.scalar_tensor_tensor(out=R(Sn[:]), in0=So[:], scalar=last[:, n:n + 1], in1=tb[0:64, 0:64], op0=ALU.mult, op1=ALU.add), [So.k, "c_last", tbk], [Sn.k])
                cut()
            og, sq, c1 = W["p9"], W["p10"], CL(10)
            act(lambda e: e.copy(out=og[:], in_=ob[0:64, 0:BLK]), [obk], [og.k])
            dve(lambda e: e.tensor_tensor(out=sq[:], in0=og[:], in1=og[:], op=ALU.mult), [og.k], [sq.k])
            dve(lambda e: e.tensor_reduce(out=c1, in_=v3(sq[:]), axis=AX.X, op=ALU.add), [sq.k], ["c_c1"])
            act(lambda e: e.activation(out=c1, in_=c1, func=AF.Ln, scale=1.0 / 64, bias=1e-6), ["c_c1"], ["c_c1"])
            act(lambda e: e.activation(out=c1, in_=c1, func=AF.Exp, scale=-0.5), ["c_c1"], ["c_c1"])
            dve(lambda e: e.tensor_tensor(out=v3(og[:]), in0=v3(og[:]), in1=bc(c1), op=ALU.mult), [og.k, "c_c1"], [og.k])
            cut()
            bk, bkk = transp8(st, og, og.k)
            dve(lambda e, bk=bk: e.scalar_tensor_tensor(out=yo[:, 0, :], in0=bk[0:64, 0:BLK], scalar=PP(PC_GNW), in1=sg[:], op0=ALU.mult, op1=ALU.mult), [bkk, sg.k] + DPK, [yo.k + "0"])
            cut()

        def hgrn():
            W, st = WH, SH
            hq, hsg, hv, hgt = W["p0"], W["p1"], W["p2"], W["p3"]
            bk, bkk = proj2(st, HQ)
            act(lambda e, bk=bk: e.activation(out=hq[:], in_=bk[0:64, 0:BLK], func=AF.Silu), [bkk], [hq.k])
            act(lambda e, bk=bk: e.activation(out=hsg[:], in_=bk[64:128, 0:BLK], func=AF.Sigmoid), [bkk], [hsg.k])
            cut()
            bk, bkk = proj2(st, HI)
            act(lambda e, bk=bk: e.copy(out=hv[:], in_=bk[0:64, 0:BLK]), [bkk], [hv.k])
            act(lambda e, bk=bk: e.activation(out=hgt[:], in_=bk[64:128, 0:BLK], func=AF.Silu), [bkk], [hgt.k])
            cut()
            hk, lg, bT = W["p4"], W["p5"], W["p6"]
            dve(lambda e: e.tensor_scalar(out=hk[:], in0=hsg[:], scalar1=DP(D_NOML), scalar2=DP(D_OML), op0=ALU.mult, op1=ALU.add), [hsg.k] + DPK, [hk.k])
            dve(lambda e: e.tensor_scalar(out=lg[:], in0=hsg[:], scalar1=DP(D_OML), scalar2=DP(D_LB), op0=ALU.mult, op1=ALU.add), [hsg.k] + DPK, [lg.k])
            act(lambda e: e.activation(out=lg[:], in_=lg[:], func=AF.Ln), [lg.k], [lg.k])
            dve(lambda e: e.tensor_tensor_scan(out=bT[:], data0=C(C_RST), data1=lg[:], initial=0.0, op0=ALU.mult, op1=ALU.add), [lg.k, cst.k], [bT.k])
            cut()
            bmid, blast, ebl = CL(11), CL(12), CL(13)
            dve(lambda e: e.tensor_copy(out=bmid, in_=v3(bT[:])[:, :, 31]), [bT.k], ["c_bmid"])
            dve(lambda e: e.tensor_copy(out=blast, in_=v3(bT[:])[:, :, 63]), [bT.k], ["c_blast"])
            act(lambda e: e.activation(out=ebl, in_=blast, func=AF.Exp), ["c_blast"], ["c_ebl"])
            qtT, ktT, qbT, kpT = W["r0"], W["r1"], W["r2"], W["p7"]
            tA, tB, tC, t1, t2 = W["p8"], W["p9"], W["p10"], W["p11"], W["p12"]
            dve(lambda e: e.tensor_tensor(out=v3(t1[:]), in0=v3(bT[:]), in1=bc(bmid), op=ALU.subtract), [bT.k, "c_bmid"], [t1.k])
            dve(lambda e: e.tensor_tensor(out=v3(t2[:]), in0=v3(bT[:]), in1=bc(blast), op=ALU.subtract), [bT.k, "c_blast"], [t2.k])
            act(lambda e: e.activation(out=tA[:], in_=t1[:], func=AF.Exp), [t1.k], [tA.k])
            act(lambda e: e.activation(out=tB[:], in_=t1[:], func=AF.Exp, scale=-1.0), [t1.k], [tB.k])
            act(lambda e: e.activation(out=tC[:], in_=bT[:], func=AF.Exp), [bT.k], [tC.k])
            act(lambda e: e.activation(out=kpT[:], in_=t2[:], func=AF.Exp, scale=-1.0), [t2.k], [kpT.k])
            cut()
            dve(lambda e: e.tensor_tensor(out=R(qtT[:]), in0=tA[:], in1=hq[:], op=ALU.mult), [tA.k, hq.k], [qtT.k])
            dve(lambda e: e.tensor_tensor(out=R(ktT[:]), in0=tB[:], in1=hk[:], op=ALU.mult), [tB.k, hk.k], [ktT.k])
            dve(lambda e: e.tensor_tensor(out=R(qbT[:]), in0=tC[:], in1=hq[:], op=ALU.mult), [tC.k, hq.k], [qbT.k])
            dve(lambda e: e.tensor_tensor(out=kpT[:], in0=kpT[:], in1=hk[:], op=ALU.mult), [kpT.k, hk.k], [kpT.k])
            cut()
            attm, kpTM, hvTM = W["r3"], W["r4"], W["r5"]
            bk, bkk = chunk_mm(st, ktT, ktT.k, qtT, qtT.k)
            dve(lambda e, bk=bk: e.tensor_tensor(out=R(attm[:]), in0=bk[0:64, 0:BLK], in1=C(C_M01I), op=ALU.mult), [bkk, cst.k], [attm.k])
            cut()
            bk, bkk = transp8(st, kpT, kpT.k)
            act(lambda e, bk=bk: e.copy(out=R(kpTM[:]), in_=bk[0:64, 0:BLK]), [bkk], [kpTM.k])
            cut()
            bk, bkk = transp8(st, hv, hv.k)
            act(lambda e, bk=bk: e.copy(out=R(hvTM[:]), in_=bk[0:64, 0:BLK]), [bkk], [hvTM.k])
            cut()
            ob, obk = st.ob, st.obk
            for n in range(NCH):
                cs = slice(n * 64, (n + 1) * 64)
                ci = b * NCH + n
                So, Sn = S["h"][ci % 2], S["h"][(ci + 1) % 2]
                pe(lambda e, cs=cs: e.matmul(ob[0:64, cs], lhsT=R(attm[:, cs]), rhs=R(hvTM[:, cs]), start=True, stop=False), [attm.k, hvTM.k], [obk])
                pe(lambda e, cs=cs, So=So: e.matmul(ob[0:64, cs], lhsT=R(qbT[:, cs]), rhs=R(So[:]), start=False, stop=True), [qbT.k, So.k], [obk])
                tc_, tck = st.nb()
                pe(lambda e, cs=cs, tc_=tc_: e.matmul(tc_[0:64, 0:64], lhsT=R(kpTM[:, cs]), rhs=R(hvTM[:, cs]), start=True, stop=True), [kpTM.k, hvTM.k], [tck])
                dve(lambda e, n=n, So=So, Sn=Sn, tc_=tc_: e.scalar_tensor_tensor(out=R(Sn[:]), in0=So[:], scalar=ebl[:, n:n + 1], in1=tc_[0:64, 0:64], op0=ALU.mult, op1=ALU.add), [So.k, "c_ebl", tck], [Sn.k])
                cut()
            oh, sqh = W["p8"], W["p13"]
            act(lambda e: e.copy(out=oh[:], in_=ob[0:64, 0:BLK]), [obk], [oh.k])
            dve(lambda e: e.tensor_tensor(out=sqh[:], in0=oh[:], in1=oh[:], op=ALU.mult), [oh.k], [sqh.k])
            dve(lambda e: e.tensor_reduce(out=ssall[:, b * NCH:(b + 1) * NCH], in_=v3(sqh[:]), axis=AX.X, op=ALU.add), [sqh.k], [ssall.k])
            cut()
            bk, bkk = transp8(st, oh, oh.k)
            dve(lambda e, bk=bk: e.scalar_tensor_tensor(out=yo[:, 2, :], in0=bk[0:64, 0:BLK], scalar=PP(PC_HNW), in1=hgt[:], op0=ALU.mult, op1=ALU.mult), [bkk, hgt.k] + DPK, [yo.k + "2"])
            cut()

        def rwkv():
            W, st = WR, SR
            for g0 in (RR, RV, RWD):
                bk, bkk = proj2(st, g0)
                raw_evac(g0, bk, bkk, 0); raw_evac(g0 + 1, bk, bkk, 1)
                cut()
            def shift(g, outt, mi):
                r_ = raw[g][s]
                dve(lambda e: e.tensor_scalar(out=outt[:], in0=r_[:, 3:BLK + 3], scalar1=DP(D_OMM + mi), scalar2=None, op0=ALU.mult), [r_.k] + DPK, [outt.k])
                dve(lambda e: e.scalar_tensor_tensor(out=outt[:], in0=r_[:, 2:BLK + 2], scalar=PP(PC_MU + mi), in1=outt[:], op0=ALU.mult, op1=ALU.add), [r_.k, outt.k], [outt.k])
                cut()
            rr_, rk_, rv_, rwd_, rad_, rg_ = W["p0"], W["p1"], W["p2"], W["p3"], W["p4"], W["p5"]
            shift(RR, rr_, 0); shift(RK, rk_, 1); shift(RV, rv_, 2); shift(RWD, rwd_, 3); shift(RAD, rad_, 4); shift(RG, rg_, 5)
            act(lambda e: e.activation(out=rg_[:], in_=rg_[:], func=AF.Silu), [rg_.k], [rg_.k])
            act(lambda e: e.activation(out=rwd_[:], in_=rwd_[:], func=AF.Tanh), [rwd_.k], [rwd_.k])
            lw, aa = W["p6"], W["p7"]
            bk, bkk = st.nb()
            pe(lambda e, bk=bk: e.matmul(bk[0:64, 0:BLK], lhsT=lora[:, 0:64], rhs=rwd_[:], start=True, stop=True), [lora.k, rwd_.k], [bkk])
            act(lambda e, bk=bk: e.activation(out=lw[:], in_=bk[0:64, 0:BLK], func=AF.Exp, scale=-1.0, bias=DP(D_NW0)), [bkk] + DPK, [lw.k])
            dve(lambda e: e.tensor_scalar(out=lw[:], in0=lw[:], scalar1=1.0, scalar2=None, op0=ALU.add), [lw.k], [lw.k])
            dve(lambda e: e.reciprocal(out=lw[:], in_=lw[:]), [lw.k], [lw.k])
            dve(lambda e: e.tensor_scalar(out=lw[:], in0=lw[:], scalar1=-0.6065306597126334, scalar2=None, op0=ALU.mult), [lw.k], [lw.k])
            cut()
            bk, bkk = st.nb()
            pe(lambda e, bk=bk: e.matmul(bk[0:64, 0:BLK], lhsT=lora[:, 64:128], rhs=rad_[:], start=True, stop=True), [lora.k, rad_.k], [bkk])
            act(lambda e, bk=bk: e.activation(out=aa[:], in_=bk[0:64, 0:BLK], func=AF.Sigmoid, bias=PP(PC_A0)), [bkk] + DPK, [aa.k])
            cut()
            kk, sqk, kkr = W["r0"], W["r1"], W["p8"]
            dve(lambda e: e.tensor_scalar(out=kkr[:], in0=rk_[:], scalar1=PP(PC_KK), scalar2=None, op0=ALU.mult), [rk_.k] + DPK, [kkr.k])
            l2n(st, kkr, kk, 1.0, sqk, W["p9"])
            k2 = W["p10"]
            dve(lambda e: e.tensor_scalar(out=k2[:], in0=aa[:], scalar1=PP(PC_KA), scalar2=DP(D_OMKA), op0=ALU.mult, op1=ALU.add), [aa.k] + DPK, [k2.k])
            dve(lambda e: e.tensor_tensor(out=k2[:], in0=k2[:], in1=rk_[:], op=ALU.mult), [k2.k, rk_.k], [k2.k])
            bon, bonF = W["r2"], W["p11"]
            dve(lambda e: e.scalar_tensor_tensor(out=R(bon[:]), in0=rr_[:], scalar=PP(PC_RK), in1=k2[:], op0=ALU.mult, op1=ALU.mult), [rr_.k, k2.k] + DPK, [bon.k])
            bk, bkk = ones_mm(st, bon[:], bon.k)
            dve(lambda e, bk=bk: e.tensor_tensor(out=bonF[:], in0=bk[0:64, 0:BLK], in1=rv_[:], op=ALU.mult), [bkk, rv_.k], [bonF.k])
            cut()
            cT, cxT = W["p12"], W["p13"]
            dve(lambda e: e.tensor_tensor_scan(out=cT[:], data0=C(C_RST), data1=lw[:], initial=0.0, op0=ALU.mult, op1=ALU.add), [lw.k, cst.k], [cT.k])
            dve(lambda e: e.tensor_tensor(out=cxT[:], in0=cT[:], in1=lw[:], op=ALU.subtract), [cT.k, lw.k], [cxT.k])
            clast, ecl = CL(15), CL(16)
            dve(lambda e: e.tensor_copy(out=clast, in_=v3(cT[:])[:, :, 63]), [cT.k], ["c_clast"])
            act(lambda e: e.activation(out=ecl, in_=clast, func=AF.Exp), ["c_clast"], ["c_ecl"])
            atT, rtT, btT, ktT2, bhT, khT = W["r3"], W["r4"], W["r5"], W["r6"], W["p14"], W["p15"]
            ec, enc, ecx, ecc = W["p16"], W["p17"], W["p18"], W["p19"]
            act(lambda e: e.activation(out=ec[:], in_=cT[:], func=AF.Exp), [cT.k], [ec.k])
            act(lambda e: e.activation(out=enc[:], in_=cT[:], func=AF.Exp, scale=-1.0), [cT.k], [enc.k])
            act(lambda e: e.activation(out=ecx[:], in_=cxT[:], func=AF.Exp), [cxT.k], [ecx.k])
            dve(lambda e: e.tensor_tensor(out=v3(ecc[:]), in0=v3(cT[:]), in1=bc(clast), op=ALU.subtract), [cT.k, "c_clast"], [ecc.k])
            act(lambda e: e.activation(out=ecc[:], in_=ecc[:], func=AF.Exp, scale=-1.0), [ecc.k], [ecc.k])
            cut()
            ka = W["p20"]
            dve(lambda e: e.tensor_tensor(out=ka[:], in0=kk[:], in1=aa[:], op=ALU.mult), [kk.k, aa.k], [ka.k])
            dve(lambda e: e.scalar_tensor_tensor(out=R(atT[:]), in0=kk[:], scalar=-1.0, in1=ecx[:], op0=ALU.mult, op1=ALU.mult), [kk.k, ecx.k], [atT.k])
            dve(lambda e: e.tensor_tensor(out=R(rtT[:]), in0=rr_[:], in1=ec[:], op=ALU.mult), [rr_.k, ec.k], [rtT.k])
            dve(lambda e: e.tensor_tensor(out=R(btT[:]), in0=ka[:], in1=enc[:], op=ALU.mult), [ka.k, enc.k], [btT.k])
            cut()
            dve(lambda e: e.tensor_tensor(out=R(ktT2[:]), in0=k2[:], in1=enc[:], op=ALU.mult), [k2.k, enc.k], [ktT2.k])
            dve(lambda e: e.tensor_tensor(out=bhT[:], in0=ka[:], in1=ecc[:], op=ALU.mult), [ka.k, ecc.k], [bhT.k])
            dve(lambda e: e.tensor_tensor(out=khT[:], in0=k2[:], in1=ecc[:], op=ALU.mult), [k2.k, ecc.k], [khT.k])
            cut()
            Z0r, AakT, ArbT, ArkT, Y0r, Wr = W["r7"], W["r8"], W["r9"], W["r10"], W["r11"], W["r12"]
            def mmask(l, r, maskc, outt):
                bk, bkk = chunk_mm(st, l, l.k, r, r.k)
                dve(lambda e, bk=bk: e.tensor_tensor(out=R(outt[:]), in0=bk[0:64, 0:BLK], in1=C(maskc), op=ALU.mult), [bkk, cst.k], [outt.k])
                cut()
            mmask(btT, atT, C_M01S, Z0r)
            bk, bkk = transp8(st, Z0r, Z0r.k)
            act(lambda e, bk=bk: e.copy(out=R(Y0r[:]), in_=bk[0:64, 0:BLK]), [bkk], [Y0r.k])
            cut()
            mmask(ktT2, atT, C_M01S, AakT)
            mmask(btT, rtT, C_M01I, ArbT)
            mmask(ktT2, rtT, C_M01I, ArkT)
            aTM, bhTM, khTM, vTM = W["r15"], W["r16"], W["r17"], W["r18"]
            for src, dst in ((atT, aTM), (bhT, bhTM), (khT, khTM), (rv_, vTM)):
                bk, bkk = transp8(st, src, src.k)
                act(lambda e, bk=bk, dst=dst: e.copy(out=R(dst[:]), in_=bk[0:64, 0:BLK]), [bkk], [dst.k])
                cut()
            neumann(st, Z0r, Y0r, Wr, W["r13"], W["r14"])
            ApT, X2, UV = W["r19"], W["r20"], W["p21"]
            bk, bkk = chunk_mm(st, aTM, aTM.k, Wr, Wr.k)
            act(lambda e, bk=bk: e.copy(out=R(ApT[:]), in_=bk[0:64, 0:BLK]), [bkk], [ApT.k])
            cut()
            bk, bkk = chunk_mm(st, AakT, AakT.k, vTM, vTM.k)
            act(lambda e, bk=bk: e.copy(out=R(X2[:]), in_=bk[0:64, 0:BLK]), [bkk], [X2.k])
            cut()
            bk, bkk = chunk_mm(st, Wr, Wr.k, X2, X2.k)
            act(lambda e, bk=bk: e.copy(out=UV[:], in_=bk[0:64, 0:BLK]), [bkk], [UV.k])
            cut()
            ob, obk = st.ob, st.obk
            Ut = W["r21"]
            for n in range(NCH):
                cs = slice(n * 64, (n + 1) * 64)
                ci = b * NCH + n
                So, Sn = S["r"][ci % 2], S["r"][(ci + 1) % 2]
                td, tdk = st.nb()
                pe(lambda e, cs=cs, So=So, td=td: e.matmul(td[0:64, 0:64], lhsT=R(ApT[:, cs]), rhs=R(So[:]), start=True, stop=True), [ApT.k, So.k], [tdk])
                dve(lambda e, cs=cs, td=td: e.tensor_tensor(out=R(Ut[:, cs]), in0=UV[:, cs], in1=td[0:64, 0:64], op=ALU.add), [UV.k, tdk], [Ut.k + str(n)])
                pe(lambda e, cs=cs, So=So: e.matmul(ob[0:64, cs], lhsT=R(rtT[:, cs]), rhs=R(So[:]), start=True, stop=False), [rtT.k, So.k], [obk])
                pe(lambda e, cs=cs: e.matmul(ob[0:64, cs], lhsT=R(ArbT[:, cs]), rhs=R(Ut[:, cs]), start=False, stop=False), [ArbT.k, Ut.k + str(n)], [obk])
                pe(lambda e, cs=cs: e.matmul(ob[0:64, cs], lhsT=R(ArkT[:, cs]), rhs=R(vTM[:, cs]), start=False, stop=True), [ArkT.k, vTM.k], [obk])
                te, tek = st.nb()
                pe(lambda e, cs=cs, te=te: e.matmul(te[0:64, 0:64], lhsT=R(bhTM[:, cs]), rhs=R(Ut[:, cs]), start=True, stop=False), [bhTM.k, Ut.k + str(n)], [tek])
                pe(lambda e, cs=cs, te=te: e.matmul(te[0:64, 0:64], lhsT=R(khTM[:, cs]), rhs=R(vTM[:, cs]), start=False, stop=True), [khTM.k, vTM.k], [tek])
                dve(lambda e, n=n, So=So, Sn=Sn, te=te: e.scalar_tensor_tensor(out=R(Sn[:]), in0=So[:], scalar=ecl[:, n:n + 1], in1=te[0:64, 0:64], op0=ALU.mult, op1=ALU.add), [So.k, "c_ecl", tek], [Sn.k])
                cut()
            orr, sqr = W["p16"], W["p17"]
            m1, m2 = CL(17), CL(18)
            act(lambda e: e.copy(out=orr[:], in_=ob[0:64, 0:BLK]), [obk], [orr.k])
            dve(lambda e: e.tensor_reduce(out=m1, in_=v3(orr[:]), axis=AX.X, op=ALU.add), [orr.k], ["c_m1"])
            dve(lambda e: e.tensor_scalar(out=m1, in0=m1, scalar1=1.0 / 64, scalar2=None, op0=ALU.mult), ["c_m1"], ["c_m1"])
            dve(lambda e: e.tensor_tensor(out=v3(orr[:]), in0=v3(orr[:]), in1=bc(m1), op=ALU.subtract), [orr.k, "c_m1"], [orr.k])
            dve(lambda e: e.tensor_tensor(out=sqr[:], in0=orr[:], in1=orr[:], op=ALU.mult), [orr.k], [sqr.k])
            dve(lambda e: e.tensor_reduce(out=m2, in_=v3(sqr[:]), axis=AX.X, op=ALU.add), [sqr.k], ["c_m2"])
            act(lambda e: e.activation(out=m2, in_=m2, func=AF.Ln, scale=1.0 / 64, bias=64e-5), ["c_m2"], ["c_m2"])
            act(lambda e: e.activation(out=m2, in_=m2, func=AF.Exp, scale=-0.5), ["c_m2"], ["c_m2"])
            dve(lambda e: e.tensor_tensor(out=v3(orr[:]), in0=v3(orr[:]), in1=bc(m2), op=ALU.mult), [orr.k, "c_m2"], [orr.k])
            cut()
            bk, bkk = transp8(st, orr, orr.k)
            dve(lambda e, bk=bk: e.tensor_scalar(out=sqr[:], in0=bk[0:64, 0:BLK], scalar1=PP(PC_LNW), scalar2=PP(PC_LNB), op0=ALU.mult, op1=ALU.add), [bkk] + DPK, [sqr.k])
            dve(lambda e: e.tensor_tensor(out=sqr[:], in0=sqr[:], in1=bonF[:], op=ALU.add), [sqr.k, bonF.k], [sqr.k])
            dve(lambda e: e.tensor_tensor(out=yo[:, 1, :], in0=sqr[:], in1=rg_[:], op=ALU.mult), [sqr.k, rg_.k], [yo.k + "1"])
            cut()

        blk_streams = []
        for fn in (gdn, rwkv, hgrn):
            cur[0] = [[]]
            fn()
            blk_streams.append([g for g in cur[0] if g])
            cur[0] = None
        all_streams.append(blk_streams)
        stores.append(lambda b=b, s=s, yo=yo: p.dma("sp", f"d_y{s}", A["y_block"](b), yo[:], reads=[yo.k + "0", yo.k + "1", yo.k + "2"], writes=[A["y_key"]]))

    ns = len(all_streams[0])
    tot = [sum(len(all_streams[b][i]) for b in range(NB)) for i in range(ns)]
    blk = [0] * ns; gi = [0] * ns; done = [0] * ns
    loaded = -1; stored = -1
    def advance(i):
        while blk[i] < NB and gi[i] >= len(all_streams[blk[i]][i]):
            blk[i] += 1; gi[i] = 0
    for i in range(ns):
        advance(i)
    while True:
        active = [i for i in range(ns) if blk[i] < NB]
        mb = min(blk)
        while stored < min(mb, NB) - 1:
            stored += 1
            stores[stored]()
        if not active:
            break
        cand = [i for i in active if blk[i] <= mb + 1]
        i = min(cand, key=lambda j: done[j] / max(1, tot[j]))
        while loaded < blk[i]:
            loaded += 1
            loads[loaded]()
        for th in all_streams[blk[i]][i][gi[i]]:
            th()
        gi[i] += 1; done[i] += 1
        advance(i)

    ncol = T // 64
    sst = Tl("sst", (128, 2, 64))
    evs = []
    for h in range((ncol + 127) // 128):
        w = min(128, ncol - h * 128)
        bk, bkk = banks[0], "bank0"
        pe(lambda e, h=h, w=w, bk=bk: e.transpose(out=bk[0:w, 0:64], in_=ssall[:, h * 128:h * 128 + w], identity=IDN), [ssall.k, cst.k], [bkk])
        act(lambda e, h=h, w=w, bk=bk: e.copy(out=sst[0:w, h, :], in_=bk[0:w, 0:64]), [bkk], [sst.k])
        evs.append(p.dma("sp", "d_ss", A["ssd"][h * 128:h * 128 + w, :], sst[0:w, h, :], reads=[sst.k], writes=[A["ss_key"]]))
    return evs


def build_mixer(nc, T, layer):
    p = Prog(nc)
    hT = nc.dram_tensor("hT", [1024, T], BF16, kind="ExternalInput").ap()
    wd = nc.dram_tensor("w", [1024, NG * 64], F32, kind="ExternalInput").ap()
    ppd = nc.dram_tensor("pp", [64, NPAR], F32, kind="ExternalInput").ap()
    lorad = nc.dram_tensor("lora", [64, 128], F32, kind="ExternalInput").ap()
    cd = nc.dram_tensor("consts", [64, NCONST * BLK], F32, kind="ExternalInput").ap()
    yT = nc.dram_tensor("yT", [192, T], BF16, kind="ExternalOutput").ap()
    ssd = nc.dram_tensor("ss", [T // 64, 64], F32, kind="ExternalOutput").ap()
    A = {"w": wd, "pp": ppd, "lora": lorad, "consts": cd, "ssd": ssd,
         "hT_block": lambda b: hT.rearrange("(kc q) t -> q kc t", q=128)[:, :, b * BLK:(b + 1) * BLK],
         "y_block": lambda b: yT.rearrange("(m c) t -> c m t", m=3)[:, :, b * BLK:(b + 1) * BLK],
         "hT_key": "hT_ext", "y_key": "y_ext", "ss_key": "ss_ext"}
    evs = mixer_body(p, nc, T, layer, A, "")
    for ev in evs:
        p.finish_wait("sp", ev)
    for s in range(2):
        if f"d_y{s}" in p.dma_count:
            p.finish_wait("sp", (f"d_y{s}", p.dma_count[f"d_y{s}"]))
    p.emit()
    p.close()
    return nc


def mixer_stage(p, nc, T, layer, hT_all, wd, ppd, lorad, cd, y_loc, ssd, tag, NT=2048):
    p.push_scope()
    bps = NT // BLK
    A = {"w": wd, "pp": ppd, "lora": lorad, "consts": cd, "ssd": ssd,
         "hT_block": lambda b: hT_all.rearrange("(r kc q) t -> r q kc t", r=8, q=128)[b // bps][:, :, (b % bps) * BLK:(b % bps + 1) * BLK],
         "y_block": lambda b: y_loc[(b // bps) * 192:(b // bps + 1) * 192, (b % bps) * BLK:(b % bps + 1) * BLK].rearrange("(m c) t -> c m t", m=3),
         "hT_key": "hT_all", "y_key": "y_loc", "ss_key": "ss_loc"}
    mixer_body(p, nc, T, layer, A, tag)
    p.pop_scope()


def make_ident_bf():
    import ml_dtypes
    return np.eye(128, dtype=np.float32).astype(ml_dtypes.bfloat16)


def build_on(nc, NT, has_proj, final):
    p = Prog(nc)
    ntile = NT // 128
    xd = nc.dram_tensor("x", [NT, 1024], F32, kind="ExternalInput").ap()
    nwd = nc.dram_tensor("nw", [128, 1024], F32, kind="ExternalInput").ap()
    if has_proj:
        yd = nc.dram_tensor("yT", [1536, NT], BF16, kind="ExternalInput").ap()
        ssd = nc.dram_tensor("ss", [128, ntile * 8], F32, kind="ExternalInput").ap()
        wod = nc.dram_tensor("wo", [1536, 1024], F32, kind="ExternalInput").ap()
    if final:
        outd = nc.dram_tensor("out", [NT, 1024], F32, kind="ExternalOutput").ap()
    else:
        idd = nc.dram_tensor("identb", [128, 128], BF16, kind="ExternalInput").ap()
        hTd = nc.dram_tensor("hT", [1024, NT], BF16, kind="ExternalOutput").ap()
        if has_proj:
            xnd = nc.dram_tensor("xn", [NT, 1024], F32, kind="ExternalOutput").ap()

    class Tl:
        def __init__(s, name, shape, dtype=F32):
            s.t = p.sb("sb_" + name, shape, dtype); s.k = name
        def __getitem__(s, idx):
            return s.t[idx]
    act = lambda fn, r, w: p.op("act", fn, reads=r, writes=w)
    dve = lambda fn, r, w: p.op("dve", fn, reads=r, writes=w)
    pool = lambda fn, r, w: p.op("pool", fn, reads=r, writes=w)
    pe = lambda fn, r, w: p.op("pe", fn, reads=r, writes=w)

    nw = Tl("nw", (128, 1024))
    p.dma("sp", "d_nw", nw[:], nwd[:, :], writes=[nw.k])
    if not final:
        idb = Tl("idb", (128, 128), BF16)
        p.dma("sp", "d_id", idb[:], idd[:, :], writes=[idb.k])
        pst = p.ps("pst", [128, 1024], BF16)
    if has_proj:
        ps1 = p.ps("ps1", [128, 1024], F32)
        ps2 = p.ps("ps2", [128, 1024], F32)
        wob = Tl("wob", (128, 12, 1024), BF16)
        wst = [Tl(f"wst{i}", (128, 1024)) for i in range(2)]
        for kc in range(12):
            s = wst[kc % 2]
            p.dma("sp", f"d_w{kc % 2}", s[:], wod[kc * 128:(kc + 1) * 128, :], writes=[s.k])
            if kc % 2 == 0:
                act(lambda e: e.copy(out=wob[:, kc, :], in_=s[:]), [s.k], [f"wob{kc}"])
            else:
                dve(lambda e: e.tensor_copy(out=wob[:, kc, :], in_=s[:]), [s.k], [f"wob{kc}"])
        ysb = Tl("ysb", (128, 12, NT), BF16)
        for q in range(4):
            qs = slice(q * (NT // 4), (q + 1) * (NT // 4))
            p.dma("sp", f"d_y{q}", ysb[:, :, qs], yd.rearrange("(kc q) t -> q kc t", q=128)[:, :, qs], writes=[f"ysb{q}"])
        sst = Tl("sst", (128, ntile, 8))
        p.dma("sp", "d_ss", sst[:], ssd.rearrange("p (a h) -> p a h", h=8), writes=[sst.k])
        rst = Tl("rst", (128, ntile))
        dve(lambda e: e.tensor_reduce(out=rst[:], in_=sst[:], axis=AX.X, op=ALU.add), [sst.k], [rst.k])
        act(lambda e: e.activation(out=rst[:], in_=rst[:], func=AF.Ln, scale=1.0 / 512, bias=1e-6), [rst.k], [rst.k])
        act(lambda e: e.activation(out=rst[:], in_=rst[:], func=AF.Exp, scale=-0.5), [rst.k], [rst.k])

    xt = [Tl(f"xt{i}", (128, 1024)) for i in range(2)]
    xn = [Tl(f"xn{i}", (128, 1024)) for i in range(2)]
    sq = Tl("sq", (128, 1024))
    hb = [Tl(f"hb{i}", (128, 1024), F32 if final else BF16) for i in range(2)]
    hTs = [Tl(f"hTs{i}", (128, 8, 128), BF16) for i in range(2)]
    cc = Tl("cc", (128, 2 * ntile))
    evs = []
    for t in range(ntile):
        s = t % 2
        ts_ = slice(t * 128, (t + 1) * 128)
        x_ = xt[s]
        p.dma("sp", f"d_x{s}", x_[:], xd[ts_, :], writes=[x_.k])
        if has_proj:
            q = t // (ntile // 4)
            for half in range(2):
                hs = slice(half * 512, (half + 1) * 512)
                for kc in range(8):
                    pe(lambda e: e.matmul(ps1[:, hs], lhsT=ysb[:, kc, ts_], rhs=wob[:, kc, hs], start=(kc == 0), stop=(kc == 7)), [f"ysb{q}", f"wob{kc}"], ["ps1"])
                for kc in range(8, 12):
                    pe(lambda e: e.matmul(ps2[:, hs], lhsT=ysb[:, kc, ts_], rhs=wob[:, kc, hs], start=(kc == 8), stop=(kc == 11)), [f"ysb{q}", f"wob{kc}"], ["ps2"])
            xo = xn[s]
            dve(lambda e: e.tensor_tensor(out=xo[:], in0=ps1[:, :], in1=x_[:], op=ALU.add), ["ps1", x_.k], [xo.k])
            dve(lambda e: e.scalar_tensor_tensor(out=xo[:], in0=ps2[:, :], scalar=rst[:, t:t + 1], in1=xo[:], op0=ALU.mult, op1=ALU.add), ["ps2", rst.k, xo.k], [xo.k])
            if not final:
                evs.append(p.dma("sp", f"d_xo{s}", xnd[ts_, :], xo[:], reads=[xo.k]))
        else:
            xo = x_
        c1 = cc[:, 2 * t:2 * t + 1]
        act(lambda e: e.activation(out=sq[:], in_=xo[:], func=AF.Square, accum_out=c1), [xo.k], [sq.k, f"cc{t}"])
        act(lambda e: e.activation(out=c1, in_=c1, func=AF.Ln, scale=1.0 / 1024, bias=1e-6), [f"cc{t}"], [f"cc{t}"])
        act(lambda e: e.activation(out=c1, in_=c1, func=AF.Exp, scale=-0.5), [f"cc{t}"], [f"cc{t}"])
        h_ = hb[s]
        dve(lambda e: e.scalar_tensor_tensor(out=h_[:], in0=xo[:], scalar=c1, in1=nw[:], op0=ALU.mult, op1=ALU.mult), [xo.k, f"cc{t}", nw.k], [h_.k])
        if final:
            evs.append(p.dma("sp", f"d_o{s}", outd[ts_, :], h_[:], reads=[h_.k]))
        else:
            for kc in range(8):
                pe(lambda e: e.transpose(out=pst[:, kc * 128:(kc + 1) * 128], in_=h_[:, kc * 128:(kc + 1) * 128], identity=idb[:]), [h_.k, idb.k], ["pst"])
            ht = hTs[s]
            act(lambda e: e.copy(out=ht[:].rearrange("p a b -> p (a b)"), in_=pst[:, :]), ["pst"], [ht.k])
            evs.append(p.dma("sp", f"d_h{s}", hTd.rearrange("(kc q) t -> q kc t", q=128)[:, :, ts_], ht[:], reads=[ht.k]))
    for ev in evs[-6:]:
        p.finish_wait("sp", ev)
    p.emit()
    p.close()
    return nc


T_FULL = 16384
NCORE = 8
NT_CORE = T_FULL // NCORE
GDN_PROJ = 512 * 3 + 16 + 512
RWKV_PROJ = 512 * 3 + 128 + 512


def core_inputs(inp, l, h, hT_bf, consts):
    w_in = inp['w_in'][l]
    g0 = 0; r0 = GDN_PROJ; h0 = GDN_PROJ + RWKV_PROJ
    hs = slice(h * 64, (h + 1) * 64)
    def col(base, idx): return w_in[:, base + idx * 512 + h * 64: base + idx * 512 + (h + 1) * 64]
    w = np.zeros((1024, NG * 64), np.float32)
    w[:, GQ*64:(GQ+1)*64] = col(g0, 0); w[:, GK*64:(GK+1)*64] = col(g0, 1); w[:, GV*64:(GV+1)*64] = col(g0, 2)
    w[:, GSC*64 + 0] = w_in[:, g0 + 1536 + h]; w[:, GSC*64 + 32] = w_in[:, g0 + 1536 + 8 + h]
    w[:, GG*64:(GG+1)*64] = w_in[:, g0 + 1552 + h*64: g0 + 1552 + (h+1)*64]
    w[:, RR*64:(RR+1)*64] = col(r0, 0); w[:, RK*64:(RK+1)*64] = col(r0, 1); w[:, RV*64:(RV+1)*64] = col(r0, 2)
    w[:, RWD*64:(RWD+1)*64] = w_in[:, r0 + 1536: r0 + 1600]; w[:, RAD*64:(RAD+1)*64] = w_in[:, r0 + 1600: r0 + 1664]
    w[:, RG*64:(RG+1)*64] = w_in[:, r0 + 1664 + h*64: r0 + 1664 + (h+1)*64]
    for gi, g in enumerate((HQ, HF, HI, HGT)):
        w[:, g*64:(g+1)*64] = col(h0, gi)
    pp = np.zeros((64, NPAR), np.float32)
    cw = inp['gdn_conv_w'][l]
    for ci in range(3):
        for j in range(4):
            pp[:, PC_CONV + ci*4 + j] = cw[j, ci*512 + h*64: ci*512 + (h+1)*64]
    pp[:, PC_GNW] = inp['gdn_norm_w'][l]
    pp[:, PC_ALOG] = inp['gdn_a_log'][l, h]; pp[:, PC_DTB] = inp['gdn_dt_bias'][l, h]
    mu = inp['rwkv_mu'][l]
    pp[:, PC_MU+0] = mu[0+h*64:0+(h+1)*64]; pp[:, PC_MU+1] = mu[512+h*64:512+(h+1)*64]; pp[:, PC_MU+2] = mu[1024+h*64:1024+(h+1)*64]
    pp[:, PC_MU+3] = mu[1536:1600]; pp[:, PC_MU+4] = mu[1600:1664]; pp[:, PC_MU+5] = mu[1664+h*64:1664+(h+1)*64]
    pp[:, PC_W0] = inp['rwkv_w0'][l, hs]; pp[:, PC_A0] = inp['rwkv_a0'][l, hs]; pp[:, PC_KK] = inp['rwkv_k_k'][l, hs]
    pp[:, PC_KA] = inp['rwkv_k_a'][l, hs]; pp[:, PC_RK] = inp['rwkv_r_k'][l, h]; pp[:, PC_LNW] = inp['rwkv_ln_w'][l, hs]
    pp[:, PC_LNB] = inp['rwkv_ln_b'][l, hs]
    pp[:, PC_LB0] = inp['hgrn_lower_bounds'][0, hs]; pp[:, PC_LB1] = inp['hgrn_lower_bounds'][1, hs]
    pp[:, PC_HNW] = inp['hgrn_norm_w'][l, hs]
    lora = np.concatenate([inp['rwkv_w_up'][l][:, hs], inp['rwkv_a_up'][l][:, hs]], axis=1).astype(np.float32)
    return {"hT": hT_bf, "w": w, "pp": pp, "lora": np.ascontiguousarray(lora), "consts": consts}


def _run_mixer(inp, l, hT_full, consts):
    nc = bass.Bass("TRN2", target_bir_lowering=False)
    build_mixer(nc, T_FULL, l)
    maps = [core_inputs(inp, l, h, hT_full, consts) for h in range(NCORE)]
    res = run_bass_kernel_spmd(nc, maps, core_ids=list(range(NCORE))).results
    yT_full = np.empty((1536, T_FULL), dtype=res[0]["yT"].dtype)
    for h in range(NCORE):
        yh = res[h]["yT"]
        for m in range(3):
            yT_full[m * 512 + h * 64:m * 512 + (h + 1) * 64] = yh[m * 64:(m + 1) * 64]
    ss = np.stack([res[h]["ss"].reshape(T_FULL) for h in range(NCORE)], axis=0)
    return yT_full, ss


def _run_on(xs, nwv, has_proj, final, yT_full=None, ss=None, wo=None):
    nc = bass.Bass("TRN2", target_bir_lowering=False)
    build_on(nc, NT_CORE, has_proj, final)
    nw = np.ascontiguousarray(np.broadcast_to(nwv[None, :], (128, 1024))).astype(np.float32)
    idb = make_ident_bf()
    maps = []
    for c in range(NCORE):
        m = {"x": xs[c], "nw": nw}
        if not final:
            m["identb"] = idb
        if has_proj:
            tsl = slice(c * NT_CORE, (c + 1) * NT_CORE)
            m["yT"] = np.ascontiguousarray(yT_full[:, tsl])
            ssl = np.stack([ss[h, tsl].reshape(NT_CORE // 128, 128).T for h in range(NCORE)], axis=-1)
            m["ss"] = np.ascontiguousarray(ssl.reshape(128, -1)).astype(np.float32)
            m["wo"] = wo
        maps.append(m)
    return run_bass_kernel_spmd(nc, maps, core_ids=list(range(NCORE))).results


def kernel(**inputs):
    inp = {k: np.ascontiguousarray(np.asarray(v)) for k, v in inputs.items()}
    x = inp['x'][0]
    xs = [np.ascontiguousarray(x[c * NT_CORE:(c + 1) * NT_CORE]) for c in range(NCORE)]
    consts = make_consts()
    r = _run_on(xs, inp['norm_w'][0], False, False)
    hT_full = np.concatenate([r[c]["hT"] for c in range(NCORE)], axis=1)
    yT_full, ss = _run_mixer(inp, 0, hT_full, consts)
    r = _run_on(xs, inp['norm_w'][1], True, False, yT_full, ss, inp['w_out'][0])
    xs = [r[c]["xn"] for c in range(NCORE)]
    hT_full = np.concatenate([r[c]["hT"] for c in range(NCORE)], axis=1)
    yT_full, ss = _run_mixer(inp, 1, hT_full, consts)
    r = _run_on(xs, inp['final_norm_w'], True, True, yT_full, ss, inp['w_out'][1])
    out = np.concatenate([r[c]["out"] for c in range(NCORE)], axis=0)
    return out[None].astype(np.float32)
```
